# Optimizing a Trainium2 kernel written in Bass

```python
import math
import jax
import jax.numpy as jnp
from jax import lax
import numpy as np

D_MODEL = 1024
BATCH = 16
SEQ = 4096
DEPTH = 2

N_MIXERS = 2
N_CONV_LAYERS = (DEPTH + 1) // 2
N_GDN_LAYERS = DEPTH // 2
D_FF = 4 * D_MODEL
CONV_WIDTH = 31
GDN_HEADS = 8
GDN_HEAD_K = D_MODEL // GDN_HEADS
GDN_HEAD_V = D_MODEL // GDN_HEADS
GDN_KEY_DIM = GDN_HEADS * GDN_HEAD_K
GDN_VAL_DIM = GDN_HEADS * GDN_HEAD_V
GDN_QKV_DIM = 2 * GDN_KEY_DIM + GDN_VAL_DIM
GDN_IN_DIM = GDN_QKV_DIM + GDN_VAL_DIM + 2 * GDN_HEADS
SHORT_CONV_WIDTH = 4
CHUNK = 64
NORM_EPS = 1e-6

kernel_name = 'hybrid_conformer_conv_gated_deltanet_trunk'


def rms_norm(x, g, eps=NORM_EPS):
    xf = x.astype(jnp.float32)
    y = xf * lax.rsqrt(jnp.mean(xf * xf, axis=-1, keepdims=True) + eps)
    return (y * g.astype(jnp.float32)).astype(x.dtype)


def layer_norm(x, g, b, eps=NORM_EPS):
    xf = x.astype(jnp.float32)
    mu = jnp.mean(xf, axis=-1, keepdims=True)
    xc = xf - mu
    y = xc * lax.rsqrt(jnp.mean(xc * xc, axis=-1, keepdims=True) + eps)
    return (y * g.astype(jnp.float32) + b.astype(jnp.float32)).astype(x.dtype)


def l2norm(x, eps=1e-6):
    xf = x.astype(jnp.float32)
    return xf * lax.rsqrt(jnp.sum(xf * xf, axis=-1, keepdims=True) + eps)


def causal_depthwise_conv(x, w):
    K, C = w.shape
    return lax.conv_general_dilated(
        x, w[:, None, :].astype(x.dtype), window_strides=(1,),
        padding=[(K - 1, 0)], dimension_numbers=('NWC', 'WIO', 'NWC'),
        feature_group_count=C)


def conformer_conv(h, w_pw1, b_pw1, w_dw, b_dw, ln_g, ln_b, w_pw2, b_pw2):
    u = h @ w_pw1 + b_pw1
    u = jax.nn.glu(u, axis=-1)
    u = causal_depthwise_conv(u, w_dw) + b_dw
    u = jax.nn.silu(layer_norm(u, ln_g, ln_b))
    return u @ w_pw2 + b_pw2


def chunk_gated_delta_rule(q, k, v, g, beta):
    B, S, H, dk = q.shape
    dv = v.shape[-1]
    N = S // CHUNK
    q, k, v = [jnp.swapaxes(t, 1, 2).reshape(B, H, N, CHUNK, -1) for t in (q, k, v)]
    g, beta = [jnp.swapaxes(t, 1, 2).reshape(B, H, N, CHUNK) for t in (g, beta)]
    g = jnp.cumsum(g, axis=-1)
    idx = jnp.arange(CHUNK)
    causal = idx[:, None] >= idx[None, :]
    strict = idx[:, None] > idx[None, :]
    decay = jnp.exp(jnp.where(causal, g[..., :, None] - g[..., None, :], -jnp.inf))
    k_beta = k * beta[..., None]
    kk = jnp.einsum('bhnid,bhnjd->bhnij', k_beta, k) * decay
    m = jnp.where(strict, kk, 0.0) + jnp.eye(CHUNK, dtype=jnp.float32)
    rhs = jnp.concatenate([v * beta[..., None], k_beta * jnp.exp(g)[..., None]], axis=-1)
    sol = lax.linalg.triangular_solve(m, rhs, left_side=True, lower=True, unit_diagonal=True)
    u = sol[..., :dv]
    w = sol[..., dv:]
    qk = jnp.einsum('bhnid,bhnjd->bhnij', q, k) * decay

    def step(state, xs):
        q_c, k_c, u_c, w_c, qk_c, g_c = xs
        v_new = u_c - jnp.einsum('bhcd,bhde->bhce', w_c, state)
        o = (jnp.einsum('bhcd,bhde->bhce', q_c * jnp.exp(g_c)[..., None], state)
             + jnp.einsum('bhij,bhje->bhie', qk_c, v_new))
        g_last = g_c[..., -1]
        state = (state * jnp.exp(g_last)[..., None, None]
                 + jnp.einsum('bhcd,bhce->bhde',
                              k_c * jnp.exp(g_last[..., None] - g_c)[..., None], v_new))
        return state, o

    xs = tuple(jnp.moveaxis(t, 2, 0) for t in (q, k, u, w, qk, g))
    state0 = jnp.zeros((B, H, dk, dv), jnp.float32)
    _, o = lax.scan(step, state0, xs)
    o = jnp.moveaxis(o, 0, 2).reshape(B, H, S, dv)
    return jnp.swapaxes(o, 1, 2)


def gated_deltanet(h, w_in, conv_w, a_log, dt_bias, norm_g, w_out):
    B, S, _ = h.shape
    proj = h @ w_in
    qkv = proj[..., :GDN_QKV_DIM]
    z = proj[..., GDN_QKV_DIM:GDN_QKV_DIM + GDN_VAL_DIM]
    a_raw = proj[..., GDN_QKV_DIM + GDN_VAL_DIM:GDN_QKV_DIM + GDN_VAL_DIM + GDN_HEADS]
    b_raw = proj[..., GDN_QKV_DIM + GDN_VAL_DIM + GDN_HEADS:]
    qkv = jax.nn.silu(causal_depthwise_conv(qkv, conv_w))
    q = qkv[..., :GDN_KEY_DIM].reshape(B, S, GDN_HEADS, GDN_HEAD_K)
    k = qkv[..., GDN_KEY_DIM:2 * GDN_KEY_DIM].reshape(B, S, GDN_HEADS, GDN_HEAD_K)
    v = qkv[..., 2 * GDN_KEY_DIM:].reshape(B, S, GDN_HEADS, GDN_HEAD_V).astype(jnp.float32)
    q = l2norm(q) * (GDN_HEAD_K ** -0.5)
    k = l2norm(k)
    beta = jax.nn.sigmoid(b_raw.astype(jnp.float32))
    g = -jnp.exp(a_log.astype(jnp.float32)) * jax.nn.softplus(
        a_raw.astype(jnp.float32) + dt_bias.astype(jnp.float32))
    o = chunk_gated_delta_rule(q, k, v, g, beta)
    zf = z.reshape(B, S, GDN_HEADS, GDN_HEAD_V).astype(jnp.float32)
    o = rms_norm(o, norm_g) * jax.nn.silu(zf)
    return o.reshape(B, S, GDN_VAL_DIM).astype(h.dtype) @ w_out


def sqrelu_mlp(h, w1, w2):
    return jnp.square(jax.nn.relu(h @ w1)) @ w2


def _normal(key, shape, fan_in):
    return jax.random.normal(key, shape, jnp.float32) * (fan_in ** -0.5)


def setup_inputs(seed: int = 0) -> dict:
    key = jax.random.key(seed)
    ks = jax.random.split(key, 24)
    D = D_MODEL
    Nc, Ng = N_CONV_LAYERS, N_GDN_LAYERS
    x = jax.random.normal(ks[0], (BATCH, SEQ, D), jnp.float32)
    norm_mix_g = 1.0 + 0.02 * jax.random.normal(ks[1], (DEPTH, D), jnp.float32)
    norm_ffn_g = 1.0 + 0.02 * jax.random.normal(ks[2], (DEPTH, D), jnp.float32)
    final_norm_g = 1.0 + 0.02 * jax.random.normal(ks[3], (D,), jnp.float32)
    cv_w_pw1 = _normal(ks[4], (Nc, D, 2 * D), D)
    cv_b_pw1 = 0.01 * jax.random.normal(ks[5], (Nc, 2 * D), jnp.float32)
    cv_w_dw = _normal(ks[6], (Nc, CONV_WIDTH, D), CONV_WIDTH)
    cv_b_dw = 0.01 * jax.random.normal(ks[7], (Nc, D), jnp.float32)
    cv_ln_g = 1.0 + 0.02 * jax.random.normal(ks[8], (Nc, D), jnp.float32)
    cv_ln_b = 0.01 * jax.random.normal(ks[9], (Nc, D), jnp.float32)
    cv_w_pw2 = _normal(ks[10], (Nc, D, D), D)
    cv_b_pw2 = 0.01 * jax.random.normal(ks[11], (Nc, D), jnp.float32)
    gdn_w_in = _normal(ks[12], (Ng, D, GDN_IN_DIM), D)
    gdn_conv_w = _normal(ks[13], (Ng, SHORT_CONV_WIDTH, GDN_QKV_DIM), SHORT_CONV_WIDTH)
    gdn_a_log = jnp.log(jax.random.uniform(ks[14], (Ng, GDN_HEADS), jnp.float32, 1.0, 16.0))
    dt = jnp.exp(jax.random.uniform(ks[15], (Ng, GDN_HEADS), jnp.float32,
                                    math.log(1e-3), math.log(1e-1)))
    gdn_dt_bias = dt + jnp.log(-jnp.expm1(-dt))
    gdn_norm_g = 1.0 + 0.02 * jax.random.normal(ks[16], (Ng, GDN_HEAD_V), jnp.float32)
    gdn_w_out = _normal(ks[17], (Ng, GDN_VAL_DIM, D), GDN_VAL_DIM)
    mlp_w1 = _normal(ks[18], (DEPTH, D, D_FF), D)
    mlp_w2 = _normal(ks[19], (DEPTH, D_FF, D), D_FF)
    return {'x': x, 'norm_mix_g': norm_mix_g, 'norm_ffn_g': norm_ffn_g,
            'final_norm_g': final_norm_g,
            'cv_w_pw1': cv_w_pw1, 'cv_b_pw1': cv_b_pw1, 'cv_w_dw': cv_w_dw,
            'cv_b_dw': cv_b_dw, 'cv_ln_g': cv_ln_g, 'cv_ln_b': cv_ln_b,
            'cv_w_pw2': cv_w_pw2, 'cv_b_pw2': cv_b_pw2,
            'gdn_w_in': gdn_w_in, 'gdn_conv_w': gdn_conv_w, 'gdn_a_log': gdn_a_log,
            'gdn_dt_bias': gdn_dt_bias, 'gdn_norm_g': gdn_norm_g, 'gdn_w_out': gdn_w_out,
            'mlp_w1': mlp_w1, 'mlp_w2': mlp_w2}


def reference(x, norm_mix_g, norm_ffn_g, final_norm_g,
              cv_w_pw1, cv_b_pw1, cv_w_dw, cv_b_dw, cv_ln_g, cv_ln_b, cv_w_pw2, cv_b_pw2,
              gdn_w_in, gdn_conv_w, gdn_a_log, gdn_dt_bias, gdn_norm_g, gdn_w_out,
              mlp_w1, mlp_w2):
    h = x
    for i in range(DEPTH):
        hn = rms_norm(h, norm_mix_g[i])
        j = i // N_MIXERS
        if i % N_MIXERS == 0:
            mix = conformer_conv(hn, cv_w_pw1[j], cv_b_pw1[j], cv_w_dw[j], cv_b_dw[j],
                                 cv_ln_g[j], cv_ln_b[j], cv_w_pw2[j], cv_b_pw2[j])
        else:
            mix = gated_deltanet(hn, gdn_w_in[j], gdn_conv_w[j], gdn_a_log[j],
                                 gdn_dt_bias[j], gdn_norm_g[j], gdn_w_out[j])
        h = h + mix
        h = h + sqrelu_mlp(rms_norm(h, norm_ffn_g[i]), mlp_w1[i], mlp_w2[i])
    return rms_norm(h, final_norm_g)
```

```python
import numpy as np
import concourse.bass as bass
import concourse.mybir as mybir
from concourse.bass_utils import run_bass_kernel_spmd

F32 = mybir.dt.float32
F32R = mybir.dt.float32r
CHAIN_R = [False]
BF16 = mybir.dt.bfloat16
AF = mybir.ActivationFunctionType
ALU = mybir.AluOpType

D = 1024
NC8 = 8
NT = 512
EPS = 1e-6
KW = 31
H = 8
QKV = 3072
DEBUG_STOP = [99]
GIN = 4112

COMPUTE = ("pe", "act", "dve", "pool")


class T:
    __slots__ = ("ap", "w", "r", "name")

    def __init__(self, ap, name=""):
        self.ap = ap
        self.w = None
        self.r = {}
        self.name = name


class TV:
    __slots__ = ("ap", "base", "name")

    def __init__(self, base, ap):
        self.base = base
        self.ap = ap
        self.name = base.name + "_v"

    @property
    def w(self):
        return self.base.w

    @w.setter
    def w(self, v):
        self.base.w = v

    @property
    def r(self):
        return self.base.r

    @r.setter
    def r(self, v):
        self.base.r = v


class Prog:
    def __init__(self):
        self.ops = []
        self.last_dma = {}

    def op(self, eng, fn, reads=(), writes=(), dma=None):
        i = len(self.ops)
        deps = set()
        for t in reads:
            if t.w is not None:
                deps.add(t.w)
        for t in writes:
            if t.w is not None:
                deps.add(t.w)
            deps.update(t.r.values())
        if dma is not None and dma in self.last_dma:
            deps.add(self.last_dma[dma])
        if dma is not None:
            self.last_dma[dma] = i
        for t in writes:
            t.w = i
            t.r = {}
        for t in reads:
            if dma is not None:
                t.r[("dma", i)] = i
            else:
                t.r[eng] = i
        self.ops.append((eng, fn, deps, dma))
        return i

    def emit(self, nc, sem_limit=30000):
        ops = self.ops
        n = len(ops)
        red = []
        needed = [False] * n
        for i, (eng, fn, deps, dma) in enumerate(ops):
            best = {}
            dl = []
            for d in deps:
                de, _, _, ddma = ops[d]
                if ddma is not None:
                    dl.append(d)
                else:
                    if de == eng and eng == "pe":
                        continue
                    if de not in best or best[de] < d:
                        best[de] = d
            dl.extend(best.values())
            red.append(dl)
            for d in dl:
                needed[d] = True
        ordinal = [0] * n
        cnt = {}
        dcnt = {}
        for i, (eng, fn, deps, dma) in enumerate(ops):
            if dma is not None:
                dcnt[dma] = dcnt.get(dma, 0) + 1
                ordinal[i] = 16 * dcnt[dma]
            elif needed[i]:
                cnt[eng] = cnt.get(eng, 0) + 1
                ordinal[i] = cnt[eng]
        self.stats = dict(cnt)
        sems = {}

        def get_sem(key):
            if key not in sems:
                sems[key] = nc.alloc_semaphore(name="s_" + "_".join(str(k) for k in key))
            return sems[key]

        def signal_of(i):
            eng, _, _, dma = ops[i]
            if dma is not None:
                return ("d", dma), ordinal[i], ordinal[i]
            o = ordinal[i]
            ep = (o - 1) // sem_limit
            return ("e", eng, ep), o - ep * sem_limit, o

        for i in range(n):
            if ops[i][3] is not None or needed[i]:
                get_sem(signal_of(i)[0])

        by_eng = {}
        for i, o in enumerate(ops):
            by_eng.setdefault(o[0], []).append(i)

        def run_engine(ename, e):
            known = {}
            nwait = 0
            for i in by_eng.get(ename, []):
                eng, fn, deps, dma = ops[i]
                for d in red[i]:
                    key, local, glob = signal_of(d)
                    kk = key if key[0] == "d" else ("e", key[1])
                    if known.get(kk, 0) >= glob:
                        continue
                    known[kk] = glob
                    e.wait_ge(get_sem(key), local)
                    nwait += 1
                ins = fn(e)
                if dma is not None:
                    ins.then_inc(get_sem(("d", dma)), 16)
                elif needed[i]:
                    ins.then_inc(get_sem(signal_of(i)[0]), 1)
            self.stats["wait_" + ename] = nwait

        with nc.Block() as block:
            @block.tensor
            def _(e):
                run_engine("pe", e)

            @block.scalar
            def _(e):
                run_engine("act", e)

            @block.vector
            def _(e):
                run_engine("dve", e)

            @block.gpsimd
            def _(e):
                run_engine("pool", e)

            @block.sync
            def _(e):
                run_engine("sp", e)


class UnitPool:
    def __init__(self, nc, n, dtype, name, width=NT):
        self.free = []
        for i in range(n):
            ap = nc.alloc_sbuf_tensor(f"{name}{i}", [128, width], dtype).ap()
            self.free.append(T(ap, f"{name}{i}"))
        self.n = n

    def get(self):
        if not self.free:
            raise RuntimeError("unit pool exhausted")
        return self.free.pop(0)

    def put(self, t):
        self.free.append(t)


def _pv_layout():
    cols = {}
    off = 0

    def add(name, n):
        nonlocal off
        cols[name] = off
        off += n
    for i in range(2):
        add(f"nmg{i}", 8)
        add(f"nfg{i}", 8)
    add("fng", 8)
    add("bpw1", 16)
    add("wdw", KW * 8)
    add("bdw", 8)
    add("lng", 8)
    add("lnb", 8)
    add("bpw2", 8)
    add("gcw", 4 * 24)
    add("gng", 1)
    add("alog", 8)
    add("dtb", 8)
    add("alog4", 32)
    add("dtb4", 32)
    return cols, off


PV, NPV = _pv_layout()


def _chunked(v):
    return np.ascontiguousarray(v.reshape(-1, 128).T)


def pack_pvec(inp):
    pv = np.zeros((128, NPV), np.float32)

    def put(name, arr):
        pv[:, PV[name]:PV[name] + arr.shape[1]] = arr
    for i in range(2):
        put(f"nmg{i}", _chunked(inp["norm_mix_g"][i]))
        put(f"nfg{i}", _chunked(inp["norm_ffn_g"][i]))
    put("fng", _chunked(inp["final_norm_g"]))
    put("bpw1", _chunked(inp["cv_b_pw1"][0]))
    wdw = inp["cv_w_dw"][0]
    put("wdw", np.concatenate([_chunked(wdw[k]) for k in range(KW)], axis=1))
    put("bdw", _chunked(inp["cv_b_dw"][0]))
    put("lng", _chunked(inp["cv_ln_g"][0]))
    put("lnb", _chunked(inp["cv_ln_b"][0]))
    put("bpw2", _chunked(inp["cv_b_pw2"][0]))
    gcw = inp["gdn_conv_w"][0]
    put("gcw", np.concatenate([_chunked(gcw[k]) for k in range(4)], axis=1))
    put("gng", inp["gdn_norm_g"][0].reshape(128, 1))
    put("alog", np.broadcast_to(inp["gdn_a_log"][0][None, :], (128, 8)))
    put("dtb", np.broadcast_to(inp["gdn_dt_bias"][0][None, :], (128, 8)))
    put("alog4", np.broadcast_to(np.tile(inp["gdn_a_log"][0], 4)[None, :], (128, 32)))
    put("dtb4", np.broadcast_to(np.tile(inp["gdn_dt_bias"][0], 4)[None, :], (128, 32)))
    return pv


def make_consts():
    idx = np.arange(128)
    ident = np.eye(128, dtype=np.float32)
    triu = (idx[:, None] <= idx[None, :]).astype(np.float32)
    ones = np.ones((128, 128), np.float32)
    stri = (idx[:, None] < idx[None, :]).astype(np.float32)
    maskneg = np.where(idx[:, None] <= idx[None, :], 0.0, -30000.0).astype(np.float32)
    lv = []
    for l in range(7):
        sz = 1 << l
        i = idx[None, :]
        j = idx[:, None]
        m = ((i // (2 * sz)) == (j // (2 * sz))) & ((i % (2 * sz)) >= sz) & ((j % (2 * sz)) < sz)
        lv.append(m.astype(np.float32))
    return np.concatenate([ident, triu, ones, stri, maskneg] + lv, axis=1)


def slab_table(depth):
    sl = []
    for li in range(depth):
        if li % 2 == 0:
            for half in range(2):
                sl.append(("cv_w_pw1", 0, 0, half * 512))
                sl.append(("cv_w_pw1", 0, 0, 1024 + half * 512))
                for j in range(4):
                    sl.append(("diag", half * 4 + j, 0, 0))
            for m0 in (0, 512):
                sl.append(("cv_w_pw2", 0, 0, m0))
        else:
            for m0 in range(0, 4096, 512):
                sl.append(("gdn_w_in", 0, 0, m0))
            for m0 in (0, 512):
                sl.append(("gdn_w_out", 0, 0, m0))
        for m0 in range(0, 4096, 512):
            sl.append(("mlp_w1", li, 0, m0))
        for mg in range(2):
            for kg in range(4):
                sl.append(("mlp_w2", li, kg * 1024, mg * 512))
    return sl


def build_nc(nseq, S, depth, nring=3):
    ntok = nseq * S
    ntiles = ntok // NT
    tiles_per_seq = S // NT
    nc = bass.Bass("TRN2", target_bir_lowering=False)
    P = Prog()

    xT = nc.dram_tensor("xT", [D, ntok], F32, kind="ExternalInput").ap()
    outT = nc.dram_tensor("outT", [D, ntok], F32, kind="ExternalOutput").ap()
    pvec_d = nc.dram_tensor("pvec", [128, NPV], F32, kind="ExternalInput").ap()
    cst_d = nc.dram_tensor("cst", [128, 1536], F32, kind="ExternalInput").ap()
    wd = {}
    wd["cv_w_pw1"] = nc.dram_tensor("cv_w_pw1", [1, D, 2 * D], F32, kind="ExternalInput").ap()
    wd["cv_w_pw2"] = nc.dram_tensor("cv_w_pw2", [1, D, D], F32, kind="ExternalInput").ap()
    wd["mlp_w1"] = nc.dram_tensor("mlp_w1", [2, D, 4 * D], F32, kind="ExternalInput").ap()
    wd["mlp_w2"] = nc.dram_tensor("mlp_w2", [2, 4 * D, D], F32, kind="ExternalInput").ap()
    if depth > 1:
        wd["gdn_w_in"] = nc.dram_tensor("gdn_w_in", [1, D, GIN], F32, kind="ExternalInput").ap()
        wd["gdn_w_out"] = nc.dram_tensor("gdn_w_out", [1, D, D], F32, kind="ExternalInput").ap()
    slabs = slab_table(depth)
    nslab = len(slabs)
    wbf = nc.dram_tensor("wbf", [nslab, 128, 4096], BF16, kind="Internal").ap()
    wdiag = nc.dram_tensor("wdiag", [8, 128, 4096], BF16, kind="Internal").ap()

    def sb(name, shape, dt):
        return nc.alloc_sbuf_tensor(name, shape, dt).ap()

    pv = T(sb("pv", [128, NPV], F32), "pv")
    cst = T(sb("cst_sb", [128, 1536], F32), "cst")
    cbf = T(sb("cbf", [128, 640], BF16), "cbf")
    onesD = T(sb("onesD", [128, 128], BF16), "onesD")
    onesH = T(sb("onesH", [128, 128], BF16), "onesH")
    onesD32 = T(sb("onesD32", [128, 128], F32), "onesD32")
    hbuf = [[T(sb(f"h{b}_{c}", [128, NT], F32), f"h{b}_{c}") for c in range(8)] for b in range(2)]
    ring = [T(sb(f"ring{i}", [128, 4096], BF16), f"ring{i}") for i in range(nring)]
    p16 = UnitPool(nc, 41, BF16, "u16_")
    p32 = UnitPool(nc, 14, F32, "u32_")
    glu_tmp = [T(sb(f"glu{i}", [128, NT + KW - 1], BF16), f"glu{i}") for i in range(8)]
    halo0 = [T(sb(f"halo0_{c}", [128, KW - 1], BF16), f"halo0_{c}") for c in range(8)]
    psum = [T(nc.alloc_psum_tensor(f"ps{i}", [128, NT], F32).ap(), f"ps{i}") for i in range(8)]
    ps_i = [0]

    def next_ps():
        t = psum[ps_i[0] % 8]
        ps_i[0] += 1
        return t

    def pvc(name, j=0):
        o = PV[name] + j
        return pv.ap[:, o:o + 1]

    ident32 = cst.ap[:, 0:128]
    triu32 = cst.ap[:, 128:256]
    ones32 = cst.ap[:, 256:384]
    identbf = cbf.ap[:, 0:128]
    stribf = cbf.ap[:, 384:512]
    maskneg32 = cst.ap[:, 512:640]
    masknegbf = cbf.ap[:, 512:640]

    P.op("sp", lambda e: e.dma_start(out=pv.ap, in_=pvec_d), writes=[pv], dma="pv")
    P.op("sp", lambda e: e.dma_start(out=cst.ap, in_=cst_d), writes=[cst], dma="cst")
    P.op("dve", lambda e: e.tensor_copy(out=cbf.ap, in_=cst.ap[:, 0:640]), reads=[cst], writes=[cbf])
    P.op("dve", lambda e: e.memset(onesD.ap, 1.0 / D), writes=[onesD])
    P.op("dve", lambda e: e.memset(onesH.ap, 1.0), writes=[onesH])
    P.op("dve", lambda e: e.memset(onesD32.ap, 1.0 / D), writes=[onesD32])
    for c in range(8):
        P.op("dve", (lambda c: lambda e: e.memset(halo0[c].ap, 0.0))(c), writes=[halo0[c]])

    cast_done_ops = []
    for s_, (wn, li, k0, m0) in enumerate(slabs):
        if wn == "diag":
            continue
        src = wd[wn][li, k0:k0 + 1024, m0:m0 + 512].rearrange("(kc p) m -> p kc m", p=128)
        dst = wbf[s_].rearrange("p (kc m) -> p kc m", kc=8)
        P.op("pool", (lambda src, dst: lambda e: e.dma_start(out=dst, in_=src))(src, dst),
             writes=[], dma=("wc", s_ % 4))
    cast_done_ops += [P.last_dma[("wc", k)] for k in range(4) if ("wc", k) in P.last_dma]
    for c in range(8):
        stage = ring[c % nring]
        for k in range(KW):
            P.op("dve", (lambda stage, c, k: lambda e: e.tensor_scalar(
                out=stage.ap[:, k * 128:(k + 1) * 128], in0=ident32, scalar1=pvc("wdw", k * 8 + c), scalar2=None,
                op0=ALU.mult))(stage, c, k), reads=[cst, pv], writes=[stage])
        P.op("dve", (lambda stage: lambda e: e.memset(stage.ap[:, KW * 128:], 0.0))(stage), writes=[stage])
        i = P.op("sp", (lambda stage, c: lambda e: e.dma_start(out=wdiag[c], in_=stage.ap))(stage, c),
                 reads=[stage], dma=("dg", c % nring))
        cast_done_ops.append(i)

    slab_ctr = [0]

    def next_slab(tile_first_use_guard=None):
        g = slab_ctr[0]
        slab_ctr[0] += 1
        s = g % nslab
        slot = ring[g % nring]
        srcap = wdiag[slabs[s][1]] if slabs[s][0] == "diag" else wbf[s]
        i = P.op("sp", (lambda srcap, slot: lambda e: e.dma_start(out=slot.ap, in_=srcap))(srcap, slot),
                 writes=[slot], dma=("ring", g % nring))
        if g < nring:
            P.ops[i][2].update(cast_done_ops)
        elif g < nslab:
            pass
        return slot, slot.ap.rearrange("p (kc m) -> p kc m", kc=8)


    def mm(ps, ps_ap, lt, lt_ap, rt, rt_ap, start, stop):
        P.op("pe", lambda e: e.matmul(ps_ap, lt_ap, rt_ap, start=start, stop=stop),
             reads=[lt, rt], writes=[ps])

    def act(out_t, out_ap, in_t, in_ap, func, bias=None, scale=None, extra_reads=()):
        kw = {}
        if bias is not None:
            kw["bias"] = bias
        if scale is not None:
            kw["scale"] = scale
        P.op("act", lambda e: e.activation(out_ap, in_ap, func, **kw),
             reads=[in_t, *extra_reads], writes=[out_t])

    def rstd_from(ps_stat, eps=EPS):
        t = p32.get()
        act(t, t.ap, ps_stat, ps_stat.ap, AF.Ln, bias=eps_ap(eps), extra_reads=[epsT])
        act(t, t.ap, t, t.ap, AF.Exp, scale=-0.5)
        return t

    epsT = T(sb("epsc", [128, 2], F32), "epsc")
    P.op("dve", lambda e: e.memset(epsT.ap[:, 0:1], EPS), writes=[epsT])
    P.op("dve", lambda e: e.memset(epsT.ap[:, 1:2], 1.0), writes=[epsT])

    def eps_ap(eps):
        return epsT.ap[:, 0:1]

    def rmsnorm(h, gname):
        ps = next_ps()
        for c in range(8):
            sq = p16.get()
            act(sq, sq.ap, h[c], h[c].ap, AF.Square)
            mm(ps, ps.ap, onesD, onesD.ap, sq, sq.ap, c == 0, c == 7)
            p16.put(sq)
        r = rstd_from(ps)
        out = []
        for c in range(8):
            o = p16.get()
            P.op("dve", (lambda o, c: lambda e: e.scalar_tensor_tensor(
                out=o.ap, in0=h[c].ap, scalar=pvc(gname, c), in1=r.ap, op0=ALU.mult, op1=ALU.mult))(o, c),
                reads=[h[c], pv, r], writes=[o])
            out.append(o)
        p32.put(r)
        return out

    def conformer(h, hn, first_of_seq):
        acc = [None] * 8
        for half in range(2):
            wa_t, wa = next_slab()
            wg_t, wg = next_slab()
            pga = [next_ps() for _ in range(4)]
            for kc in range(8):
                for j in range(4):
                    mm(pga[j], pga[j].ap, wa_t, wa[:, kc, j * 128:(j + 1) * 128], hn[kc], hn[kc].ap, kc == 0, kc == 7)
            pgg = [next_ps() for _ in range(4)]
            for kc in range(8):
                for j in range(4):
                    mm(pgg[j], pgg[j].ap, wg_t, wg[:, kc, j * 128:(j + 1) * 128], hn[kc], hn[kc].ap, kc == 0, kc == 7)
            for j in range(4):
                c = half * 4 + j
                psa = pga[j]
                psg = pgg[j]
                sg = p32.get()
                act(sg, sg.ap, psg, psg.ap, AF.Sigmoid, bias=pvc("bpw1", 8 + c), extra_reads=[pv])
                gt = glu_tmp[c]
                P.op("dve", (lambda gt, psa, sg, c: lambda e: e.scalar_tensor_tensor(
                    out=gt.ap[:, KW - 1:], in0=psa.ap, scalar=pvc("bpw1", c), in1=sg.ap,
                    op0=ALU.add, op1=ALU.mult))(gt, psa, sg, c),
                    reads=[psa, sg, pv], writes=[gt])
                p32.put(sg)
                if first_of_seq:
                    P.op("pool", (lambda gt: lambda e: e.memset(gt.ap[:, 0:KW - 1], 0.0))(gt), writes=[gt])
                else:
                    P.op("pool", (lambda gt, c: lambda e: e.tensor_copy(out=gt.ap[:, 0:KW - 1], in_=halo0[c].ap))(gt, c),
                         reads=[halo0[c]], writes=[gt])
                P.op("pool", (lambda gt, c: lambda e: e.tensor_copy(out=halo0[c].ap, in_=gt.ap[:, NT:NT + KW - 1]))(gt, c),
                     reads=[gt], writes=[halo0[c]])
            for j in range(4):
                c = half * 4 + j
                d_t, _ = next_slab()
                gt = glu_tmp[c]
                psc = next_ps()
                for k in range(KW):
                    mm(psc, psc.ap, d_t, d_t.ap[:, k * 128:(k + 1) * 128], gt, gt.ap[:, k:k + NT], k == 0, k == KW - 1)
                a = p32.get()
                act(a, a.ap, psc, psc.ap, AF.Identity, bias=pvc("bdw", c), extra_reads=[pv])
                acc[c] = a
        for t in hn:
            p16.put(t)
        psm = next_ps()
        for c in range(8):
            mm(psm, psm.ap, onesD32, onesD32.ap, acc[c], acc[c].ap, c == 0, c == 7)
        for c in range(8):
            P.op("dve", (lambda c: lambda e: e.tensor_tensor(
                out=acc[c].ap, in0=acc[c].ap, in1=psm.ap, op=ALU.subtract))(c),
                reads=[acc[c], psm], writes=[acc[c]])
        psv = next_ps()
        for c in range(8):
            sq = p16.get()
            act(sq, sq.ap, acc[c], acc[c].ap, AF.Square)
            mm(psv, psv.ap, onesD, onesD.ap, sq, sq.ap, c == 0, c == 7)
            p16.put(sq)
        r = rstd_from(psv)
        ua = []
        for c in range(8):
            P.op("dve", (lambda c: lambda e: e.tensor_tensor(
                out=acc[c].ap, in0=acc[c].ap, in1=r.ap, op=ALU.mult))(c),
                reads=[acc[c], r], writes=[acc[c]])
            o = p16.get()
            act(o, o.ap, acc[c], acc[c].ap, AF.Silu, bias=pvc("lnb", c), scale=pvc("lng", c), extra_reads=[pv])
            ua.append(o)
            p32.put(acc[c])
        p32.put(r)
        for half in range(2):
            w_t, w = next_slab()
            pgrp = [next_ps() for _ in range(4)]
            for kc in range(8):
                for j in range(4):
                    mm(pgrp[j], pgrp[j].ap, w_t, w[:, kc, j * 128:(j + 1) * 128], ua[kc], ua[kc].ap, kc == 0, kc == 7)
            for j in range(4):
                mc = half * 4 + j
                ps = pgrp[j]
                P.op("dve", (lambda ps, mc: lambda e: e.scalar_tensor_tensor(
                    out=h[mc].ap, in0=ps.ap, scalar=pvc("bpw2", mc), in1=h[mc].ap,
                    op0=ALU.add, op1=ALU.add))(ps, mc), reads=[ps, pv, h[mc]], writes=[h[mc]])
        for t in ua:
            p16.put(t)

    def mlp(h, hn):
        hid = []
        for g in range(8):
            w_t, w = next_slab()
            pgrp = [next_ps() for _ in range(4)]
            for kc in range(8):
                for j in range(4):
                    mm(pgrp[j], pgrp[j].ap, w_t, w[:, kc, j * 128:(j + 1) * 128], hn[kc], hn[kc].ap, kc == 0, kc == 7)
            for j in range(4):
                ps = pgrp[j]
                sq = p32.get()
                act(sq, sq.ap, ps, ps.ap, AF.Square)
                o = p16.get()
                P.op("dve", (lambda o, ps, sq: lambda e: e.scalar_tensor_tensor(
                    out=o.ap, in0=ps.ap, scalar=0.0, in1=sq.ap, op0=ALU.is_gt, op1=ALU.mult))(o, ps, sq),
                    reads=[ps, sq], writes=[o])
                p32.put(sq)
                hid.append(o)
        for t in hn:
            p16.put(t)
        for mg in range(2):
            pss = [next_ps() for _ in range(4)]
            for kg in range(4):
                w_t, w = next_slab()
                for j in range(4):
                    for kc in range(8):
                        mm(pss[j], pss[j].ap, w_t, w[:, kc, j * 128:(j + 1) * 128],
                           hid[kg * 8 + kc], hid[kg * 8 + kc].ap, kg == 0 and kc == 0, kg == 3 and kc == 7)
            for j in range(4):
                mc = mg * 4 + j
                P.op("dve", (lambda ps, mc: lambda e: e.tensor_tensor(
                    out=h[mc].ap, in0=h[mc].ap, in1=ps.ap, op=ALU.add))(pss[j], mc),
                    reads=[pss[j], h[mc]], writes=[h[mc]])
        for t in hid:
            p16.put(t)


    if depth > 1:
        wab32 = T(sb("wab32", [128, 8, 16], F32), "wab32")
        wab = T(sb("wab", [128, 8, 16], BF16), "wab")
        negA4 = T(sb("negA4", [128, 32], F32), "negA4")
        onesE = T(sb("onesE", [128, 128], BF16), "onesE")
        halo1 = T(sb("halo1", [128, 72], F32), "halo1")
        S32 = [T(sb(f"S32_{i}", [128, 512], F32), f"S32_{i}") for i in range(2)]
        Sbf = [T(sb(f"Sbf_{i}", [128, 512], BF16), f"Sbf_{i}") for i in range(2)]
        pre_tmp = [T(sb(f"pre{i}", [128, NT + 3], F32), f"pre{i}") for i in range(3)]
        GE = T(sb("GE", [128, 1024], F32), "GE")

        def b16(name):
            return T(sb(name, [128, 1024], BF16), name)
        Yk, kd, vtok, qkT, qgT, dec, egcR = [b16(n) for n in ("Yk", "kd", "vtok", "qkT", "qgT", "dec", "egcR")]
        decs = T(sb("decs", [128, 1024], F32), "decs")

        def pair(name, dt):
            return [T(sb(f"{name}_{i}", [128, 512], dt), f"{name}_{i}") for i in range(2)]
        CDT = BF16
        TW = pair("TW", CDT)
        P0 = pair("P0", CDT)
        Am = pair("Am", CDT)
        Xm = pair("Xm", CDT)
        MT = Am
        identR = T(sb("identR", [128, 128], CDT), "identR")
        P.op("dve", lambda e: e.tensor_copy(out=identR.ap, in_=ident32), reads=[cst], writes=[identR])
        nZw = pair("nZw", BF16)
        vnew = pair("vnew", BF16)
        gatesT = T(sb("gates", [128, 12 * 32], F32), "gates")

        class NS:
            pass
        BS = [NS(), NS()]
        BS[0].Yk, BS[0].kd, BS[0].vtok, BS[0].qkT, BS[0].qgT = Yk, kd, vtok, qkT, qgT
        BS[0].TW, BS[0].P0, BS[0].Am, BS[0].Xm, BS[0].nZw, BS[0].vnew = TW, P0, Am, Xm, nZw, vnew
        BS[1].Yk = TV(pre_tmp[0], pre_tmp[0].ap.bitcast(BF16)[:, 0:1024])
        BS[1].kd = TV(pre_tmp[1], pre_tmp[1].ap.bitcast(BF16)[:, 0:1024])
        BS[1].vtok = TV(pre_tmp[2], pre_tmp[2].ap.bitcast(BF16)[:, 0:1024])
        BS[1].qkT = b16("qkT1")
        BS[1].qgT = b16("qgT1")
        BS[1].TW = pair("TW1", CDT)
        BS[1].P0 = pair("P01", CDT)
        BS[1].Am = pair("Am1", CDT)
        BS[1].Xm = pair("Xm1", CDT)
        BS[1].nZw = [TV(glu_tmp[i], glu_tmp[i].ap[:, 0:512]) for i in range(2)]
        BS[1].vnew = [TV(glu_tmp[2 + i], glu_tmp[2 + i].ap[:, 0:512]) for i in range(2)]
        src = wd["gdn_w_in"][0, :, 4096:4112].rearrange("(kc p) m -> p kc m", p=128)
        P.op("sp", lambda e: e.dma_start(out=wab32.ap, in_=src), writes=[wab32], dma="wab")
        P.op("dve", lambda e: e.tensor_copy(out=wab.ap, in_=wab32.ap), reads=[wab32], writes=[wab])
        P.op("dve", lambda e: e.memset(onesE.ap, 1.0 / 128), writes=[onesE])
        P.op("act", lambda e: e.activation(negA4.ap, pv.ap[:, PV["alog4"]:PV["alog4"] + 32], AF.Exp),
             reads=[pv], writes=[negA4])
        P.op("dve", lambda e: e.tensor_scalar(out=negA4.ap, in0=negA4.ap, scalar1=-1.0, scalar2=None, op0=ALU.mult),
             reads=[negA4], writes=[negA4])

    def v3(ap, n):
        return ap.rearrange("p (h i) -> p h i", h=n)

    def q4(t, h4):
        return t.ap[:, h4 * 128:(h4 + 1) * 128]

    def q4r(t, h4):
        return t.ap[:, h4 * 128:(h4 + 1) * 128]

    def f32(ap):
        return ap.bitcast(F32) if CHAIN_R[0] else ap

    def hs(t, h):
        return t.ap[:, h * 128:(h + 1) * 128]

    def gdn(h, hn, first):
        if first:
            for i in range(2):
                P.op("pool", lambda e, i=i: e.memset(S32[i].ap, 0.0), writes=[S32[i]])
                P.op("pool", lambda e, i=i: e.memset(Sbf[i].ap, 0.0), writes=[Sbf[i]])
        one_ap = epsT.ap[:, 1:2]
        gs = gatesT

        def G(i):
            return gs.ap[:, i * 32:(i + 1) * 32]

        def G3(i):
            return G(i).rearrange("p (b x) -> p b x", x=8)

        def gop(eng, fn, extra_r=(), extra_w=()):
            P.op(eng, fn, reads=[gs, *extra_r], writes=[gs, *extra_w])
        pab = next_ps()
        for b in range(4):
            for kc in range(8):
                mm(pab, pab.ap[:, b * 16:(b + 1) * 16], hn[kc], hn[kc].ap[:, b * 128:(b + 1) * 128],
                   wab, wab.ap[:, kc, :], kc == 0, kc == 7)
        pab3 = pab.ap[:, 0:64].rearrange("p (b x) -> p b x", x=16)
        dtb3 = pv.ap[:, PV["dtb4"]:PV["dtb4"] + 32].rearrange("p (b x) -> p b x", x=8)
        gop("dve", lambda e, pab3=pab3: e.tensor_tensor(out=G3(0), in0=pab3[:, :, 0:8], in1=dtb3, op=ALU.add),
            extra_r=[pab, pv])
        gop("act", lambda e: e.activation(G(1), G(0), AF.Exp))
        gop("act", lambda e: e.activation(G(1), G(1), AF.Ln, bias=one_ap), extra_r=[epsT])
        gop("dve", lambda e: e.tensor_tensor(out=G(2), in0=G(1), in1=negA4.ap, op=ALU.mult), extra_r=[negA4])
        gop("act", lambda e, pab3=pab3: e.activation(G3(3), pab3[:, :, 8:16], AF.Exp, scale=-1.0), extra_r=[pab])
        gop("dve", lambda e: e.tensor_scalar(out=G(3), in0=G(3), scalar1=1.0, scalar2=None, op0=ALU.add))
        gop("dve", lambda e: e.reciprocal(out=G(4), in_=G(3)))
        gop("dve", lambda e: e.tensor_scalar(out=G(5), in0=G(4), scalar1=-1.0, scalar2=None, op0=ALU.mult))
        pcs = next_ps()
        mm(pcs, pcs.ap[:, 0:32], cst, triu32, gs, G(2), True, True)
        mm(pcs, pcs.ap[:, 32:64], cst, ones32, gs, G(2), True, True)
        gop("act", lambda e, pcs=pcs: e.activation(G(6), pcs.ap[:, 0:32], AF.Copy), extra_r=[pcs])
        gop("act", lambda e, pcs=pcs: e.activation(G(7), pcs.ap[:, 0:32], AF.Exp), extra_r=[pcs])
        gop("dve", lambda e: e.tensor_scalar(out=G(10), in0=G(6), scalar1=-1.0, scalar2=None, op0=ALU.mult))
        gop("dve", lambda e, pcs=pcs: e.tensor_tensor(out=G(8), in0=pcs.ap[:, 32:64], in1=G(6), op=ALU.subtract),
            extra_r=[pcs])
        gop("act", lambda e: e.activation(G(8), G(8), AF.Exp))
        gop("act", lambda e, pcs=pcs: e.activation(G(9), pcs.ap[:, 32:64], AF.Exp), extra_r=[pcs])

        qT = [None] * 8
        kT = [None] * 8
        vT = [None] * 8
        sz = [None] * 8
        grp = {}

        def S1(g):
            w_t, w = next_slab()
            st = grp[g] = {"a": [None] * 4}
            pgrp = [next_ps() for _ in range(4)]
            for kc in range(8):
                for j in range(4):
                    mm(pgrp[j], pgrp[j].ap, w_t, w[:, kc, j * 128:(j + 1) * 128], hn[kc], hn[kc].ap, kc == 0, kc == 7)
            for j in range(4):
                cc = g * 4 + j
                ps = pgrp[j]
                if cc >= 24:
                    o = p16.get()
                    act(o, o.ap, ps, ps.ap, AF.Silu)
                    sz[cc - 24] = o
                    continue
                pre = pre_tmp[cc % 3]
                act(pre, pre.ap[:, 3:], ps, ps.ap, AF.Copy)
                if first:
                    P.op("pool", (lambda pre: lambda e: e.memset(pre.ap[:, 0:3], 0.0))(pre), writes=[pre])
                else:
                    P.op("pool", (lambda pre, cc: lambda e: e.tensor_copy(
                        out=pre.ap[:, 0:3], in_=halo1.ap[:, cc * 3:cc * 3 + 3]))(pre, cc),
                        reads=[halo1], writes=[pre])
                P.op("pool", (lambda pre, cc: lambda e: e.tensor_copy(
                    out=halo1.ap[:, cc * 3:cc * 3 + 3], in_=pre.ap[:, NT:NT + 3]))(pre, cc),
                    reads=[pre], writes=[halo1])
                a = p32.get()
                P.op("dve", (lambda a, pre, cc: lambda e: e.tensor_scalar(
                    out=a.ap, in0=pre.ap[:, 0:NT], scalar1=pvc("gcw", cc), scalar2=None, op0=ALU.mult))(a, pre, cc),
                    reads=[pre, pv], writes=[a])
                for k in range(1, 4):
                    P.op("dve", (lambda a, pre, cc, k: lambda e: e.scalar_tensor_tensor(
                        out=a.ap, in0=pre.ap[:, k:k + NT], scalar=pvc("gcw", k * 24 + cc), in1=a.ap,
                        op0=ALU.mult, op1=ALU.add))(a, pre, cc, k), reads=[pre, pv, a], writes=[a])
                st["a"][j] = a

        def S2(g):
            if g >= 6:
                return
            for j in range(4):
                cc = g * 4 + j
                a = grp[g]["a"][j]
                if cc >= 16:
                    o = p16.get()
                    act(o, o.ap, a, a.ap, AF.Silu)
                    vT[cc - 16] = o
                    p32.put(a)
                else:
                    act(a, a.ap, a, a.ap, AF.Silu)

        def S3(g):
            if g >= 4:
                return
            pss = []
            sqs = []
            for j in range(4):
                a = grp[g]["a"][j]
                sq = p16.get()
                act(sq, sq.ap, a, a.ap, AF.Square)
                ps2 = next_ps()
                mm(ps2, ps2.ap, onesH, onesH.ap, sq, sq.ap, True, True)
                pss.append(ps2)
                sqs.append(sq)
            for sq in sqs:
                p16.put(sq)
            rs = []
            for j in range(4):
                t = p32.get()
                act(t, t.ap, pss[j], pss[j].ap, AF.Ln, bias=eps_ap(EPS), extra_reads=[epsT])
                rs.append(t)
            for j in range(4):
                act(rs[j], rs[j].ap, rs[j], rs[j].ap, AF.Exp, scale=-0.5)
            for j in range(4):
                cc = g * 4 + j
                a = grp[g]["a"][j]
                r = rs[j]
                o = p16.get()
                if cc < 8:
                    P.op("dve", (lambda o, a, r: lambda e: e.scalar_tensor_tensor(
                        out=o.ap, in0=a.ap, scalar=float(128 ** -0.5), in1=r.ap, op0=ALU.mult, op1=ALU.mult))(o, a, r),
                        reads=[a, r], writes=[o])
                    qT[cc] = o
                else:
                    P.op("dve", (lambda o, a, r: lambda e: e.tensor_tensor(
                        out=o.ap, in0=a.ap, in1=r.ap, op=ALU.mult))(o, a, r), reads=[a, r], writes=[o])
                    kT[cc - 8] = o
                p32.put(a)
                p32.put(r)

        for step in range(7):
            if step < 6:
                S1(step)
            if step >= 1:
                S2(step - 1)
                S3(step - 1)

        oT = [p32.get() for _ in range(8)]
        triu_b8 = triu32.unsqueeze(1).broadcast_to([128, 8, 128])
        stri_b8 = cst.ap[:, 384:512].unsqueeze(1).broadcast_to([128, 8, 128])
        ident_b4 = ident32.unsqueeze(1).broadcast_to([128, 4, 128])
        def lmask(l):
            return cst.ap[:, 640 + l * 128:640 + (l + 1) * 128].unsqueeze(1).broadcast_to([128, 4, 128])

        def prep(b, B):
            tb = slice(b * 128, (b + 1) * 128)

            def gb(i):
                return gs.ap[:, i * 32 + b * 8:i * 32 + b * 8 + 8]

            def g1(i, hh):
                return gs.ap[:, i * 32 + b * 8 + hh:i * 32 + b * 8 + hh + 1]
            P.op("dve", lambda e: e.tensor_copy(
                out=v3(GE.ap, 8), in_=gb(2).unsqueeze(2).broadcast_to([128, 8, 128])),
                reads=[gs], writes=[GE])
            pR = [next_ps(), next_ps()]
            pRm = [next_ps(), next_ps()]
            for hh in range(8):
                mm(pR[hh // 4], q4(pR[hh // 4], hh % 4), GE, hs(GE, hh), cst, triu32, True, True)
                mm(pRm[hh // 4], q4(pRm[hh // 4], hh % 4), GE, hs(GE, hh), cst, triu32, True, False)
                mm(pRm[hh // 4], q4(pRm[hh // 4], hh % 4), cbf, identbf, cbf, masknegbf, False, True)
            ptr = next_ps()
            ptr_bf = ptr.ap.bitcast(BF16)
            for hh in range(8):
                P.op("pe", lambda e, hh=hh: e.transpose(
                    out=ptr_bf[:, hh * 128:(hh + 1) * 128], in_=kT[hh].ap[:, tb], identity=identbf),
                    reads=[kT[hh], cbf], writes=[ptr])
            ptv = next_ps()
            ptv_bf = ptv.ap.bitcast(BF16)
            for hh in range(8):
                P.op("pe", lambda e, hh=hh: e.transpose(
                    out=ptv_bf[:, hh * 128:(hh + 1) * 128], in_=vT[hh].ap[:, tb], identity=identbf),
                    reads=[vT[hh], cbf], writes=[ptv])
            for hg in range(2):
                act(egcR, egcR.ap[:, hg * 512:(hg + 1) * 512], pR[hg], pR[hg].ap, AF.Exp)
            for hh in range(8):
                act(dec, hs(dec, hh), pRm[hh // 4], q4(pRm[hh // 4], hh % 4), AF.Exp,
                    bias=g1(10, hh), extra_reads=[gs])
            P.op("dve", lambda e: e.tensor_tensor(
                out=v3(decs.ap, 8), in0=v3(dec.ap, 8), in1=stri_b8, op=ALU.mult),
                reads=[dec, cst], writes=[decs])
            P.op("dve", lambda e: e.tensor_tensor(
                out=v3(decs.ap, 8), in0=v3(decs.ap, 8), in1=gb(5).unsqueeze(2).broadcast_to([128, 8, 128]), op=ALU.mult),
                reads=[decs, gs], writes=[decs])
            P.op("dve", lambda e: e.tensor_tensor(
                out=v3(B.Yk.ap, 8), in0=v3(ptr_bf, 8), in1=gb(7).unsqueeze(2).broadcast_to([128, 8, 128]), op=ALU.mult),
                reads=[ptr, gs], writes=[B.Yk])
            P.op("dve", lambda e: e.tensor_tensor(
                out=v3(B.kd.ap, 8), in0=v3(ptr_bf, 8), in1=gb(8).unsqueeze(2).broadcast_to([128, 8, 128]), op=ALU.mult),
                reads=[ptr, gs], writes=[B.kd])
            act(B.vtok, B.vtok.ap, ptv, ptv_bf, AF.Copy)
            for hg in range(2):
                pkk = next_ps()
                pqk = next_ps()
                for h4 in range(4):
                    hh = hg * 4 + h4
                    mm(pkk, q4(pkk, h4), kT[hh], kT[hh].ap[:, tb], kT[hh], kT[hh].ap[:, tb], True, True)
                    mm(pqk, q4(pqk, h4), kT[hh], kT[hh].ap[:, tb], qT[hh], qT[hh].ap[:, tb], True, True)
                P.op("dve", lambda e, hg=hg, pkk=pkk: e.tensor_tensor(
                    out=B.TW[hg].ap, in0=pkk.ap, in1=decs.ap[:, hg * 512:(hg + 1) * 512], op=ALU.mult),
                    reads=[pkk, decs], writes=[B.TW[hg]])
                P.op("dve", lambda e, hg=hg, pqk=pqk: e.tensor_tensor(
                    out=B.qkT.ap[:, hg * 512:(hg + 1) * 512], in0=pqk.ap, in1=dec.ap[:, hg * 512:(hg + 1) * 512], op=ALU.mult),
                    reads=[pqk, dec], writes=[B.qkT])
            for hh in range(8):
                P.op("pool", lambda e, hh=hh: e.tensor_tensor(
                    out=hs(B.qgT, hh), in0=qT[hh].ap[:, tb], in1=hs(egcR, hh), op=ALU.mult),
                    reads=[qT[hh], egcR], writes=[B.qgT])
            for hg in range(2):
                pp = next_ps()
                pp_b = pp.ap.bitcast(BF16)
                for h4 in range(4):
                    P.op("pe", lambda e, h4=h4, hg=hg, pp_b=pp_b: e.transpose(
                        out=pp_b[:, h4 * 128:(h4 + 1) * 128], in_=q4(B.TW[hg], h4), identity=identbf),
                        reads=[B.TW[hg], cbf], writes=[pp])
                act(B.P0[hg], B.P0[hg].ap, pp, pp_b[:, 0:512], AF.Copy)
                P.op("dve", lambda e, hg=hg: e.tensor_tensor(
                    out=v3(B.Am[hg].ap, 4), in0=v3(B.TW[hg].ap, 4), in1=lmask(0), op=ALU.mult),
                    reads=[B.TW[hg], cst], writes=[B.Am[hg]])
                P.op("dve", lambda e, hg=hg: e.tensor_tensor(
                    out=v3(B.Am[hg].ap, 4), in0=v3(B.Am[hg].ap, 4), in1=ident_b4, op=ALU.add),
                    reads=[B.Am[hg], cst], writes=[B.Am[hg]])
            for hg in range(2):
                px = next_ps()
                px_b = px.ap.bitcast(BF16)
                for h4 in range(4):
                    P.op("pe", lambda e, h4=h4, hg=hg, px_b=px_b: e.transpose(
                        out=px_b[:, h4 * 128:(h4 + 1) * 128], in_=q4(B.Am[hg], h4), identity=identbf),
                        reads=[B.Am[hg], cbf], writes=[px])
                act(B.Xm[hg], B.Xm[hg].ap, px, px_b[:, 0:512], AF.Copy)

        def level(B, l):
            for hg in range(2):
                pW = next_ps()
                for h4 in range(4):
                    mm(pW, q4(pW, h4), B.P0[hg], q4(B.P0[hg], h4), B.Am[hg], q4(B.Am[hg], h4), True, True)
                P.op("dve", lambda e, hg=hg, pW=pW: e.tensor_tensor(
                    out=v3(B.TW[hg].ap, 4), in0=v3(pW.ap, 4), in1=lmask(l), op=ALU.mult),
                    reads=[pW, cst], writes=[B.TW[hg]])
            for hg in range(2):
                pA = next_ps()
                for h4 in range(4):
                    mm(pA, q4(pA, h4), B.Xm[hg], q4(B.Xm[hg], h4), B.TW[hg], q4(B.TW[hg], h4), True, False)
                    mm(pA, q4(pA, h4), cbf, identbf, B.Am[hg], q4(B.Am[hg], h4), False, True)
                if l < 6:
                    pX = next_ps()
                    for h4 in range(4):
                        mm(pX, q4(pX, h4), B.TW[hg], q4(B.TW[hg], h4), B.Xm[hg], q4(B.Xm[hg], h4), True, False)
                        mm(pX, q4(pX, h4), cbf, identbf, B.Xm[hg], q4(B.Xm[hg], h4), False, True)
                act(B.Am[hg], B.Am[hg].ap, pA, pA.ap, AF.Copy)
                if l < 6:
                    act(B.Xm[hg], B.Xm[hg].ap, pX, pX.ap, AF.Copy)

        def post(B):
            for hg in range(2):
                pz = next_ps()
                for h4 in range(4):
                    hh = hg * 4 + h4
                    mm(pz, q4(pz, h4), B.Yk, hs(B.Yk, hh), B.Am[hg], q4(B.Am[hg], h4), True, True)
                act(B.nZw[hg], B.nZw[hg].ap, pz, pz.ap, AF.Copy, scale=-1.0)

        def scan(b, B):
            tb = slice(b * 128, (b + 1) * 128)

            def gb(i):
                return gs.ap[:, i * 32 + b * 8:i * 32 + b * 8 + 8]

            def g1(i, hh):
                return gs.ap[:, i * 32 + b * 8 + hh:i * 32 + b * 8 + hh + 1]
            for hg in range(2):
                pvn = next_ps()
                for h4 in range(4):
                    hh = hg * 4 + h4
                    mm(pvn, q4(pvn, h4), B.Am[hg], q4(B.Am[hg], h4), B.vtok, hs(B.vtok, hh), True, False)
                    mm(pvn, q4(pvn, h4), B.nZw[hg], q4(B.nZw[hg], h4), Sbf[hg], q4(Sbf[hg], h4), False, True)
                P.op("dve", lambda e, hg=hg, pvn=pvn: e.tensor_tensor(
                    out=v3(B.vnew[hg].ap, 4), in0=v3(pvn.ap, 4),
                    in1=gb(4)[:, hg * 4:hg * 4 + 4].unsqueeze(2).broadcast_to([128, 4, 128]), op=ALU.mult),
                    reads=[pvn, gs], writes=[B.vnew[hg]])
            for hg in range(2):
                po = next_ps()
                for h4 in range(4):
                    hh = hg * 4 + h4
                    mm(po, q4(po, h4), Sbf[hg], q4(Sbf[hg], h4), B.qgT, hs(B.qgT, hh), True, False)
                    mm(po, q4(po, h4), B.vnew[hg], q4(B.vnew[hg], h4), B.qkT, hs(B.qkT, hh), False, True)
                for h4 in range(4):
                    hh = hg * 4 + h4
                    act(oT[hh], oT[hh].ap[:, tb], po, q4(po, h4), AF.Copy)
            for hg in range(2):
                pS = next_ps()
                for h4 in range(4):
                    hh = hg * 4 + h4
                    mm(pS, q4(pS, h4), B.kd, hs(B.kd, hh), B.vnew[hg], q4(B.vnew[hg], h4), True, True)
                for h4 in range(4):
                    hh = hg * 4 + h4
                    P.op("dve", lambda e, hh=hh, h4=h4, hg=hg, pS=pS: e.scalar_tensor_tensor(
                        out=q4(S32[hg], h4), in0=q4(S32[hg], h4), scalar=g1(9, hh), in1=q4(pS, h4),
                        op0=ALU.mult, op1=ALU.add), reads=[S32[hg], gs, pS], writes=[S32[hg]])
                act(Sbf[hg], Sbf[hg].ap, S32[hg], S32[hg].ap, AF.Copy)

        for pr in range(2):
            b0, b1 = 2 * pr, 2 * pr + 1
            prep(b0, BS[0])
            prep(b1, BS[1])
            S1(6 + pr)
            for l in range(1, 7):
                level(BS[0], l)
                level(BS[1], l)
            post(BS[0])
            post(BS[1])
            scan(b0, BS[0])
            scan(b1, BS[1])
        for t in hn + qT + kT + vT:
            p16.put(t)
        onT = []
        for hh in range(8):
            sq = p16.get()
            act(sq, sq.ap, oT[hh], oT[hh].ap, AF.Square)
            ps = next_ps()
            mm(ps, ps.ap, onesE, onesE.ap, sq, sq.ap, True, True)
            p16.put(sq)
            r = rstd_from(ps)
            P.op("dve", lambda e, hh=hh, r=r: e.scalar_tensor_tensor(
                out=oT[hh].ap, in0=oT[hh].ap, scalar=pvc("gng"), in1=r.ap, op0=ALU.mult, op1=ALU.mult),
                reads=[oT[hh], pv, r], writes=[oT[hh]])
            o = p16.get()
            P.op("dve", lambda e, hh=hh, o=o: e.tensor_tensor(
                out=o.ap, in0=oT[hh].ap, in1=sz[hh].ap, op=ALU.mult), reads=[oT[hh], sz[hh]], writes=[o])
            onT.append(o)
            p32.put(r)
            p32.put(oT[hh])
            p16.put(sz[hh])
        for half in range(2):
            w_t, w = next_slab()
            pgrp = [next_ps() for _ in range(4)]
            for kc in range(8):
                for j in range(4):
                    mm(pgrp[j], pgrp[j].ap, w_t, w[:, kc, j * 128:(j + 1) * 128], onT[kc], onT[kc].ap, kc == 0, kc == 7)
            for j in range(4):
                mc = half * 4 + j
                ps = pgrp[j]
                P.op("dve", lambda e, ps=ps, mc=mc: e.tensor_tensor(
                    out=h[mc].ap, in0=h[mc].ap, in1=ps.ap, op=ALU.add), reads=[ps, h[mc]], writes=[h[mc]])
        for t in onT:
            p16.put(t)

    def load_x(t):
        hb = hbuf[t % 2]
        for c in range(8):
            P.op("sp", (lambda c, hb, t: lambda e: e.dma_start(
                out=hb[c].ap, in_=xT[c * 128:(c + 1) * 128, t * NT:(t + 1) * NT]))(c, hb, t),
                writes=[hb[c]], dma=("x", t % 2, c))

    store_ops = []

    def tile_body(t):
        h = hbuf[t % 2]
        first = (t % tiles_per_seq) == 0
        if t + 1 < ntiles:
            load_x(t + 1)
        for li in range(depth):
            hn = rmsnorm(h, f"nmg{li}")
            if li % 2 == 0:
                conformer(h, hn, first)
            else:
                gdn(h, hn, first)
            hn = rmsnorm(h, f"nfg{li}")
            mlp(h, hn)
        ps = next_ps()
        for c in range(8):
            sq = p16.get()
            act(sq, sq.ap, h[c], h[c].ap, AF.Square)
            mm(ps, ps.ap, onesD, onesD.ap, sq, sq.ap, c == 0, c == 7)
            p16.put(sq)
        r = rstd_from(ps)
        for c in range(8):
            o = p32.get()
            P.op("dve", (lambda o, c: lambda e: e.scalar_tensor_tensor(
                out=o.ap, in0=h[c].ap, scalar=pvc("fng", c), in1=r.ap, op0=ALU.mult, op1=ALU.mult))(o, c),
                reads=[h[c], pv, r], writes=[o])
            i = P.op("pool", (lambda o, c: lambda e: e.dma_start(
                out=outT[c * 128:(c + 1) * 128, t * NT:(t + 1) * NT], in_=o.ap))(o, c),
                reads=[o], dma=("st", o.name))
            store_ops.append(i)
            p32.put(o)
        p32.put(r)

    load_x(0)
    for t in range(ntiles):
        tile_body(t)
    fin = P.op("pool", lambda e: e.nop(), reads=[], writes=[])
    P.ops[fin][2].update(store_ops)
    P.emit(nc)
    return nc, P


_CACHE = {}


def _run(inputs, nseq, S, depth, ncores):
    key = (nseq, S, depth)
    if key not in _CACHE:
        _CACHE[key] = build_nc(nseq, S, depth)
    nc, P = _CACHE[key]
    x = np.asarray(inputs["x"], np.float32)
    B = x.shape[0]
    assert B == nseq * ncores
    pvec = pack_pvec(inputs)
    cst = make_consts()
    in_maps = []
    for i in range(ncores):
        xs = x[i * nseq:(i + 1) * nseq].reshape(nseq * S, D)
        m = {"xT": np.ascontiguousarray(xs.T), "pvec": pvec, "cst": cst,
             "cv_w_pw1": np.asarray(inputs["cv_w_pw1"], np.float32),
             "cv_w_pw2": np.asarray(inputs["cv_w_pw2"], np.float32),
             "mlp_w1": np.asarray(inputs["mlp_w1"], np.float32),
             "mlp_w2": np.asarray(inputs["mlp_w2"], np.float32)}
        if depth > 1:
            m["gdn_w_in"] = np.asarray(inputs["gdn_w_in"], np.float32)
            m["gdn_w_out"] = np.asarray(inputs["gdn_w_out"], np.float32)
        in_maps.append(m)
    res = run_bass_kernel_spmd(nc, in_maps, core_ids=list(range(ncores)))
    outs = [np.asarray(r["outT"]).T.reshape(nseq, S, D) for r in res.results]
    return np.ascontiguousarray(np.concatenate(outs, axis=0).astype(np.float32))


def kernel(**inputs):
    return _run(inputs, 2, 4096, 2, 8)
```

```python
import numpy as np
import concourse.bass as bass
import concourse.mybir as mybir
from concourse.bass_utils import run_bass_kernel_spmd

F32 = mybir.dt.float32
F32R = mybir.dt.float32r
CHAIN_R = [False]
BF16 = mybir.dt.bfloat16
AF = mybir.ActivationFunctionType
ALU = mybir.AluOpType

D = 1024
NC8 = 8
NT = 512
EPS = 1e-6
KW = 31
H = 8
QKV = 3072
DEBUG_STOP = [99]
GIN = 4112

COMPUTE = ("pe", "act", "dve", "pool")


class T:
    __slots__ = ("ap", "w", "r", "name")

    def __init__(self, ap, name=""):
        self.ap = ap
        self.w = None
        self.r = {}
        self.name = name


class TV:
    __slots__ = ("ap", "base", "name")

    def __init__(self, base, ap):
        self.base = base
        self.ap = ap
        self.name = base.name + "_v"

    @property
    def w(self):
        return self.base.w

    @w.setter
    def w(self, v):
        self.base.w = v

    @property
    def r(self):
        return self.base.r

    @r.setter
    def r(self, v):
        self.base.r = v


class Prog:
    def __init__(self):
        self.ops = []
        self.last_dma = {}

    def op(self, eng, fn, reads=(), writes=(), dma=None):
        i = len(self.ops)
        deps = set()
        for t in reads:
            if t.w is not None:
                deps.add(t.w)
        for t in writes:
            if t.w is not None:
                deps.add(t.w)
            deps.update(t.r.values())
        if dma is not None and dma in self.last_dma:
            deps.add(self.last_dma[dma])
        if dma is not None:
            self.last_dma[dma] = i
        for t in writes:
            t.w = i
            t.r = {}
        for t in reads:
            if dma is not None:
                t.r[("dma", i)] = i
            else:
                t.r[eng] = i
        self.ops.append((eng, fn, deps, dma))
        return i

    def emit(self, nc, sem_limit=30000):
        ops = self.ops
        n = len(ops)
        red = []
        needed = [False] * n
        for i, (eng, fn, deps, dma) in enumerate(ops):
            best = {}
            dl = []
            for d in deps:
                de, _, _, ddma = ops[d]
                if ddma is not None:
                    dl.append(d)
                else:
                    if de == eng and eng == "pe":
                        continue
                    if de not in best or best[de] < d:
                        best[de] = d
            dl.extend(best.values())
            red.append(dl)
            for d in dl:
                needed[d] = True
        ordinal = [0] * n
        cnt = {}
        dcnt = {}
        for i, (eng, fn, deps, dma) in enumerate(ops):
            if dma is not None:
                dcnt[dma] = dcnt.get(dma, 0) + 1
                ordinal[i] = 16 * dcnt[dma]
            elif needed[i]:
                cnt[eng] = cnt.get(eng, 0) + 1
                ordinal[i] = cnt[eng]
        self.stats = dict(cnt)
        sems = {}

        def get_sem(key):
            if key not in sems:
                sems[key] = nc.alloc_semaphore(name="s_" + "_".join(str(k) for k in key))
            return sems[key]

        def signal_of(i):
            eng, _, _, dma = ops[i]
            if dma is not None:
                return ("d", dma), ordinal[i], ordinal[i]
            o = ordinal[i]
            ep = (o - 1) // sem_limit
            return ("e", eng, ep), o - ep * sem_limit, o

        for i in range(n):
            if ops[i][3] is not None or needed[i]:
                get_sem(signal_of(i)[0])

        by_eng = {}
        for i, o in enumerate(ops):
            by_eng.setdefault(o[0], []).append(i)

        def run_engine(ename, e):
            known = {}
            nwait = 0
            for i in by_eng.get(ename, []):
                eng, fn, deps, dma = ops[i]
                for d in red[i]:
                    key, local, glob = signal_of(d)
                    kk = key if key[0] == "d" else ("e", key[1])
                    if known.get(kk, 0) >= glob:
                        continue
                    known[kk] = glob
                    e.wait_ge(get_sem(key), local)
                    nwait += 1
                ins = fn(e)
                if dma is not None:
                    ins.then_inc(get_sem(("d", dma)), 16)
                elif needed[i]:
                    ins.then_inc(get_sem(signal_of(i)[0]), 1)
            self.stats["wait_" + ename] = nwait

        with nc.Block() as block:
            @block.tensor
            def _(e):
                run_engine("pe", e)

            @block.scalar
            def _(e):
                run_engine("act", e)

            @block.vector
            def _(e):
                run_engine("dve", e)

            @block.gpsimd
            def _(e):
                run_engine("pool", e)

            @block.sync
            def _(e):
                run_engine("sp", e)


class UnitPool:
    def __init__(self, nc, n, dtype, name, width=NT):
        self.free = []
        for i in range(n):
            ap = nc.alloc_sbuf_tensor(f"{name}{i}", [128, width], dtype).ap()
            self.free.append(T(ap, f"{name}{i}"))
        self.n = n

    def get(self):
        if not self.free:
            raise RuntimeError("unit pool exhausted")
        return self.free.pop(0)

    def put(self, t):
        self.free.append(t)


def _pv_layout():
    cols = {}
    off = 0

    def add(name, n):
        nonlocal off
        cols[name] = off
        off += n
    for i in range(2):
        add(f"nmg{i}", 8)
        add(f"nfg{i}", 8)
    add("fng", 8)
    add("bpw1", 16)
    add("wdw", KW * 8)
    add("bdw", 8)
    add("lng", 8)
    add("lnb", 8)
    add("bpw2", 8)
    add("gcw", 4 * 24)
    add("gng", 1)
    add("alog", 8)
    add("dtb", 8)
    add("alog4", 32)
    add("dtb4", 32)
    return cols, off


PV, NPV = _pv_layout()


def _chunked(v):
    return np.ascontiguousarray(v.reshape(-1, 128).T)


def pack_pvec(inp):
    pv = np.zeros((128, NPV), np.float32)

    def put(name, arr):
        pv[:, PV[name]:PV[name] + arr.shape[1]] = arr
    for i in range(2):
        put(f"nmg{i}", _chunked(inp["norm_mix_g"][i]))
        put(f"nfg{i}", _chunked(inp["norm_ffn_g"][i]))
    put("fng", _chunked(inp["final_norm_g"]))
    put("bpw1", _chunked(inp["cv_b_pw1"][0]))
    wdw = inp["cv_w_dw"][0]
    put("wdw", np.concatenate([_chunked(wdw[k]) for k in range(KW)], axis=1))
    put("bdw", _chunked(inp["cv_b_dw"][0]))
    put("lng", _chunked(inp["cv_ln_g"][0]))
    put("lnb", _chunked(inp["cv_ln_b"][0]))
    put("bpw2", _chunked(inp["cv_b_pw2"][0]))
    gcw = inp["gdn_conv_w"][0]
    put("gcw", np.concatenate([_chunked(gcw[k]) for k in range(4)], axis=1))
    put("gng", inp["gdn_norm_g"][0].reshape(128, 1))
    put("alog", np.broadcast_to(inp["gdn_a_log"][0][None, :], (128, 8)))
    put("dtb", np.broadcast_to(inp["gdn_dt_bias"][0][None, :], (128, 8)))
    put("alog4", np.broadcast_to(np.tile(inp["gdn_a_log"][0], 4)[None, :], (128, 32)))
    put("dtb4", np.broadcast_to(np.tile(inp["gdn_dt_bias"][0], 4)[None, :], (128, 32)))
    return pv


def make_consts():
    idx = np.arange(128)
    ident = np.eye(128, dtype=np.float32)
    triu = (idx[:, None] <= idx[None, :]).astype(np.float32)
    ones = np.ones((128, 128), np.float32)
    stri = (idx[:, None] < idx[None, :]).astype(np.float32)
    maskneg = np.where(idx[:, None] <= idx[None, :], 0.0, -30000.0).astype(np.float32)
    lv = []
    for l in range(7):
        sz = 1 << l
        i = idx[None, :]
        j = idx[:, None]
        m = ((i // (2 * sz)) == (j // (2 * sz))) & ((i % (2 * sz)) >= sz) & ((j % (2 * sz)) < sz)
        lv.append(m.astype(np.float32))
    return np.concatenate([ident, triu, ones, stri, maskneg] + lv, axis=1)


def slab_table(depth):
    sl = []
    for li in range(depth):
        if li % 2 == 0:
            for half in range(2):
                sl.append(("cv_w_pw1", 0, 0, half * 512))
                sl.append(("cv_w_pw1", 0, 0, 1024 + half * 512))
                for j in range(4):
                    sl.append(("diag", half * 4 + j, 0, 0))
            for m0 in (0, 512):
                sl.append(("cv_w_pw2", 0, 0, m0))
        else:
            for m0 in range(0, 4096, 512):
                sl.append(("gdn_w_in", 0, 0, m0))
            for m0 in (0, 512):
                sl.append(("gdn_w_out", 0, 0, m0))
        for m0 in range(0, 4096, 512):
            sl.append(("mlp_w1", li, 0, m0))
        for mg in range(2):
            for kg in range(4):
                sl.append(("mlp_w2", li, kg * 1024, mg * 512))
    return sl


def build_nc(nseq, S, depth, nring=3):
    ntok = nseq * S
    ntiles = ntok // NT
    tiles_per_seq = S // NT
    nc = bass.Bass("TRN2", target_bir_lowering=False)
    P = Prog()

    xT = nc.dram_tensor("xT", [D, ntok], F32, kind="ExternalInput").ap()
    outT = nc.dram_tensor("outT", [D, ntok], F32, kind="ExternalOutput").ap()
    pvec_d = nc.dram_tensor("pvec", [128, NPV], F32, kind="ExternalInput").ap()
    cst_d = nc.dram_tensor("cst", [128, 1536], F32, kind="ExternalInput").ap()
    wd = {}
    wd["cv_w_pw1"] = nc.dram_tensor("cv_w_pw1", [1, D, 2 * D], F32, kind="ExternalInput").ap()
    wd["cv_w_pw2"] = nc.dram_tensor("cv_w_pw2", [1, D, D], F32, kind="ExternalInput").ap()
    wd["mlp_w1"] = nc.dram_tensor("mlp_w1", [2, D, 4 * D], F32, kind="ExternalInput").ap()
    wd["mlp_w2"] = nc.dram_tensor("mlp_w2", [2, 4 * D, D], F32, kind="ExternalInput").ap()
    if depth > 1:
        wd["gdn_w_in"] = nc.dram_tensor("gdn_w_in", [1, D, GIN], F32, kind="ExternalInput").ap()
        wd["gdn_w_out"] = nc.dram_tensor("gdn_w_out", [1, D, D], F32, kind="ExternalInput").ap()
    slabs = slab_table(depth)
    nslab = len(slabs)
    wbf = nc.dram_tensor("wbf", [nslab, 128, 4096], BF16, kind="Internal").ap()
    wdiag = nc.dram_tensor("wdiag", [8, 128, 4096], BF16, kind="Internal").ap()

    def sb(name, shape, dt):
        return nc.alloc_sbuf_tensor(name, shape, dt).ap()

    pv = T(sb("pv", [128, NPV], F32), "pv")
    cst = T(sb("cst_sb", [128, 1536], F32), "cst")
    cbf = T(sb("cbf", [128, 640], BF16), "cbf")
    onesD = T(sb("onesD", [128, 128], BF16), "onesD")
    onesH = T(sb("onesH", [128, 128], BF16), "onesH")
    onesD32 = T(sb("onesD32", [128, 128], F32), "onesD32")
    hbuf = [[T(sb(f"h{b}_{c}", [128, NT], F32), f"h{b}_{c}") for c in range(8)] for b in range(2)]
    ring = [T(sb(f"ring{i}", [128, 4096], BF16), f"ring{i}") for i in range(nring)]
    p16 = UnitPool(nc, 41, BF16, "u16_")
    p32 = UnitPool(nc, 14, F32, "u32_")
    glu_tmp = [T(sb(f"glu{i}", [128, NT + KW - 1], BF16), f"glu{i}") for i in range(8)]
    halo0 = [T(sb(f"halo0_{c}", [128, KW - 1], BF16), f"halo0_{c}") for c in range(8)]
    psum = [T(nc.alloc_psum_tensor(f"ps{i}", [128, NT], F32).ap(), f"ps{i}") for i in range(8)]
    ps_i = [0]

    def next_ps():
        t = psum[ps_i[0] % 8]
        ps_i[0] += 1
        return t

    def pvc(name, j=0):
        o = PV[name] + j
        return pv.ap[:, o:o + 1]

    ident32 = cst.ap[:, 0:128]
    triu32 = cst.ap[:, 128:256]
    ones32 = cst.ap[:, 256:384]
    identbf = cbf.ap[:, 0:128]
    stribf = cbf.ap[:, 384:512]
    maskneg32 = cst.ap[:, 512:640]
    masknegbf = cbf.ap[:, 512:640]

    P.op("sp", lambda e: e.dma_start(out=pv.ap, in_=pvec_d), writes=[pv], dma="pv")
    P.op("sp", lambda e: e.dma_start(out=cst.ap, in_=cst_d), writes=[cst], dma="cst")
    P.op("dve", lambda e: e.tensor_copy(out=cbf.ap, in_=cst.ap[:, 0:640]), reads=[cst], writes=[cbf])
    P.op("dve", lambda e: e.memset(onesD.ap, 1.0 / D), writes=[onesD])
    P.op("dve", lambda e: e.memset(onesH.ap, 1.0), writes=[onesH])
    P.op("dve", lambda e: e.memset(onesD32.ap, 1.0 / D), writes=[onesD32])
    for c in range(8):
        P.op("dve", (lambda c: lambda e: e.memset(halo0[c].ap, 0.0))(c), writes=[halo0[c]])

    cast_done_ops = []
    for s_, (wn, li, k0, m0) in enumerate(slabs):
        if wn == "diag":
            continue
        src = wd[wn][li, k0:k0 + 1024, m0:m0 + 512].rearrange("(kc p) m -> p kc m", p=128)
        dst = wbf[s_].rearrange("p (kc m) -> p kc m", kc=8)
        P.op("pool", (lambda src, dst: lambda e: e.dma_start(out=dst, in_=src))(src, dst),
             writes=[], dma=("wc", s_ % 4))
    cast_done_ops += [P.last_dma[("wc", k)] for k in range(4) if ("wc", k) in P.last_dma]
    for c in range(8):
        stage = ring[c % nring]
        for k in range(KW):
            P.op("dve", (lambda stage, c, k: lambda e: e.tensor_scalar(
                out=stage.ap[:, k * 128:(k + 1) * 128], in0=ident32, scalar1=pvc("wdw", k * 8 + c), scalar2=None,
                op0=ALU.mult))(stage, c, k), reads=[cst, pv], writes=[stage])
        P.op("dve", (lambda stage: lambda e: e.memset(stage.ap[:, KW * 128:], 0.0))(stage), writes=[stage])
        i = P.op("sp", (lambda stage, c: lambda e: e.dma_start(out=wdiag[c], in_=stage.ap))(stage, c),
                 reads=[stage], dma=("dg", c % nring))
        cast_done_ops.append(i)

    slab_ctr = [0]

    def next_slab(tile_first_use_guard=None):
        g = slab_ctr[0]
        slab_ctr[0] += 1
        s = g % nslab
        slot = ring[g % nring]
        srcap = wdiag[slabs[s][1]] if slabs[s][0] == "diag" else wbf[s]
        i = P.op("sp", (lambda srcap, slot: lambda e: e.dma_start(out=slot.ap, in_=srcap))(srcap, slot),
                 writes=[slot], dma=("ring", g % nring))
        if g < nring:
            P.ops[i][2].update(cast_done_ops)
        elif g < nslab:
            pass
        return slot, slot.ap.rearrange("p (kc m) -> p kc m", kc=8)


    def mm(ps, ps_ap, lt, lt_ap, rt, rt_ap, start, stop):
        P.op("pe", lambda e: e.matmul(ps_ap, lt_ap, rt_ap, start=start, stop=stop),
             reads=[lt, rt], writes=[ps])

    def act(out_t, out_ap, in_t, in_ap, func, bias=None, scale=None, extra_reads=()):
        kw = {}
        if bias is not None:
            kw["bias"] = bias
        if scale is not None:
            kw["scale"] = scale
        P.op("act", lambda e: e.activation(out_ap, in_ap, func, **kw),
             reads=[in_t, *extra_reads], writes=[out_t])

    def rstd_from(ps_stat, eps=EPS):
        t = p32.get()
        act(t, t.ap, ps_stat, ps_stat.ap, AF.Ln, bias=eps_ap(eps), extra_reads=[epsT])
        act(t, t.ap, t, t.ap, AF.Exp, scale=-0.5)
        return t

    epsT = T(sb("epsc", [128, 2], F32), "epsc")
    P.op("dve", lambda e: e.memset(epsT.ap[:, 0:1], EPS), writes=[epsT])
    P.op("dve", lambda e: e.memset(epsT.ap[:, 1:2], 1.0), writes=[epsT])

    def eps_ap(eps):
        return epsT.ap[:, 0:1]

    def rmsnorm(h, gname):
        ps = next_ps()
        for c in range(8):
            sq = p16.get()
            act(sq, sq.ap, h[c], h[c].ap, AF.Square)
            mm(ps, ps.ap, onesD, onesD.ap, sq, sq.ap, c == 0, c == 7)
            p16.put(sq)
        r = rstd_from(ps)
        out = []
        for c in range(8):
            o = p16.get()
            P.op("dve", (lambda o, c: lambda e: e.scalar_tensor_tensor(
                out=o.ap, in0=h[c].ap, scalar=pvc(gname, c), in1=r.ap, op0=ALU.mult, op1=ALU.mult))(o, c),
                reads=[h[c], pv, r], writes=[o])
            out.append(o)
        p32.put(r)
        return out

    def conformer(h, hn, first_of_seq):
        acc = [None] * 8
        for half in range(2):
            wa_t, wa = next_slab()
            wg_t, wg = next_slab()
            pga = [next_ps() for _ in range(4)]
            for kc in range(8):
                for j in range(4):
                    mm(pga[j], pga[j].ap, wa_t, wa[:, kc, j * 128:(j + 1) * 128], hn[kc], hn[kc].ap, kc == 0, kc == 7)
            pgg = [next_ps() for _ in range(4)]
            for kc in range(8):
                for j in range(4):
                    mm(pgg[j], pgg[j].ap, wg_t, wg[:, kc, j * 128:(j + 1) * 128], hn[kc], hn[kc].ap, kc == 0, kc == 7)
            for j in range(4):
                c = half * 4 + j
                psa = pga[j]
                psg = pgg[j]
                sg = p32.get()
                act(sg, sg.ap, psg, psg.ap, AF.Sigmoid, bias=pvc("bpw1", 8 + c), extra_reads=[pv])
                gt = glu_tmp[c]
                P.op("dve", (lambda gt, psa, sg, c: lambda e: e.scalar_tensor_tensor(
                    out=gt.ap[:, KW - 1:], in0=psa.ap, scalar=pvc("bpw1", c), in1=sg.ap,
                    op0=ALU.add, op1=ALU.mult))(gt, psa, sg, c),
                    reads=[psa, sg, pv], writes=[gt])
                p32.put(sg)
                if first_of_seq:
                    P.op("pool", (lambda gt: lambda e: e.memset(gt.ap[:, 0:KW - 1], 0.0))(gt), writes=[gt])
                else:
                    P.op("pool", (lambda gt, c: lambda e: e.tensor_copy(out=gt.ap[:, 0:KW - 1], in_=halo0[c].ap))(gt, c),
                         reads=[halo0[c]], writes=[gt])
                P.op("pool", (lambda gt, c: lambda e: e.tensor_copy(out=halo0[c].ap, in_=gt.ap[:, NT:NT + KW - 1]))(gt, c),
                     reads=[gt], writes=[halo0[c]])
            for j in range(4):
                c = half * 4 + j
                d_t, _ = next_slab()
                gt = glu_tmp[c]
                psc = next_ps()
                for k in range(KW):
                    mm(psc, psc.ap, d_t, d_t.ap[:, k * 128:(k + 1) * 128], gt, gt.ap[:, k:k + NT], k == 0, k == KW - 1)
                a = p32.get()
                act(a, a.ap, psc, psc.ap, AF.Identity, bias=pvc("bdw", c), extra_reads=[pv])
                acc[c] = a
        for t in hn:
            p16.put(t)
        psm = next_ps()
        for c in range(8):
            mm(psm, psm.ap, onesD32, onesD32.ap, acc[c], acc[c].ap, c == 0, c == 7)
        for c in range(8):
            P.op("dve", (lambda c: lambda e: e.tensor_tensor(
                out=acc[c].ap, in0=acc[c].ap, in1=psm.ap, op=ALU.subtract))(c),
                reads=[acc[c], psm], writes=[acc[c]])
        psv = next_ps()
        for c in range(8):
            sq = p16.get()
            act(sq, sq.ap, acc[c], acc[c].ap, AF.Square)
            mm(psv, psv.ap, onesD, onesD.ap, sq, sq.ap, c == 0, c == 7)
            p16.put(sq)
        r = rstd_from(psv)
        ua = []
        for c in range(8):
            P.op("dve", (lambda c: lambda e: e.tensor_tensor(
                out=acc[c].ap, in0=acc[c].ap, in1=r.ap, op=ALU.mult))(c),
                reads=[acc[c], r], writes=[acc[c]])
            o = p16.get()
            act(o, o.ap, acc[c], acc[c].ap, AF.Silu, bias=pvc("lnb", c), scale=pvc("lng", c), extra_reads=[pv])
            ua.append(o)
            p32.put(acc[c])
        p32.put(r)
        for half in range(2):
            w_t, w = next_slab()
            pgrp = [next_ps() for _ in range(4)]
            for kc in range(8):
                for j in range(4):
                    mm(pgrp[j], pgrp[j].ap, w_t, w[:, kc, j * 128:(j + 1) * 128], ua[kc], ua[kc].ap, kc == 0, kc == 7)
            for j in range(4):
                mc = half * 4 + j
                ps = pgrp[j]
                P.op("dve", (lambda ps, mc: lambda e: e.scalar_tensor_tensor(
                    out=h[mc].ap, in0=ps.ap, scalar=pvc("bpw2", mc), in1=h[mc].ap,
                    op0=ALU.add, op1=ALU.add))(ps, mc), reads=[ps, pv, h[mc]], writes=[h[mc]])
        for t in ua:
            p16.put(t)

    def mlp(h, hn):
        hid = []
        for g in range(8):
            w_t, w = next_slab()
            pgrp = [next_ps() for _ in range(4)]
            for kc in range(8):
                for j in range(4):
                    mm(pgrp[j], pgrp[j].ap, w_t, w[:, kc, j * 128:(j + 1) * 128], hn[kc], hn[kc].ap, kc == 0, kc == 7)
            for j in range(4):
                ps = pgrp[j]
                sq = p32.get()
                act(sq, sq.ap, ps, ps.ap, AF.Square)
                o = p16.get()
                P.op("dve", (lambda o, ps, sq: lambda e: e.scalar_tensor_tensor(
                    out=o.ap, in0=ps.ap, scalar=0.0, in1=sq.ap, op0=ALU.is_gt, op1=ALU.mult))(o, ps, sq),
                    reads=[ps, sq], writes=[o])
                p32.put(sq)
                hid.append(o)
        for t in hn:
            p16.put(t)
        for mg in range(2):
            pss = [next_ps() for _ in range(4)]
            for kg in range(4):
                w_t, w = next_slab()
                for j in range(4):
                    for kc in range(8):
                        mm(pss[j], pss[j].ap, w_t, w[:, kc, j * 128:(j + 1) * 128],
                           hid[kg * 8 + kc], hid[kg * 8 + kc].ap, kg == 0 and kc == 0, kg == 3 and kc == 7)
            for j in range(4):
                mc = mg * 4 + j
                P.op("dve", (lambda ps, mc: lambda e: e.tensor_tensor(
                    out=h[mc].ap, in0=h[mc].ap, in1=ps.ap, op=ALU.add))(pss[j], mc),
                    reads=[pss[j], h[mc]], writes=[h[mc]])
        for t in hid:
            p16.put(t)


    if depth > 1:
        wab32 = T(sb("wab32", [128, 8, 16], F32), "wab32")
        wab = T(sb("wab", [128, 8, 16], BF16), "wab")
        negA4 = T(sb("negA4", [128, 32], F32), "negA4")
        onesE = T(sb("onesE", [128, 128], BF16), "onesE")
        halo1 = T(sb("halo1", [128, 72], F32), "halo1")
        S32 = [T(sb(f"S32_{i}", [128, 512], F32), f"S32_{i}") for i in range(2)]
        Sbf = [T(sb(f"Sbf_{i}", [128, 512], BF16), f"Sbf_{i}") for i in range(2)]
        pre_tmp = [T(sb(f"pre{i}", [128, NT + 3], F32), f"pre{i}") for i in range(4)]
        GE = T(sb("GE", [128, 1024], F32), "GE")

        def b16(name):
            return T(sb(name, [128, 1024], BF16), name)
        Yk, kd, vtok, qkT, qgT, dec, egcR = [b16(n) for n in ("Yk", "kd", "vtok", "qkT", "qgT", "dec", "egcR")]
        decs = T(sb("decs", [128, 1024], F32), "decs")

        def pair(name, dt):
            return [T(sb(f"{name}_{i}", [128, 512], dt), f"{name}_{i}") for i in range(2)]
        CDT = BF16
        TW = pair("TW", CDT)
        P0 = pair("P0", CDT)
        Am = pair("Am", CDT)
        Xm = pair("Xm", CDT)
        MT = Am
        identR = T(sb("identR", [128, 128], CDT), "identR")
        P.op("dve", lambda e: e.tensor_copy(out=identR.ap, in_=ident32), reads=[cst], writes=[identR])
        nZw = pair("nZw", BF16)
        vnew = pair("vnew", BF16)
        gatesT = T(sb("gates", [128, 12 * 32], F32), "gates")

        class NS:
            pass
        BS = [NS(), NS()]
        BS[0].Yk, BS[0].kd, BS[0].vtok, BS[0].qkT, BS[0].qgT = Yk, kd, vtok, qkT, qgT
        BS[0].TW, BS[0].P0, BS[0].Am, BS[0].Xm, BS[0].nZw, BS[0].vnew = TW, P0, Am, Xm, nZw, vnew
        BS[1].Yk = TV(pre_tmp[0], pre_tmp[0].ap.bitcast(BF16)[:, 0:1024])
        BS[1].kd = TV(pre_tmp[1], pre_tmp[1].ap.bitcast(BF16)[:, 0:1024])
        BS[1].vtok = TV(pre_tmp[2], pre_tmp[2].ap.bitcast(BF16)[:, 0:1024])
        BS[1].qkT = b16("qkT1")
        BS[1].qgT = b16("qgT1")
        BS[1].TW = pair("TW1", CDT)
        BS[1].P0 = pair("P01", CDT)
        BS[1].Am = pair("Am1", CDT)
        BS[1].Xm = pair("Xm1", CDT)
        BS[1].nZw = [TV(glu_tmp[i], glu_tmp[i].ap[:, 0:512]) for i in range(2)]
        BS[1].vnew = [TV(glu_tmp[2 + i], glu_tmp[2 + i].ap[:, 0:512]) for i in range(2)]
        src = wd["gdn_w_in"][0, :, 4096:4112].rearrange("(kc p) m -> p kc m", p=128)
        P.op("sp", lambda e: e.dma_start(out=wab32.ap, in_=src), writes=[wab32], dma="wab")
        P.op("dve", lambda e: e.tensor_copy(out=wab.ap, in_=wab32.ap), reads=[wab32], writes=[wab])
        P.op("dve", lambda e: e.memset(onesE.ap, 1.0 / 128), writes=[onesE])
        P.op("act", lambda e: e.activation(negA4.ap, pv.ap[:, PV["alog4"]:PV["alog4"] + 32], AF.Exp),
             reads=[pv], writes=[negA4])
        P.op("dve", lambda e: e.tensor_scalar(out=negA4.ap, in0=negA4.ap, scalar1=-1.0, scalar2=None, op0=ALU.mult),
             reads=[negA4], writes=[negA4])

    def v3(ap, n):
        return ap.rearrange("p (h i) -> p h i", h=n)

    def q4(t, h4):
        return t.ap[:, h4 * 128:(h4 + 1) * 128]

    def q4r(t, h4):
        return t.ap[:, h4 * 128:(h4 + 1) * 128]

    def f32(ap):
        return ap.bitcast(F32) if CHAIN_R[0] else ap

    def hs(t, h):
        return t.ap[:, h * 128:(h + 1) * 128]

    def gdn(h, hn, first):
        if first:
            for i in range(2):
                P.op("pool", lambda e, i=i: e.memset(S32[i].ap, 0.0), writes=[S32[i]])
                P.op("pool", lambda e, i=i: e.memset(Sbf[i].ap, 0.0), writes=[Sbf[i]])
        one_ap = epsT.ap[:, 1:2]
        gs = gatesT

        def G(i):
            return gs.ap[:, i * 32:(i + 1) * 32]

        def G3(i):
            return G(i).rearrange("p (b x) -> p b x", x=8)

        def gop(eng, fn, extra_r=(), extra_w=()):
            P.op(eng, fn, reads=[gs, *extra_r], writes=[gs, *extra_w])
        pab = next_ps()
        for b in range(4):
            for kc in range(8):
                mm(pab, pab.ap[:, b * 16:(b + 1) * 16], hn[kc], hn[kc].ap[:, b * 128:(b + 1) * 128],
                   wab, wab.ap[:, kc, :], kc == 0, kc == 7)
        pab3 = pab.ap[:, 0:64].rearrange("p (b x) -> p b x", x=16)
        dtb3 = pv.ap[:, PV["dtb4"]:PV["dtb4"] + 32].rearrange("p (b x) -> p b x", x=8)
        gop("dve", lambda e, pab3=pab3: e.tensor_tensor(out=G3(0), in0=pab3[:, :, 0:8], in1=dtb3, op=ALU.add),
            extra_r=[pab, pv])
        gop("act", lambda e: e.activation(G(1), G(0), AF.Exp))
        gop("act", lambda e: e.activation(G(1), G(1), AF.Ln, bias=one_ap), extra_r=[epsT])
        gop("dve", lambda e: e.tensor_tensor(out=G(2), in0=G(1), in1=negA4.ap, op=ALU.mult), extra_r=[negA4])
        gop("act", lambda e, pab3=pab3: e.activation(G3(3), pab3[:, :, 8:16], AF.Exp, scale=-1.0), extra_r=[pab])
        gop("dve", lambda e: e.tensor_scalar(out=G(3), in0=G(3), scalar1=1.0, scalar2=None, op0=ALU.add))
        gop("dve", lambda e: e.reciprocal(out=G(4), in_=G(3)))
        gop("dve", lambda e: e.tensor_scalar(out=G(5), in0=G(4), scalar1=-1.0, scalar2=None, op0=ALU.mult))
        pcs = next_ps()
        mm(pcs, pcs.ap[:, 0:32], cst, triu32, gs, G(2), True, True)
        mm(pcs, pcs.ap[:, 32:64], cst, ones32, gs, G(2), True, True)
        gop("act", lambda e, pcs=pcs: e.activation(G(6), pcs.ap[:, 0:32], AF.Copy), extra_r=[pcs])
        gop("act", lambda e, pcs=pcs: e.activation(G(7), pcs.ap[:, 0:32], AF.Exp), extra_r=[pcs])
        gop("dve", lambda e: e.tensor_scalar(out=G(10), in0=G(6), scalar1=-1.0, scalar2=None, op0=ALU.mult))
        gop("dve", lambda e, pcs=pcs: e.tensor_tensor(out=G(8), in0=pcs.ap[:, 32:64], in1=G(6), op=ALU.subtract),
            extra_r=[pcs])
        gop("act", lambda e: e.activation(G(8), G(8), AF.Exp))
        gop("act", lambda e, pcs=pcs: e.activation(G(9), pcs.ap[:, 32:64], AF.Exp), extra_r=[pcs])

        qT = [None] * 8
        kT = [None] * 8
        vT = [None] * 8
        sz = [None] * 8
        grp = {}

        def S1(g):
            w_t, w = next_slab()
            st = grp[g] = {"a": [None] * 4}
            pgrp = [next_ps() for _ in range(4)]
            for kc in range(8):
                for j in range(4):
                    mm(pgrp[j], pgrp[j].ap, w_t, w[:, kc, j * 128:(j + 1) * 128], hn[kc], hn[kc].ap, kc == 0, kc == 7)
            for j in range(4):
                cc = g * 4 + j
                ps = pgrp[j]
                if cc >= 24:
                    o = p16.get()
                    act(o, o.ap, ps, ps.ap, AF.Silu)
                    sz[cc - 24] = o
                    continue
                pre = pre_tmp[cc % 4]
                act(pre, pre.ap[:, 3:], ps, ps.ap, AF.Copy)
                if first:
                    P.op("pool", (lambda pre: lambda e: e.memset(pre.ap[:, 0:3], 0.0))(pre), writes=[pre])
                else:
                    P.op("pool", (lambda pre, cc: lambda e: e.tensor_copy(
                        out=pre.ap[:, 0:3], in_=halo1.ap[:, cc * 3:cc * 3 + 3]))(pre, cc),
                        reads=[halo1], writes=[pre])
                P.op("pool", (lambda pre, cc: lambda e: e.tensor_copy(
                    out=halo1.ap[:, cc * 3:cc * 3 + 3], in_=pre.ap[:, NT:NT + 3]))(pre, cc),
                    reads=[pre], writes=[halo1])
                a = p32.get()
                P.op("dve", (lambda a, pre, cc: lambda e: e.tensor_scalar(
                    out=a.ap, in0=pre.ap[:, 0:NT], scalar1=pvc("gcw", cc), scalar2=None, op0=ALU.mult))(a, pre, cc),
                    reads=[pre, pv], writes=[a])
                for k in range(1, 4):
                    P.op("dve", (lambda a, pre, cc, k: lambda e: e.scalar_tensor_tensor(
                        out=a.ap, in0=pre.ap[:, k:k + NT], scalar=pvc("gcw", k * 24 + cc), in1=a.ap,
                        op0=ALU.mult, op1=ALU.add))(a, pre, cc, k), reads=[pre, pv, a], writes=[a])
                st["a"][j] = a

        def S2(g):
            if g >= 6:
                return
            for j in range(4):
                cc = g * 4 + j
                a = grp[g]["a"][j]
                if cc >= 16:
                    o = p16.get()
                    act(o, o.ap, a, a.ap, AF.Silu)
                    vT[cc - 16] = o
                    p32.put(a)
                else:
                    act(a, a.ap, a, a.ap, AF.Silu)

        def S3(g):
            if g >= 4:
                return
            pss = []
            sqs = []
            for j in range(4):
                a = grp[g]["a"][j]
                sq = p16.get()
                act(sq, sq.ap, a, a.ap, AF.Square)
                ps2 = next_ps()
                mm(ps2, ps2.ap, onesH, onesH.ap, sq, sq.ap, True, True)
                pss.append(ps2)
                sqs.append(sq)
            for sq in sqs:
                p16.put(sq)
            rs = []
            for j in range(4):
                t = p32.get()
                act(t, t.ap, pss[j], pss[j].ap, AF.Ln, bias=eps_ap(EPS), extra_reads=[epsT])
                rs.append(t)
            for j in range(4):
                act(rs[j], rs[j].ap, rs[j], rs[j].ap, AF.Exp, scale=-0.5)
            for j in range(4):
                cc = g * 4 + j
                a = grp[g]["a"][j]
                r = rs[j]
                o = p16.get()
                if cc < 8:
                    P.op("dve", (lambda o, a, r: lambda e: e.scalar_tensor_tensor(
                        out=o.ap, in0=a.ap, scalar=float(128 ** -0.5), in1=r.ap, op0=ALU.mult, op1=ALU.mult))(o, a, r),
                        reads=[a, r], writes=[o])
                    qT[cc] = o
                else:
                    P.op("dve", (lambda o, a, r: lambda e: e.tensor_tensor(
                        out=o.ap, in0=a.ap, in1=r.ap, op=ALU.mult))(o, a, r), reads=[a, r], writes=[o])
                    kT[cc - 8] = o
                p32.put(a)
                p32.put(r)

        for step in range(9):
            if step < 8:
                S1(step)
            if step >= 1:
                S2(step - 1)
                S3(step - 1)

        oT = [p32.get() for _ in range(8)]
        triu_b8 = triu32.unsqueeze(1).broadcast_to([128, 8, 128])
        stri_b8 = cst.ap[:, 384:512].unsqueeze(1).broadcast_to([128, 8, 128])
        ident_b4 = ident32.unsqueeze(1).broadcast_to([128, 4, 128])
        def lmask(l):
            return cst.ap[:, 640 + l * 128:640 + (l + 1) * 128].unsqueeze(1).broadcast_to([128, 4, 128])

        def prep(b, B):
            tb = slice(b * 128, (b + 1) * 128)

            def gb(i):
                return gs.ap[:, i * 32 + b * 8:i * 32 + b * 8 + 8]

            def g1(i, hh):
                return gs.ap[:, i * 32 + b * 8 + hh:i * 32 + b * 8 + hh + 1]
            P.op("dve", lambda e: e.tensor_copy(
                out=v3(GE.ap, 8), in_=gb(2).unsqueeze(2).broadcast_to([128, 8, 128])),
                reads=[gs], writes=[GE])
            pR = [next_ps(), next_ps()]
            pRm = [next_ps(), next_ps()]
            for hh in range(8):
                mm(pR[hh // 4], q4(pR[hh // 4], hh % 4), GE, hs(GE, hh), cst, triu32, True, True)
                mm(pRm[hh // 4], q4(pRm[hh // 4], hh % 4), GE, hs(GE, hh), cst, triu32, True, False)
                mm(pRm[hh // 4], q4(pRm[hh // 4], hh % 4), cbf, identbf, cbf, masknegbf, False, True)
            ptr = next_ps()
            ptr_bf = ptr.ap.bitcast(BF16)
            for hh in range(8):
                P.op("pe", lambda e, hh=hh: e.transpose(
                    out=ptr_bf[:, hh * 128:(hh + 1) * 128], in_=kT[hh].ap[:, tb], identity=identbf),
                    reads=[kT[hh], cbf], writes=[ptr])
            ptv = next_ps()
            ptv_bf = ptv.ap.bitcast(BF16)
            for hh in range(8):
                P.op("pe", lambda e, hh=hh: e.transpose(
                    out=ptv_bf[:, hh * 128:(hh + 1) * 128], in_=vT[hh].ap[:, tb], identity=identbf),
                    reads=[vT[hh], cbf], writes=[ptv])
            for hg in range(2):
                act(egcR, egcR.ap[:, hg * 512:(hg + 1) * 512], pR[hg], pR[hg].ap, AF.Exp)
            for hh in range(8):
                act(dec, hs(dec, hh), pRm[hh // 4], q4(pRm[hh // 4], hh % 4), AF.Exp,
                    bias=g1(10, hh), extra_reads=[gs])
            P.op("dve", lambda e: e.tensor_tensor(
                out=v3(decs.ap, 8), in0=v3(dec.ap, 8), in1=stri_b8, op=ALU.mult),
                reads=[dec, cst], writes=[decs])
            P.op("dve", lambda e: e.tensor_tensor(
                out=v3(decs.ap, 8), in0=v3(decs.ap, 8), in1=gb(5).unsqueeze(2).broadcast_to([128, 8, 128]), op=ALU.mult),
                reads=[decs, gs], writes=[decs])
            P.op("dve", lambda e: e.tensor_tensor(
                out=v3(B.Yk.ap, 8), in0=v3(ptr_bf, 8), in1=gb(7).unsqueeze(2).broadcast_to([128, 8, 128]), op=ALU.mult),
                reads=[ptr, gs], writes=[B.Yk])
            P.op("dve", lambda e: e.tensor_tensor(
                out=v3(B.kd.ap, 8), in0=v3(ptr_bf, 8), in1=gb(8).unsqueeze(2).broadcast_to([128, 8, 128]), op=ALU.mult),
                reads=[ptr, gs], writes=[B.kd])
            act(B.vtok, B.vtok.ap, ptv, ptv_bf, AF.Copy)
            for hg in range(2):
                pkk = next_ps()
                pqk = next_ps()
                for h4 in range(4):
                    hh = hg * 4 + h4
                    mm(pkk, q4(pkk, h4), kT[hh], kT[hh].ap[:, tb], kT[hh], kT[hh].ap[:, tb], True, True)
                    mm(pqk, q4(pqk, h4), kT[hh], kT[hh].ap[:, tb], qT[hh], qT[hh].ap[:, tb], True, True)
                P.op("dve", lambda e, hg=hg, pkk=pkk: e.tensor_tensor(
                    out=B.TW[hg].ap, in0=pkk.ap, in1=decs.ap[:, hg * 512:(hg + 1) * 512], op=ALU.mult),
                    reads=[pkk, decs], writes=[B.TW[hg]])
                P.op("dve", lambda e, hg=hg, pqk=pqk: e.tensor_tensor(
                    out=B.qkT.ap[:, hg * 512:(hg + 1) * 512], in0=pqk.ap, in1=dec.ap[:, hg * 512:(hg + 1) * 512], op=ALU.mult),
                    reads=[pqk, dec], writes=[B.qkT])
            for hh in range(8):
                P.op("pool", lambda e, hh=hh: e.tensor_tensor(
                    out=hs(B.qgT, hh), in0=qT[hh].ap[:, tb], in1=hs(egcR, hh), op=ALU.mult),
                    reads=[qT[hh], egcR], writes=[B.qgT])
            for hg in range(2):
                pp = next_ps()
                pp_b = pp.ap.bitcast(BF16)
                for h4 in range(4):
                    P.op("pe", lambda e, h4=h4, hg=hg, pp_b=pp_b: e.transpose(
                        out=pp_b[:, h4 * 128:(h4 + 1) * 128], in_=q4(B.TW[hg], h4), identity=identbf),
                        reads=[B.TW[hg], cbf], writes=[pp])
                act(B.P0[hg], B.P0[hg].ap, pp, pp_b[:, 0:512], AF.Copy)
                P.op("dve", lambda e, hg=hg: e.tensor_tensor(
                    out=v3(B.Am[hg].ap, 4), in0=v3(B.TW[hg].ap, 4), in1=lmask(0), op=ALU.mult),
                    reads=[B.TW[hg], cst], writes=[B.Am[hg]])
                P.op("dve", lambda e, hg=hg: e.tensor_tensor(
                    out=v3(B.Am[hg].ap, 4), in0=v3(B.Am[hg].ap, 4), in1=ident_b4, op=ALU.add),
                    reads=[B.Am[hg], cst], writes=[B.Am[hg]])
            for hg in range(2):
                px = next_ps()
                px_b = px.ap.bitcast(BF16)
                for h4 in range(4):
                    P.op("pe", lambda e, h4=h4, hg=hg, px_b=px_b: e.transpose(
                        out=px_b[:, h4 * 128:(h4 + 1) * 128], in_=q4(B.Am[hg], h4), identity=identbf),
                        reads=[B.Am[hg], cbf], writes=[px])
                act(B.Xm[hg], B.Xm[hg].ap, px, px_b[:, 0:512], AF.Copy)

        def level(B, l):
            for hg in range(2):
                pW = next_ps()
                for h4 in range(4):
                    mm(pW, q4(pW, h4), B.P0[hg], q4(B.P0[hg], h4), B.Am[hg], q4(B.Am[hg], h4), True, True)
                P.op("dve", lambda e, hg=hg, pW=pW: e.tensor_tensor(
                    out=v3(B.TW[hg].ap, 4), in0=v3(pW.ap, 4), in1=lmask(l), op=ALU.mult),
                    reads=[pW, cst], writes=[B.TW[hg]])
            for hg in range(2):
                pA = next_ps()
                for h4 in range(4):
                    mm(pA, q4(pA, h4), B.Xm[hg], q4(B.Xm[hg], h4), B.TW[hg], q4(B.TW[hg], h4), True, False)
                    mm(pA, q4(pA, h4), cbf, identbf, B.Am[hg], q4(B.Am[hg], h4), False, True)
                if l < 6:
                    pX = next_ps()
                    for h4 in range(4):
                        mm(pX, q4(pX, h4), B.TW[hg], q4(B.TW[hg], h4), B.Xm[hg], q4(B.Xm[hg], h4), True, False)
                        mm(pX, q4(pX, h4), cbf, identbf, B.Xm[hg], q4(B.Xm[hg], h4), False, True)
                act(B.Am[hg], B.Am[hg].ap, pA, pA.ap, AF.Copy)
                if l < 6:
                    act(B.Xm[hg], B.Xm[hg].ap, pX, pX.ap, AF.Copy)

        def post(B):
            for hg in range(2):
                pz = next_ps()
                for h4 in range(4):
                    hh = hg * 4 + h4
                    mm(pz, q4(pz, h4), B.Yk, hs(B.Yk, hh), B.Am[hg], q4(B.Am[hg], h4), True, True)
                act(B.nZw[hg], B.nZw[hg].ap, pz, pz.ap, AF.Copy, scale=-1.0)

        def scan(b, B):
            tb = slice(b * 128, (b + 1) * 128)

            def gb(i):
                return gs.ap[:, i * 32 + b * 8:i * 32 + b * 8 + 8]

            def g1(i, hh):
                return gs.ap[:, i * 32 + b * 8 + hh:i * 32 + b * 8 + hh + 1]
            for hg in range(2):
                pvn = next_ps()
                for h4 in range(4):
                    hh = hg * 4 + h4
                    mm(pvn, q4(pvn, h4), B.Am[hg], q4(B.Am[hg], h4), B.vtok, hs(B.vtok, hh), True, False)
                    mm(pvn, q4(pvn, h4), B.nZw[hg], q4(B.nZw[hg], h4), Sbf[hg], q4(Sbf[hg], h4), False, True)
                P.op("dve", lambda e, hg=hg, pvn=pvn: e.tensor_tensor(
                    out=v3(B.vnew[hg].ap, 4), in0=v3(pvn.ap, 4),
                    in1=gb(4)[:, hg * 4:hg * 4 + 4].unsqueeze(2).broadcast_to([128, 4, 128]), op=ALU.mult),
                    reads=[pvn, gs], writes=[B.vnew[hg]])
            for hg in range(2):
                po = next_ps()
                for h4 in range(4):
                    hh = hg * 4 + h4
                    mm(po, q4(po, h4), Sbf[hg], q4(Sbf[hg], h4), B.qgT, hs(B.qgT, hh), True, False)
                    mm(po, q4(po, h4), B.vnew[hg], q4(B.vnew[hg], h4), B.qkT, hs(B.qkT, hh), False, True)
                for h4 in range(4):
                    hh = hg * 4 + h4
                    act(oT[hh], oT[hh].ap[:, tb], po, q4(po, h4), AF.Copy)
            for hg in range(2):
                pS = next_ps()
                for h4 in range(4):
                    hh = hg * 4 + h4
                    mm(pS, q4(pS, h4), B.kd, hs(B.kd, hh), B.vnew[hg], q4(B.vnew[hg], h4), True, True)
                for h4 in range(4):
                    hh = hg * 4 + h4
                    P.op("dve", lambda e, hh=hh, h4=h4, hg=hg, pS=pS: e.scalar_tensor_tensor(
                        out=q4(S32[hg], h4), in0=q4(S32[hg], h4), scalar=g1(9, hh), in1=q4(pS, h4),
                        op0=ALU.mult, op1=ALU.add), reads=[S32[hg], gs, pS], writes=[S32[hg]])
                act(Sbf[hg], Sbf[hg].ap, S32[hg], S32[hg].ap, AF.Copy)

        for pr in range(2):
            b0, b1 = 2 * pr, 2 * pr + 1
            prep(b0, BS[0])
            prep(b1, BS[1])
            for l in range(1, 7):
                level(BS[0], l)
                level(BS[1], l)
            post(BS[0])
            post(BS[1])
            scan(b0, BS[0])
            scan(b1, BS[1])
        for t in hn + qT + kT + vT:
            p16.put(t)
        onT = []
        for hh in range(8):
            sq = p16.get()
            act(sq, sq.ap, oT[hh], oT[hh].ap, AF.Square)
            ps = next_ps()
            mm(ps, ps.ap, onesE, onesE.ap, sq, sq.ap, True, True)
            p16.put(sq)
            r = rstd_from(ps)
            P.op("dve", lambda e, hh=hh, r=r: e.scalar_tensor_tensor(
                out=oT[hh].ap, in0=oT[hh].ap, scalar=pvc("gng"), in1=r.ap, op0=ALU.mult, op1=ALU.mult),
                reads=[oT[hh], pv, r], writes=[oT[hh]])
            o = p16.get()
            P.op("dve", lambda e, hh=hh, o=o: e.tensor_tensor(
                out=o.ap, in0=oT[hh].ap, in1=sz[hh].ap, op=ALU.mult), reads=[oT[hh], sz[hh]], writes=[o])
            onT.append(o)
            p32.put(r)
            p32.put(oT[hh])
            p16.put(sz[hh])
        for half in range(2):
            w_t, w = next_slab()
            pgrp = [next_ps() for _ in range(4)]
            for kc in range(8):
                for j in range(4):
                    mm(pgrp[j], pgrp[j].ap, w_t, w[:, kc, j * 128:(j + 1) * 128], onT[kc], onT[kc].ap, kc == 0, kc == 7)
            for j in range(4):
                mc = half * 4 + j
                ps = pgrp[j]
                P.op("dve", lambda e, ps=ps, mc=mc: e.tensor_tensor(
                    out=h[mc].ap, in0=h[mc].ap, in1=ps.ap, op=ALU.add), reads=[ps, h[mc]], writes=[h[mc]])
        for t in onT:
            p16.put(t)

    def load_x(t):
        hb = hbuf[t % 2]
        for c in range(8):
            P.op("sp", (lambda c, hb, t: lambda e: e.dma_start(
                out=hb[c].ap, in_=xT[c * 128:(c + 1) * 128, t * NT:(t + 1) * NT]))(c, hb, t),
                writes=[hb[c]], dma=("x", t % 2, c))

    store_ops = []

    def tile_body(t):
        h = hbuf[t % 2]
        first = (t % tiles_per_seq) == 0
        if t + 1 < ntiles:
            load_x(t + 1)
        for li in range(depth):
            hn = rmsnorm(h, f"nmg{li}")
            if li % 2 == 0:
                conformer(h, hn, first)
            else:
                gdn(h, hn, first)
            hn = rmsnorm(h, f"nfg{li}")
            mlp(h, hn)
        ps = next_ps()
        for c in range(8):
            sq = p16.get()
            act(sq, sq.ap, h[c], h[c].ap, AF.Square)
            mm(ps, ps.ap, onesD, onesD.ap, sq, sq.ap, c == 0, c == 7)
            p16.put(sq)
        r = rstd_from(ps)
        for c in range(8):
            o = p32.get()
            P.op("dve", (lambda o, c: lambda e: e.scalar_tensor_tensor(
                out=o.ap, in0=h[c].ap, scalar=pvc("fng", c), in1=r.ap, op0=ALU.mult, op1=ALU.mult))(o, c),
                reads=[h[c], pv, r], writes=[o])
            i = P.op("pool", (lambda o, c: lambda e: e.dma_start(
                out=outT[c * 128:(c + 1) * 128, t * NT:(t + 1) * NT], in_=o.ap))(o, c),
                reads=[o], dma=("st", o.name))
            store_ops.append(i)
            p32.put(o)
        p32.put(r)

    load_x(0)
    for t in range(ntiles):
        tile_body(t)
    fin = P.op("pool", lambda e: e.nop(), reads=[], writes=[])
    P.ops[fin][2].update(store_ops)
    P.emit(nc)
    return nc, P


_CACHE = {}


def _run(inputs, nseq, S, depth, ncores):
    key = (nseq, S, depth)
    if key not in _CACHE:
        _CACHE[key] = build_nc(nseq, S, depth)
    nc, P = _CACHE[key]
    x = np.asarray(inputs["x"], np.float32)
    B = x.shape[0]
    assert B == nseq * ncores
    pvec = pack_pvec(inputs)
    cst = make_consts()
    in_maps = []
    for i in range(ncores):
        xs = x[i * nseq:(i + 1) * nseq].reshape(nseq * S, D)
        m = {"xT": np.ascontiguousarray(xs.T), "pvec": pvec, "cst": cst,
             "cv_w_pw1": np.asarray(inputs["cv_w_pw1"], np.float32),
             "cv_w_pw2": np.asarray(inputs["cv_w_pw2"], np.float32),
             "mlp_w1": np.asarray(inputs["mlp_w1"], np.float32),
             "mlp_w2": np.asarray(inputs["mlp_w2"], np.float32)}
        if depth > 1:
            m["gdn_w_in"] = np.asarray(inputs["gdn_w_in"], np.float32)
            m["gdn_w_out"] = np.asarray(inputs["gdn_w_out"], np.float32)
        in_maps.append(m)
    res = run_bass_kernel_spmd(nc, in_maps, core_ids=list(range(ncores)))
    outs = [np.asarray(r["outT"]).T.reshape(nseq, S, D) for r in res.results]
    return np.ascontiguousarray(np.concatenate(outs, axis=0).astype(np.float32))


def kernel(**inputs):
    return _run(inputs, 2, 4096, 2, 8)
```

```python
import numpy as np
import concourse.bass as bass
import concourse.mybir as mybir
from concourse.bass_utils import run_bass_kernel_spmd

F32 = mybir.dt.float32
F32R = mybir.dt.float32r
CHAIN_R = [False]
BF16 = mybir.dt.bfloat16
AF = mybir.ActivationFunctionType
ALU = mybir.AluOpType

D = 1024
NC8 = 8
NT = 512
EPS = 1e-6
KW = 31
H = 8
QKV = 3072
DEBUG_STOP = [99]
GIN = 4112

COMPUTE = ("pe", "act", "dve", "pool")


class T:
    __slots__ = ("ap", "w", "r", "name")

    def __init__(self, ap, name=""):
        self.ap = ap
        self.w = None
        self.r = {}
        self.name = name


class TV:
    __slots__ = ("ap", "base", "name")

    def __init__(self, base, ap):
        self.base = base
        self.ap = ap
        self.name = base.name + "_v"

    @property
    def w(self):
        return self.base.w

    @w.setter
    def w(self, v):
        self.base.w = v

    @property
    def r(self):
        return self.base.r

    @r.setter
    def r(self, v):
        self.base.r = v


class Prog:
    def __init__(self):
        self.ops = []
        self.last_dma = {}

    def op(self, eng, fn, reads=(), writes=(), dma=None):
        i = len(self.ops)
        deps = set()
        for t in reads:
            if t.w is not None:
                deps.add(t.w)
        for t in writes:
            if t.w is not None:
                deps.add(t.w)
            deps.update(t.r.values())
        if dma is not None and dma in self.last_dma:
            deps.add(self.last_dma[dma])
        if dma is not None:
            self.last_dma[dma] = i
        for t in writes:
            t.w = i
            t.r = {}
        for t in reads:
            if dma is not None:
                t.r[("dma", i)] = i
            else:
                t.r[eng] = i
        self.ops.append((eng, fn, deps, dma))
        return i

    def emit(self, nc, sem_limit=30000):
        ops = self.ops
        n = len(ops)
        red = []
        needed = [False] * n
        for i, (eng, fn, deps, dma) in enumerate(ops):
            best = {}
            dl = []
            for d in deps:
                de, _, _, ddma = ops[d]
                if ddma is not None:
                    dl.append(d)
                else:
                    if de == eng and eng == "pe":
                        continue
                    if de not in best or best[de] < d:
                        best[de] = d
            dl.extend(best.values())
            red.append(dl)
            for d in dl:
                needed[d] = True
        ordinal = [0] * n
        cnt = {}
        dcnt = {}
        for i, (eng, fn, deps, dma) in enumerate(ops):
            if dma is not None:
                dcnt[dma] = dcnt.get(dma, 0) + 1
                ordinal[i] = 16 * dcnt[dma]
            elif needed[i]:
                cnt[eng] = cnt.get(eng, 0) + 1
                ordinal[i] = cnt[eng]
        self.stats = dict(cnt)
        sems = {}

        def get_sem(key):
            if key not in sems:
                sems[key] = nc.alloc_semaphore(name="s_" + "_".join(str(k) for k in key))
            return sems[key]

        def signal_of(i):
            eng, _, _, dma = ops[i]
            if dma is not None:
                return ("d", dma), ordinal[i], ordinal[i]
            o = ordinal[i]
            ep = (o - 1) // sem_limit
            return ("e", eng, ep), o - ep * sem_limit, o

        for i in range(n):
            if ops[i][3] is not None or needed[i]:
                get_sem(signal_of(i)[0])

        by_eng = {}
        for i, o in enumerate(ops):
            by_eng.setdefault(o[0], []).append(i)

        def run_engine(ename, e):
            known = {}
            nwait = 0
            for i in by_eng.get(ename, []):
                eng, fn, deps, dma = ops[i]
                for d in red[i]:
                    key, local, glob = signal_of(d)
                    kk = key if key[0] == "d" else ("e", key[1])
                    if known.get(kk, 0) >= glob:
                        continue
                    known[kk] = glob
                    e.wait_ge(get_sem(key), local)
                    nwait += 1
                ins = fn(e)
                if dma is not None:
                    ins.then_inc(get_sem(("d", dma)), 16)
                elif needed[i]:
                    ins.then_inc(get_sem(signal_of(i)[0]), 1)
            self.stats["wait_" + ename] = nwait

        with nc.Block() as block:
            @block.tensor
            def _(e):
                run_engine("pe", e)

            @block.scalar
            def _(e):
                run_engine("act", e)

            @block.vector
            def _(e):
                run_engine("dve", e)

            @block.gpsimd
            def _(e):
                run_engine("pool", e)

            @block.sync
            def _(e):
                run_engine("sp", e)


class UnitPool:
    def __init__(self, nc, n, dtype, name, width=NT):
        self.free = []
        for i in range(n):
            ap = nc.alloc_sbuf_tensor(f"{name}{i}", [128, width], dtype).ap()
            self.free.append(T(ap, f"{name}{i}"))
        self.n = n

    def get(self):
        if not self.free:
            raise RuntimeError("unit pool exhausted")
        return self.free.pop(0)

    def put(self, t):
        self.free.append(t)


def _pv_layout():
    cols = {}
    off = 0

    def add(name, n):
        nonlocal off
        cols[name] = off
        off += n
    for i in range(2):
        add(f"nmg{i}", 8)
        add(f"nfg{i}", 8)
    add("fng", 8)
    add("bpw1", 16)
    add("wdw", KW * 8)
    add("bdw", 8)
    add("lng", 8)
    add("lnb", 8)
    add("bpw2", 8)
    add("gcw", 4 * 24)
    add("gng", 1)
    add("alog", 8)
    add("dtb", 8)
    add("alog4", 32)
    add("dtb4", 32)
    return cols, off


PV, NPV = _pv_layout()


def _chunked(v):
    return np.ascontiguousarray(v.reshape(-1, 128).T)


def pack_pvec(inp):
    pv = np.zeros((128, NPV), np.float32)

    def put(name, arr):
        pv[:, PV[name]:PV[name] + arr.shape[1]] = arr
    for i in range(2):
        put(f"nmg{i}", _chunked(inp["norm_mix_g"][i]))
        put(f"nfg{i}", _chunked(inp["norm_ffn_g"][i]))
    put("fng", _chunked(inp["final_norm_g"]))
    put("bpw1", _chunked(inp["cv_b_pw1"][0]))
    wdw = inp["cv_w_dw"][0]
    put("wdw", np.concatenate([_chunked(wdw[k]) for k in range(KW)], axis=1))
    put("bdw", _chunked(inp["cv_b_dw"][0]))
    put("lng", _chunked(inp["cv_ln_g"][0]))
    put("lnb", _chunked(inp["cv_ln_b"][0]))
    put("bpw2", _chunked(inp["cv_b_pw2"][0]))
    gcw = inp["gdn_conv_w"][0]
    put("gcw", np.concatenate([_chunked(gcw[k]) for k in range(4)], axis=1))
    put("gng", inp["gdn_norm_g"][0].reshape(128, 1))
    put("alog", np.broadcast_to(inp["gdn_a_log"][0][None, :], (128, 8)))
    put("dtb", np.broadcast_to(inp["gdn_dt_bias"][0][None, :], (128, 8)))
    put("alog4", np.broadcast_to(np.tile(inp["gdn_a_log"][0], 4)[None, :], (128, 32)))
    put("dtb4", np.broadcast_to(np.tile(inp["gdn_dt_bias"][0], 4)[None, :], (128, 32)))
    return pv


def make_consts():
    idx = np.arange(128)
    ident = np.eye(128, dtype=np.float32)
    triu = (idx[:, None] <= idx[None, :]).astype(np.float32)
    ones = np.ones((128, 128), np.float32)
    stri = (idx[:, None] < idx[None, :]).astype(np.float32)
    maskneg = np.where(idx[:, None] <= idx[None, :], 0.0, -30000.0).astype(np.float32)
    lv = []
    for l in range(7):
        sz = 1 << l
        i = idx[None, :]
        j = idx[:, None]
        m = ((i // (2 * sz)) == (j // (2 * sz))) & ((i % (2 * sz)) >= sz) & ((j % (2 * sz)) < sz)
        lv.append(m.astype(np.float32))
    return np.concatenate([ident, triu, ones, stri, maskneg] + lv, axis=1)


def slab_table(depth):
    sl = []
    for li in range(depth):
        if li % 2 == 0:
            for half in range(2):
                sl.append(("cv_w_pw1", 0, 0, half * 512))
                sl.append(("cv_w_pw1", 0, 0, 1024 + half * 512))
                for j in range(4):
                    sl.append(("diag", half * 4 + j, 0, 0))
            for m0 in (0, 512):
                sl.append(("cv_w_pw2", 0, 0, m0))
        else:
            for m0 in range(0, 4096, 512):
                sl.append(("gdn_w_in", 0, 0, m0))
            for m0 in (0, 512):
                sl.append(("gdn_w_out", 0, 0, m0))
        for m0 in range(0, 4096, 512):
            sl.append(("mlp_w1", li, 0, m0))
        for mg in range(2):
            for kg in range(4):
                sl.append(("mlp_w2", li, kg * 1024, mg * 512))
    return sl


def build_nc(nseq, S, depth, nring=3):
    ntok = nseq * S
    ntiles = ntok // NT
    tiles_per_seq = S // NT
    nc = bass.Bass("TRN2", target_bir_lowering=False)
    P = Prog()

    xT = nc.dram_tensor("xT", [D, ntok], F32, kind="ExternalInput").ap()
    outT = nc.dram_tensor("outT", [D, ntok], F32, kind="ExternalOutput").ap()
    pvec_d = nc.dram_tensor("pvec", [128, NPV], F32, kind="ExternalInput").ap()
    cst_d = nc.dram_tensor("cst", [128, 1536], F32, kind="ExternalInput").ap()
    wd = {}
    wd["cv_w_pw1"] = nc.dram_tensor("cv_w_pw1", [1, D, 2 * D], F32, kind="ExternalInput").ap()
    wd["cv_w_pw2"] = nc.dram_tensor("cv_w_pw2", [1, D, D], F32, kind="ExternalInput").ap()
    wd["mlp_w1"] = nc.dram_tensor("mlp_w1", [2, D, 4 * D], F32, kind="ExternalInput").ap()
    wd["mlp_w2"] = nc.dram_tensor("mlp_w2", [2, 4 * D, D], F32, kind="ExternalInput").ap()
    if depth > 1:
        wd["gdn_w_in"] = nc.dram_tensor("gdn_w_in", [1, D, GIN], F32, kind="ExternalInput").ap()
        wd["gdn_w_out"] = nc.dram_tensor("gdn_w_out", [1, D, D], F32, kind="ExternalInput").ap()
    slabs = slab_table(depth)
    nslab = len(slabs)
    wbf = nc.dram_tensor("wbf", [nslab, 128, 4096], BF16, kind="Internal").ap()
    wdiag = nc.dram_tensor("wdiag", [8, 128, 4096], BF16, kind="Internal").ap()

    def sb(name, shape, dt):
        return nc.alloc_sbuf_tensor(name, shape, dt).ap()

    pv = T(sb("pv", [128, NPV], F32), "pv")
    cst = T(sb("cst_sb", [128, 1536], F32), "cst")
    cbf = T(sb("cbf", [128, 640], BF16), "cbf")
    onesD = T(sb("onesD", [128, 128], BF16), "onesD")
    onesH = T(sb("onesH", [128, 128], BF16), "onesH")
    onesD32 = T(sb("onesD32", [128, 128], F32), "onesD32")
    hbuf = [[T(sb(f"h{b}_{c}", [128, NT], F32), f"h{b}_{c}") for c in range(8)] for b in range(2)]
    ring = [T(sb(f"ring{i}", [128, 4096], BF16), f"ring{i}") for i in range(nring)]
    p16 = UnitPool(nc, 41, BF16, "u16_")
    p32 = UnitPool(nc, 14, F32, "u32_")
    glu_tmp = [T(sb(f"glu{i}", [128, NT + KW - 1], BF16), f"glu{i}") for i in range(8)]
    halo0 = [T(sb(f"halo0_{c}", [128, KW - 1], BF16), f"halo0_{c}") for c in range(8)]
    psum = [T(nc.alloc_psum_tensor(f"ps{i}", [128, NT], F32).ap(), f"ps{i}") for i in range(8)]
    ps_i = [0]

    def next_ps():
        t = psum[ps_i[0] % 8]
        ps_i[0] += 1
        return t

    def pvc(name, j=0):
        o = PV[name] + j
        return pv.ap[:, o:o + 1]

    ident32 = cst.ap[:, 0:128]
    triu32 = cst.ap[:, 128:256]
    ones32 = cst.ap[:, 256:384]
    identbf = cbf.ap[:, 0:128]
    stribf = cbf.ap[:, 384:512]
    maskneg32 = cst.ap[:, 512:640]
    masknegbf = cbf.ap[:, 512:640]

    P.op("sp", lambda e: e.dma_start(out=pv.ap, in_=pvec_d), writes=[pv], dma="pv")
    P.op("sp", lambda e: e.dma_start(out=cst.ap, in_=cst_d), writes=[cst], dma="cst")
    P.op("dve", lambda e: e.tensor_copy(out=cbf.ap, in_=cst.ap[:, 0:640]), reads=[cst], writes=[cbf])
    P.op("dve", lambda e: e.memset(onesD.ap, 1.0 / D), writes=[onesD])
    P.op("dve", lambda e: e.memset(onesH.ap, 1.0), writes=[onesH])
    P.op("dve", lambda e: e.memset(onesD32.ap, 1.0 / D), writes=[onesD32])
    for c in range(8):
        P.op("dve", (lambda c: lambda e: e.memset(halo0[c].ap, 0.0))(c), writes=[halo0[c]])

    cast_op = {}
    diag_built = [False]

    def build_diags():
        for c in range(8):
            stage = ring[c % nring]
            for k in range(KW):
                P.op("dve", (lambda stage, c, k: lambda e: e.tensor_scalar(
                    out=stage.ap[:, k * 128:(k + 1) * 128], in0=ident32, scalar1=pvc("wdw", k * 8 + c), scalar2=None,
                    op0=ALU.mult))(stage, c, k), reads=[cst, pv], writes=[stage])
            P.op("dve", (lambda stage: lambda e: e.memset(stage.ap[:, KW * 128:], 0.0))(stage), writes=[stage])
            i = P.op("sp", (lambda stage, c: lambda e: e.dma_start(out=wdiag[c], in_=stage.ap))(stage, c),
                     reads=[stage], dma=("dg", c % nring))
            cast_op[("diag", c)] = i
    build_diags()
    for s_, (wn, li, k0, m0) in enumerate(slabs):
        if wn == "diag":
            continue
        src = wd[wn][li, k0:k0 + 1024, m0:m0 + 512].rearrange("(kc p) m -> p kc m", p=128)
        dst = wbf[s_].rearrange("p (kc m) -> p kc m", kc=8)
        cast_op[s_] = P.op("pool", (lambda src, dst: lambda e: e.dma_start(out=dst, in_=src))(src, dst),
                           writes=[], dma=("wc", s_ % 4))

    slab_ctr = [0]

    def next_slab(tile_first_use_guard=None):
        g = slab_ctr[0]
        slab_ctr[0] += 1
        s = g % nslab
        slot = ring[g % nring]
        srcap = wdiag[slabs[s][1]] if slabs[s][0] == "diag" else wbf[s]
        i = P.op("sp", (lambda srcap, slot: lambda e: e.dma_start(out=slot.ap, in_=srcap))(srcap, slot),
                 writes=[slot], dma=("ring", g % nring))
        if g < nslab:
            key = ("diag", slabs[s][1]) if slabs[s][0] == "diag" else s
            P.ops[i][2].add(cast_op[key])
        return slot, slot.ap.rearrange("p (kc m) -> p kc m", kc=8)


    def mm(ps, ps_ap, lt, lt_ap, rt, rt_ap, start, stop):
        P.op("pe", lambda e: e.matmul(ps_ap, lt_ap, rt_ap, start=start, stop=stop),
             reads=[lt, rt], writes=[ps])

    def act(out_t, out_ap, in_t, in_ap, func, bias=None, scale=None, extra_reads=()):
        kw = {}
        if bias is not None:
            kw["bias"] = bias
        if scale is not None:
            kw["scale"] = scale
        P.op("act", lambda e: e.activation(out_ap, in_ap, func, **kw),
             reads=[in_t, *extra_reads], writes=[out_t])

    def rstd_from(ps_stat, eps=EPS):
        t = p32.get()
        act(t, t.ap, ps_stat, ps_stat.ap, AF.Ln, bias=eps_ap(eps), extra_reads=[epsT])
        act(t, t.ap, t, t.ap, AF.Exp, scale=-0.5)
        return t

    epsT = T(sb("epsc", [128, 2], F32), "epsc")
    P.op("dve", lambda e: e.memset(epsT.ap[:, 0:1], EPS), writes=[epsT])
    P.op("dve", lambda e: e.memset(epsT.ap[:, 1:2], 1.0), writes=[epsT])

    def eps_ap(eps):
        return epsT.ap[:, 0:1]

    def rmsnorm(h, gname):
        ps = next_ps()
        for c in range(8):
            sq = p16.get()
            act(sq, sq.ap, h[c], h[c].ap, AF.Square)
            mm(ps, ps.ap, onesD, onesD.ap, sq, sq.ap, c == 0, c == 7)
            p16.put(sq)
        r = rstd_from(ps)
        out = []
        for c in range(8):
            o = p16.get()
            P.op("dve", (lambda o, c: lambda e: e.scalar_tensor_tensor(
                out=o.ap, in0=h[c].ap, scalar=pvc(gname, c), in1=r.ap, op0=ALU.mult, op1=ALU.mult))(o, c),
                reads=[h[c], pv, r], writes=[o])
            out.append(o)
        p32.put(r)
        return out

    def conformer(h, hn, first_of_seq):
        acc = [None] * 8
        for half in range(2):
            wa_t, wa = next_slab()
            wg_t, wg = next_slab()
            pga = [next_ps() for _ in range(4)]
            for kc in range(8):
                for j in range(4):
                    mm(pga[j], pga[j].ap, wa_t, wa[:, kc, j * 128:(j + 1) * 128], hn[kc], hn[kc].ap, kc == 0, kc == 7)
            pgg = [next_ps() for _ in range(4)]
            for kc in range(8):
                for j in range(4):
                    mm(pgg[j], pgg[j].ap, wg_t, wg[:, kc, j * 128:(j + 1) * 128], hn[kc], hn[kc].ap, kc == 0, kc == 7)
            for j in range(4):
                c = half * 4 + j
                psa = pga[j]
                psg = pgg[j]
                sg = p32.get()
                act(sg, sg.ap, psg, psg.ap, AF.Sigmoid, bias=pvc("bpw1", 8 + c), extra_reads=[pv])
                gt = glu_tmp[c]
                P.op("dve", (lambda gt, psa, sg, c: lambda e: e.scalar_tensor_tensor(
                    out=gt.ap[:, KW - 1:], in0=psa.ap, scalar=pvc("bpw1", c), in1=sg.ap,
                    op0=ALU.add, op1=ALU.mult))(gt, psa, sg, c),
                    reads=[psa, sg, pv], writes=[gt])
                p32.put(sg)
                if first_of_seq:
                    P.op("pool", (lambda gt: lambda e: e.memset(gt.ap[:, 0:KW - 1], 0.0))(gt), writes=[gt])
                else:
                    P.op("pool", (lambda gt, c: lambda e: e.tensor_copy(out=gt.ap[:, 0:KW - 1], in_=halo0[c].ap))(gt, c),
                         reads=[halo0[c]], writes=[gt])
                P.op("pool", (lambda gt, c: lambda e: e.tensor_copy(out=halo0[c].ap, in_=gt.ap[:, NT:NT + KW - 1]))(gt, c),
                     reads=[gt], writes=[halo0[c]])
            for j in range(4):
                c = half * 4 + j
                d_t, _ = next_slab()
                gt = glu_tmp[c]
                psc = next_ps()
                for k in range(KW):
                    mm(psc, psc.ap, d_t, d_t.ap[:, k * 128:(k + 1) * 128], gt, gt.ap[:, k:k + NT], k == 0, k == KW - 1)
                a = p32.get()
                act(a, a.ap, psc, psc.ap, AF.Identity, bias=pvc("bdw", c), extra_reads=[pv])
                acc[c] = a
        for t in hn:
            p16.put(t)
        psm = next_ps()
        for c in range(8):
            mm(psm, psm.ap, onesD32, onesD32.ap, acc[c], acc[c].ap, c == 0, c == 7)
        for c in range(8):
            P.op("dve", (lambda c: lambda e: e.tensor_tensor(
                out=acc[c].ap, in0=acc[c].ap, in1=psm.ap, op=ALU.subtract))(c),
                reads=[acc[c], psm], writes=[acc[c]])
        psv = next_ps()
        for c in range(8):
            sq = p16.get()
            act(sq, sq.ap, acc[c], acc[c].ap, AF.Square)
            mm(psv, psv.ap, onesD, onesD.ap, sq, sq.ap, c == 0, c == 7)
            p16.put(sq)
        r = rstd_from(psv)
        ua = []
        for c in range(8):
            P.op("dve", (lambda c: lambda e: e.tensor_tensor(
                out=acc[c].ap, in0=acc[c].ap, in1=r.ap, op=ALU.mult))(c),
                reads=[acc[c], r], writes=[acc[c]])
            o = p16.get()
            act(o, o.ap, acc[c], acc[c].ap, AF.Silu, bias=pvc("lnb", c), scale=pvc("lng", c), extra_reads=[pv])
            ua.append(o)
            p32.put(acc[c])
        p32.put(r)
        for half in range(2):
            w_t, w = next_slab()
            pgrp = [next_ps() for _ in range(4)]
            for kc in range(8):
                for j in range(4):
                    mm(pgrp[j], pgrp[j].ap, w_t, w[:, kc, j * 128:(j + 1) * 128], ua[kc], ua[kc].ap, kc == 0, kc == 7)
            for j in range(4):
                mc = half * 4 + j
                ps = pgrp[j]
                P.op("dve", (lambda ps, mc: lambda e: e.scalar_tensor_tensor(
                    out=h[mc].ap, in0=ps.ap, scalar=pvc("bpw2", mc), in1=h[mc].ap,
                    op0=ALU.add, op1=ALU.add))(ps, mc), reads=[ps, pv, h[mc]], writes=[h[mc]])
        for t in ua:
            p16.put(t)

    def mlp(h, hn):
        hid = []
        for g in range(8):
            w_t, w = next_slab()
            pgrp = [next_ps() for _ in range(4)]
            for kc in range(8):
                for j in range(4):
                    mm(pgrp[j], pgrp[j].ap, w_t, w[:, kc, j * 128:(j + 1) * 128], hn[kc], hn[kc].ap, kc == 0, kc == 7)
            for j in range(4):
                ps = pgrp[j]
                sq = p32.get()
                act(sq, sq.ap, ps, ps.ap, AF.Square)
                o = p16.get()
                P.op("dve", (lambda o, ps, sq: lambda e: e.scalar_tensor_tensor(
                    out=o.ap, in0=ps.ap, scalar=0.0, in1=sq.ap, op0=ALU.is_gt, op1=ALU.mult))(o, ps, sq),
                    reads=[ps, sq], writes=[o])
                p32.put(sq)
                hid.append(o)
        for t in hn:
            p16.put(t)
        for mg in range(2):
            pss = [next_ps() for _ in range(4)]
            for kg in range(4):
                w_t, w = next_slab()
                for j in range(4):
                    for kc in range(8):
                        mm(pss[j], pss[j].ap, w_t, w[:, kc, j * 128:(j + 1) * 128],
                           hid[kg * 8 + kc], hid[kg * 8 + kc].ap, kg == 0 and kc == 0, kg == 3 and kc == 7)
            for j in range(4):
                mc = mg * 4 + j
                P.op("dve", (lambda ps, mc: lambda e: e.tensor_tensor(
                    out=h[mc].ap, in0=h[mc].ap, in1=ps.ap, op=ALU.add))(pss[j], mc),
                    reads=[pss[j], h[mc]], writes=[h[mc]])
        for t in hid:
            p16.put(t)


    if depth > 1:
        wab32 = T(sb("wab32", [128, 8, 16], F32), "wab32")
        wab = T(sb("wab", [128, 8, 16], BF16), "wab")
        negA4 = T(sb("negA4", [128, 32], F32), "negA4")
        onesE = T(sb("onesE", [128, 128], BF16), "onesE")
        halo1 = T(sb("halo1", [128, 72], F32), "halo1")
        S32 = [T(sb(f"S32_{i}", [128, 512], F32), f"S32_{i}") for i in range(2)]
        Sbf = [T(sb(f"Sbf_{i}", [128, 512], BF16), f"Sbf_{i}") for i in range(2)]
        pre_tmp = [T(sb(f"pre{i}", [128, NT + 3], F32), f"pre{i}") for i in range(4)]
        GE = T(sb("GE", [128, 1024], F32), "GE")

        def b16(name):
            return T(sb(name, [128, 1024], BF16), name)
        Yk, kd, vtok, qkT, qgT, dec, egcR = [b16(n) for n in ("Yk", "kd", "vtok", "qkT", "qgT", "dec", "egcR")]
        decs = T(sb("decs", [128, 1024], F32), "decs")

        def pair(name, dt):
            return [T(sb(f"{name}_{i}", [128, 512], dt), f"{name}_{i}") for i in range(2)]
        CDT = BF16
        TW = pair("TW", CDT)
        P0 = pair("P0", CDT)
        Am = pair("Am", CDT)
        Xm = pair("Xm", CDT)
        MT = Am
        identR = T(sb("identR", [128, 128], CDT), "identR")
        P.op("dve", lambda e: e.tensor_copy(out=identR.ap, in_=ident32), reads=[cst], writes=[identR])
        nZw = pair("nZw", BF16)
        vnew = pair("vnew", BF16)
        gatesT = T(sb("gates", [128, 12 * 32], F32), "gates")

        class NS:
            pass
        BS = [NS(), NS()]
        BS[0].Yk, BS[0].kd, BS[0].vtok, BS[0].qkT, BS[0].qgT = Yk, kd, vtok, qkT, qgT
        BS[0].TW, BS[0].P0, BS[0].Am, BS[0].Xm, BS[0].nZw, BS[0].vnew = TW, P0, Am, Xm, nZw, vnew
        BS[1].Yk = TV(pre_tmp[0], pre_tmp[0].ap.bitcast(BF16)[:, 0:1024])
        BS[1].kd = TV(pre_tmp[1], pre_tmp[1].ap.bitcast(BF16)[:, 0:1024])
        BS[1].vtok = TV(pre_tmp[2], pre_tmp[2].ap.bitcast(BF16)[:, 0:1024])
        BS[1].qkT = b16("qkT1")
        BS[1].qgT = b16("qgT1")
        BS[1].TW = pair("TW1", CDT)
        BS[1].P0 = pair("P01", CDT)
        BS[1].Am = pair("Am1", CDT)
        BS[1].Xm = pair("Xm1", CDT)
        BS[1].nZw = [TV(glu_tmp[i], glu_tmp[i].ap[:, 0:512]) for i in range(2)]
        BS[1].vnew = [TV(glu_tmp[2 + i], glu_tmp[2 + i].ap[:, 0:512]) for i in range(2)]
        src = wd["gdn_w_in"][0, :, 4096:4112].rearrange("(kc p) m -> p kc m", p=128)
        P.op("sp", lambda e: e.dma_start(out=wab32.ap, in_=src), writes=[wab32], dma="wab")
        P.op("dve", lambda e: e.tensor_copy(out=wab.ap, in_=wab32.ap), reads=[wab32], writes=[wab])
        P.op("dve", lambda e: e.memset(onesE.ap, 1.0 / 128), writes=[onesE])
        P.op("act", lambda e: e.activation(negA4.ap, pv.ap[:, PV["alog4"]:PV["alog4"] + 32], AF.Exp),
             reads=[pv], writes=[negA4])
        P.op("dve", lambda e: e.tensor_scalar(out=negA4.ap, in0=negA4.ap, scalar1=-1.0, scalar2=None, op0=ALU.mult),
             reads=[negA4], writes=[negA4])

    def v3(ap, n):
        return ap.rearrange("p (h i) -> p h i", h=n)

    def q4(t, h4):
        return t.ap[:, h4 * 128:(h4 + 1) * 128]

    def q4r(t, h4):
        return t.ap[:, h4 * 128:(h4 + 1) * 128]

    def f32(ap):
        return ap.bitcast(F32) if CHAIN_R[0] else ap

    def hs(t, h):
        return t.ap[:, h * 128:(h + 1) * 128]

    def gdn(h, hn, first):
        if first:
            for i in range(2):
                P.op("pool", lambda e, i=i: e.memset(S32[i].ap, 0.0), writes=[S32[i]])
                P.op("pool", lambda e, i=i: e.memset(Sbf[i].ap, 0.0), writes=[Sbf[i]])
        one_ap = epsT.ap[:, 1:2]
        gs = gatesT

        def G(i):
            return gs.ap[:, i * 32:(i + 1) * 32]

        def G3(i):
            return G(i).rearrange("p (b x) -> p b x", x=8)

        def gop(eng, fn, extra_r=(), extra_w=()):
            P.op(eng, fn, reads=[gs, *extra_r], writes=[gs, *extra_w])
        pab = next_ps()
        for b in range(4):
            for kc in range(8):
                mm(pab, pab.ap[:, b * 16:(b + 1) * 16], hn[kc], hn[kc].ap[:, b * 128:(b + 1) * 128],
                   wab, wab.ap[:, kc, :], kc == 0, kc == 7)
        pab3 = pab.ap[:, 0:64].rearrange("p (b x) -> p b x", x=16)
        dtb3 = pv.ap[:, PV["dtb4"]:PV["dtb4"] + 32].rearrange("p (b x) -> p b x", x=8)
        gop("dve", lambda e, pab3=pab3: e.tensor_tensor(out=G3(0), in0=pab3[:, :, 0:8], in1=dtb3, op=ALU.add),
            extra_r=[pab, pv])
        gop("act", lambda e: e.activation(G(1), G(0), AF.Exp))
        gop("act", lambda e: e.activation(G(1), G(1), AF.Ln, bias=one_ap), extra_r=[epsT])
        gop("dve", lambda e: e.tensor_tensor(out=G(2), in0=G(1), in1=negA4.ap, op=ALU.mult), extra_r=[negA4])
        gop("act", lambda e, pab3=pab3: e.activation(G3(3), pab3[:, :, 8:16], AF.Exp, scale=-1.0), extra_r=[pab])
        gop("dve", lambda e: e.tensor_scalar(out=G(3), in0=G(3), scalar1=1.0, scalar2=None, op0=ALU.add))
        gop("dve", lambda e: e.reciprocal(out=G(4), in_=G(3)))
        gop("dve", lambda e: e.tensor_scalar(out=G(5), in0=G(4), scalar1=-1.0, scalar2=None, op0=ALU.mult))
        pcs = next_ps()
        mm(pcs, pcs.ap[:, 0:32], cst, triu32, gs, G(2), True, True)
        mm(pcs, pcs.ap[:, 32:64], cst, ones32, gs, G(2), True, True)
        gop("act", lambda e, pcs=pcs: e.activation(G(6), pcs.ap[:, 0:32], AF.Copy), extra_r=[pcs])
        gop("act", lambda e, pcs=pcs: e.activation(G(7), pcs.ap[:, 0:32], AF.Exp), extra_r=[pcs])
        gop("dve", lambda e: e.tensor_scalar(out=G(10), in0=G(6), scalar1=-1.0, scalar2=None, op0=ALU.mult))
        gop("dve", lambda e, pcs=pcs: e.tensor_tensor(out=G(8), in0=pcs.ap[:, 32:64], in1=G(6), op=ALU.subtract),
            extra_r=[pcs])
        gop("act", lambda e: e.activation(G(8), G(8), AF.Exp))
        gop("act", lambda e, pcs=pcs: e.activation(G(9), pcs.ap[:, 32:64], AF.Exp), extra_r=[pcs])

        qT = [None] * 8
        kT = [None] * 8
        vT = [None] * 8
        sz = [None] * 8
        grp = {}

        def S1(g):
            w_t, w = next_slab()
            st = grp[g] = {"a": [None] * 4}
            pgrp = [next_ps() for _ in range(4)]
            for kc in range(8):
                for j in range(4):
                    mm(pgrp[j], pgrp[j].ap, w_t, w[:, kc, j * 128:(j + 1) * 128], hn[kc], hn[kc].ap, kc == 0, kc == 7)
            for j in range(4):
                cc = g * 4 + j
                ps = pgrp[j]
                if cc >= 24:
                    o = p16.get()
                    act(o, o.ap, ps, ps.ap, AF.Silu)
                    sz[cc - 24] = o
                    continue
                pre = pre_tmp[cc % 4]
                act(pre, pre.ap[:, 3:], ps, ps.ap, AF.Copy)
                if first:
                    P.op("pool", (lambda pre: lambda e: e.memset(pre.ap[:, 0:3], 0.0))(pre), writes=[pre])
                else:
                    P.op("pool", (lambda pre, cc: lambda e: e.tensor_copy(
                        out=pre.ap[:, 0:3], in_=halo1.ap[:, cc * 3:cc * 3 + 3]))(pre, cc),
                        reads=[halo1], writes=[pre])
                P.op("pool", (lambda pre, cc: lambda e: e.tensor_copy(
                    out=halo1.ap[:, cc * 3:cc * 3 + 3], in_=pre.ap[:, NT:NT + 3]))(pre, cc),
                    reads=[pre], writes=[halo1])
                a = p32.get()
                P.op("dve", (lambda a, pre, cc: lambda e: e.tensor_scalar(
                    out=a.ap, in0=pre.ap[:, 0:NT], scalar1=pvc("gcw", cc), scalar2=None, op0=ALU.mult))(a, pre, cc),
                    reads=[pre, pv], writes=[a])
                for k in range(1, 4):
                    P.op("dve", (lambda a, pre, cc, k: lambda e: e.scalar_tensor_tensor(
                        out=a.ap, in0=pre.ap[:, k:k + NT], scalar=pvc("gcw", k * 24 + cc), in1=a.ap,
                        op0=ALU.mult, op1=ALU.add))(a, pre, cc, k), reads=[pre, pv, a], writes=[a])
                st["a"][j] = a

        def S2(g):
            if g >= 6:
                return
            for j in range(4):
                cc = g * 4 + j
                a = grp[g]["a"][j]
                if cc >= 16:
                    o = p16.get()
                    act(o, o.ap, a, a.ap, AF.Silu)
                    vT[cc - 16] = o
                    p32.put(a)
                else:
                    act(a, a.ap, a, a.ap, AF.Silu)

        def S3(g):
            if g >= 4:
                return
            pss = []
            sqs = []
            for j in range(4):
                a = grp[g]["a"][j]
                sq = p16.get()
                act(sq, sq.ap, a, a.ap, AF.Square)
                ps2 = next_ps()
                mm(ps2, ps2.ap, onesH, onesH.ap, sq, sq.ap, True, True)
                pss.append(ps2)
                sqs.append(sq)
            for sq in sqs:
                p16.put(sq)
            rs = []
            for j in range(4):
                t = p32.get()
                act(t, t.ap, pss[j], pss[j].ap, AF.Ln, bias=eps_ap(EPS), extra_reads=[epsT])
                rs.append(t)
            for j in range(4):
                act(rs[j], rs[j].ap, rs[j], rs[j].ap, AF.Exp, scale=-0.5)
            for j in range(4):
                cc = g * 4 + j
                a = grp[g]["a"][j]
                r = rs[j]
                o = p16.get()
                if cc < 8:
                    P.op("dve", (lambda o, a, r: lambda e: e.scalar_tensor_tensor(
                        out=o.ap, in0=a.ap, scalar=float(128 ** -0.5), in1=r.ap, op0=ALU.mult, op1=ALU.mult))(o, a, r),
                        reads=[a, r], writes=[o])
                    qT[cc] = o
                else:
                    P.op("dve", (lambda o, a, r: lambda e: e.tensor_tensor(
                        out=o.ap, in0=a.ap, in1=r.ap, op=ALU.mult))(o, a, r), reads=[a, r], writes=[o])
                    kT[cc - 8] = o
                p32.put(a)
                p32.put(r)

        for step in range(9):
            if step < 8:
                S1(step)
            if step >= 1:
                S2(step - 1)
                S3(step - 1)

        oT = [p32.get() for _ in range(8)]
        triu_b8 = triu32.unsqueeze(1).broadcast_to([128, 8, 128])
        stri_b8 = cst.ap[:, 384:512].unsqueeze(1).broadcast_to([128, 8, 128])
        ident_b4 = ident32.unsqueeze(1).broadcast_to([128, 4, 128])
        def lmask(l):
            return cst.ap[:, 640 + l * 128:640 + (l + 1) * 128].unsqueeze(1).broadcast_to([128, 4, 128])

        def prep(b, B):
            tb = slice(b * 128, (b + 1) * 128)

            def gb(i):
                return gs.ap[:, i * 32 + b * 8:i * 32 + b * 8 + 8]

            def g1(i, hh):
                return gs.ap[:, i * 32 + b * 8 + hh:i * 32 + b * 8 + hh + 1]
            P.op("dve", lambda e: e.tensor_copy(
                out=v3(GE.ap, 8), in_=gb(2).unsqueeze(2).broadcast_to([128, 8, 128])),
                reads=[gs], writes=[GE])
            pR = [next_ps(), next_ps()]
            pRm = [next_ps(), next_ps()]
            for hh in range(8):
                mm(pR[hh // 4], q4(pR[hh // 4], hh % 4), GE, hs(GE, hh), cst, triu32, True, True)
                mm(pRm[hh // 4], q4(pRm[hh // 4], hh % 4), GE, hs(GE, hh), cst, triu32, True, False)
                mm(pRm[hh // 4], q4(pRm[hh // 4], hh % 4), cbf, identbf, cbf, masknegbf, False, True)
            ptr = next_ps()
            ptr_bf = ptr.ap.bitcast(BF16)
            for hh in range(8):
                P.op("pe", lambda e, hh=hh: e.transpose(
                    out=ptr_bf[:, hh * 128:(hh + 1) * 128], in_=kT[hh].ap[:, tb], identity=identbf),
                    reads=[kT[hh], cbf], writes=[ptr])
            ptv = next_ps()
            ptv_bf = ptv.ap.bitcast(BF16)
            for hh in range(8):
                P.op("pe", lambda e, hh=hh: e.transpose(
                    out=ptv_bf[:, hh * 128:(hh + 1) * 128], in_=vT[hh].ap[:, tb], identity=identbf),
                    reads=[vT[hh], cbf], writes=[ptv])
            for hg in range(2):
                act(egcR, egcR.ap[:, hg * 512:(hg + 1) * 512], pR[hg], pR[hg].ap, AF.Exp)
            for hh in range(8):
                act(dec, hs(dec, hh), pRm[hh // 4], q4(pRm[hh // 4], hh % 4), AF.Exp,
                    bias=g1(10, hh), extra_reads=[gs])
            P.op("dve", lambda e: e.tensor_tensor(
                out=v3(decs.ap, 8), in0=v3(dec.ap, 8), in1=stri_b8, op=ALU.mult),
                reads=[dec, cst], writes=[decs])
            P.op("dve", lambda e: e.tensor_tensor(
                out=v3(decs.ap, 8), in0=v3(decs.ap, 8), in1=gb(5).unsqueeze(2).broadcast_to([128, 8, 128]), op=ALU.mult),
                reads=[decs, gs], writes=[decs])
            P.op("dve", lambda e: e.tensor_tensor(
                out=v3(B.Yk.ap, 8), in0=v3(ptr_bf, 8), in1=gb(7).unsqueeze(2).broadcast_to([128, 8, 128]), op=ALU.mult),
                reads=[ptr, gs], writes=[B.Yk])
            P.op("dve", lambda e: e.tensor_tensor(
                out=v3(B.kd.ap, 8), in0=v3(ptr_bf, 8), in1=gb(8).unsqueeze(2).broadcast_to([128, 8, 128]), op=ALU.mult),
                reads=[ptr, gs], writes=[B.kd])
            act(B.vtok, B.vtok.ap, ptv, ptv_bf, AF.Copy)
            for hg in range(2):
                pkk = next_ps()
                pqk = next_ps()
                for h4 in range(4):
                    hh = hg * 4 + h4
                    mm(pkk, q4(pkk, h4), kT[hh], kT[hh].ap[:, tb], kT[hh], kT[hh].ap[:, tb], True, True)
                    mm(pqk, q4(pqk, h4), kT[hh], kT[hh].ap[:, tb], qT[hh], qT[hh].ap[:, tb], True, True)
                P.op("dve", lambda e, hg=hg, pkk=pkk: e.tensor_tensor(
                    out=B.TW[hg].ap, in0=pkk.ap, in1=decs.ap[:, hg * 512:(hg + 1) * 512], op=ALU.mult),
                    reads=[pkk, decs], writes=[B.TW[hg]])
                P.op("dve", lambda e, hg=hg, pqk=pqk: e.tensor_tensor(
                    out=B.qkT.ap[:, hg * 512:(hg + 1) * 512], in0=pqk.ap, in1=dec.ap[:, hg * 512:(hg + 1) * 512], op=ALU.mult),
                    reads=[pqk, dec], writes=[B.qkT])
            for hh in range(8):
                P.op("pool", lambda e, hh=hh: e.tensor_tensor(
                    out=hs(B.qgT, hh), in0=qT[hh].ap[:, tb], in1=hs(egcR, hh), op=ALU.mult),
                    reads=[qT[hh], egcR], writes=[B.qgT])
            for hg in range(2):
                pp = next_ps()
                pp_b = pp.ap.bitcast(BF16)
                for h4 in range(4):
                    P.op("pe", lambda e, h4=h4, hg=hg, pp_b=pp_b: e.transpose(
                        out=pp_b[:, h4 * 128:(h4 + 1) * 128], in_=q4(B.TW[hg], h4), identity=identbf),
                        reads=[B.TW[hg], cbf], writes=[pp])
                act(B.P0[hg], B.P0[hg].ap, pp, pp_b[:, 0:512], AF.Copy)
                P.op("dve", lambda e, hg=hg: e.tensor_tensor(
                    out=v3(B.Am[hg].ap, 4), in0=v3(B.TW[hg].ap, 4), in1=lmask(0), op=ALU.mult),
                    reads=[B.TW[hg], cst], writes=[B.Am[hg]])
                P.op("dve", lambda e, hg=hg: e.tensor_tensor(
                    out=v3(B.Am[hg].ap, 4), in0=v3(B.Am[hg].ap, 4), in1=ident_b4, op=ALU.add),
                    reads=[B.Am[hg], cst], writes=[B.Am[hg]])
            for hg in range(2):
                px = next_ps()
                px_b = px.ap.bitcast(BF16)
                for h4 in range(4):
                    P.op("pe", lambda e, h4=h4, hg=hg, px_b=px_b: e.transpose(
                        out=px_b[:, h4 * 128:(h4 + 1) * 128], in_=q4(B.Am[hg], h4), identity=identbf),
                        reads=[B.Am[hg], cbf], writes=[px])
                act(B.Xm[hg], B.Xm[hg].ap, px, px_b[:, 0:512], AF.Copy)

        def level(B, l):
            for hg in range(2):
                pW = next_ps()
                for h4 in range(4):
                    mm(pW, q4(pW, h4), B.P0[hg], q4(B.P0[hg], h4), B.Am[hg], q4(B.Am[hg], h4), True, True)
                P.op("dve", lambda e, hg=hg, pW=pW: e.tensor_tensor(
                    out=v3(B.TW[hg].ap, 4), in0=v3(pW.ap, 4), in1=lmask(l), op=ALU.mult),
                    reads=[pW, cst], writes=[B.TW[hg]])
            for hg in range(2):
                pA = next_ps()
                for h4 in range(4):
                    mm(pA, q4(pA, h4), B.Xm[hg], q4(B.Xm[hg], h4), B.TW[hg], q4(B.TW[hg], h4), True, False)
                    mm(pA, q4(pA, h4), cbf, identbf, B.Am[hg], q4(B.Am[hg], h4), False, True)
                if l < 6:
                    pX = next_ps()
                    for h4 in range(4):
                        mm(pX, q4(pX, h4), B.TW[hg], q4(B.TW[hg], h4), B.Xm[hg], q4(B.Xm[hg], h4), True, False)
                        mm(pX, q4(pX, h4), cbf, identbf, B.Xm[hg], q4(B.Xm[hg], h4), False, True)
                act(B.Am[hg], B.Am[hg].ap, pA, pA.ap, AF.Copy)
                if l < 6:
                    act(B.Xm[hg], B.Xm[hg].ap, pX, pX.ap, AF.Copy)

        def post(B):
            for hg in range(2):
                pz = next_ps()
                for h4 in range(4):
                    hh = hg * 4 + h4
                    mm(pz, q4(pz, h4), B.Yk, hs(B.Yk, hh), B.Am[hg], q4(B.Am[hg], h4), True, True)
                act(B.nZw[hg], B.nZw[hg].ap, pz, pz.ap, AF.Copy, scale=-1.0)

        def scan(b, B):
            tb = slice(b * 128, (b + 1) * 128)

            def gb(i):
                return gs.ap[:, i * 32 + b * 8:i * 32 + b * 8 + 8]

            def g1(i, hh):
                return gs.ap[:, i * 32 + b * 8 + hh:i * 32 + b * 8 + hh + 1]
            for hg in range(2):
                pvn = next_ps()
                for h4 in range(4):
                    hh = hg * 4 + h4
                    mm(pvn, q4(pvn, h4), B.Am[hg], q4(B.Am[hg], h4), B.vtok, hs(B.vtok, hh), True, False)
                    mm(pvn, q4(pvn, h4), B.nZw[hg], q4(B.nZw[hg], h4), Sbf[hg], q4(Sbf[hg], h4), False, True)
                P.op("dve", lambda e, hg=hg, pvn=pvn: e.tensor_tensor(
                    out=v3(B.vnew[hg].ap, 4), in0=v3(pvn.ap, 4),
                    in1=gb(4)[:, hg * 4:hg * 4 + 4].unsqueeze(2).broadcast_to([128, 4, 128]), op=ALU.mult),
                    reads=[pvn, gs], writes=[B.vnew[hg]])
            for hg in range(2):
                po = next_ps()
                for h4 in range(4):
                    hh = hg * 4 + h4
                    mm(po, q4(po, h4), Sbf[hg], q4(Sbf[hg], h4), B.qgT, hs(B.qgT, hh), True, False)
                    mm(po, q4(po, h4), B.vnew[hg], q4(B.vnew[hg], h4), B.qkT, hs(B.qkT, hh), False, True)
                for h4 in range(4):
                    hh = hg * 4 + h4
                    act(oT[hh], oT[hh].ap[:, tb], po, q4(po, h4), AF.Copy)
            for hg in range(2):
                pS = next_ps()
                for h4 in range(4):
                    hh = hg * 4 + h4
                    mm(pS, q4(pS, h4), B.kd, hs(B.kd, hh), B.vnew[hg], q4(B.vnew[hg], h4), True, True)
                for h4 in range(4):
                    hh = hg * 4 + h4
                    P.op("dve", lambda e, hh=hh, h4=h4, hg=hg, pS=pS: e.scalar_tensor_tensor(
                        out=q4(S32[hg], h4), in0=q4(S32[hg], h4), scalar=g1(9, hh), in1=q4(pS, h4),
                        op0=ALU.mult, op1=ALU.add), reads=[S32[hg], gs, pS], writes=[S32[hg]])
                act(Sbf[hg], Sbf[hg].ap, S32[hg], S32[hg].ap, AF.Copy)

        for pr in range(2):
            b0, b1 = 2 * pr, 2 * pr + 1
            prep(b0, BS[0])
            prep(b1, BS[1])
            for l in range(1, 7):
                level(BS[0], l)
                level(BS[1], l)
            post(BS[0])
            post(BS[1])
            scan(b0, BS[0])
            scan(b1, BS[1])
        for t in hn + qT + kT + vT:
            p16.put(t)
        onT = []
        for hh in range(8):
            sq = p16.get()
            act(sq, sq.ap, oT[hh], oT[hh].ap, AF.Square)
            ps = next_ps()
            mm(ps, ps.ap, onesE, onesE.ap, sq, sq.ap, True, True)
            p16.put(sq)
            r = rstd_from(ps)
            P.op("dve", lambda e, hh=hh, r=r: e.scalar_tensor_tensor(
                out=oT[hh].ap, in0=oT[hh].ap, scalar=pvc("gng"), in1=r.ap, op0=ALU.mult, op1=ALU.mult),
                reads=[oT[hh], pv, r], writes=[oT[hh]])
            o = p16.get()
            P.op("dve", lambda e, hh=hh, o=o: e.tensor_tensor(
                out=o.ap, in0=oT[hh].ap, in1=sz[hh].ap, op=ALU.mult), reads=[oT[hh], sz[hh]], writes=[o])
            onT.append(o)
            p32.put(r)
            p32.put(oT[hh])
            p16.put(sz[hh])
        for half in range(2):
            w_t, w = next_slab()
            pgrp = [next_ps() for _ in range(4)]
            for kc in range(8):
                for j in range(4):
                    mm(pgrp[j], pgrp[j].ap, w_t, w[:, kc, j * 128:(j + 1) * 128], onT[kc], onT[kc].ap, kc == 0, kc == 7)
            for j in range(4):
                mc = half * 4 + j
                ps = pgrp[j]
                P.op("dve", lambda e, ps=ps, mc=mc: e.tensor_tensor(
                    out=h[mc].ap, in0=h[mc].ap, in1=ps.ap, op=ALU.add), reads=[ps, h[mc]], writes=[h[mc]])
        for t in onT:
            p16.put(t)

    def load_x(t):
        hb = hbuf[t % 2]
        for c in range(8):
            P.op("sp", (lambda c, hb, t: lambda e: e.dma_start(
                out=hb[c].ap, in_=xT[c * 128:(c + 1) * 128, t * NT:(t + 1) * NT]))(c, hb, t),
                writes=[hb[c]], dma=("x", t % 2, c))

    store_ops = []

    def tile_body(t):
        h = hbuf[t % 2]
        first = (t % tiles_per_seq) == 0
        if t + 1 < ntiles:
            load_x(t + 1)
        for li in range(depth):
            hn = rmsnorm(h, f"nmg{li}")
            if li % 2 == 0:
                conformer(h, hn, first)
            else:
                gdn(h, hn, first)
            hn = rmsnorm(h, f"nfg{li}")
            mlp(h, hn)
        ps = next_ps()
        for c in range(8):
            sq = p16.get()
            act(sq, sq.ap, h[c], h[c].ap, AF.Square)
            mm(ps, ps.ap, onesD, onesD.ap, sq, sq.ap, c == 0, c == 7)
            p16.put(sq)
        r = rstd_from(ps)
        for c in range(8):
            o = p32.get()
            P.op("dve", (lambda o, c: lambda e: e.scalar_tensor_tensor(
                out=o.ap, in0=h[c].ap, scalar=pvc("fng", c), in1=r.ap, op0=ALU.mult, op1=ALU.mult))(o, c),
                reads=[h[c], pv, r], writes=[o])
            i = P.op("pool", (lambda o, c: lambda e: e.dma_start(
                out=outT[c * 128:(c + 1) * 128, t * NT:(t + 1) * NT], in_=o.ap))(o, c),
                reads=[o], dma=("st", o.name))
            store_ops.append(i)
            p32.put(o)
        p32.put(r)

    load_x(0)
    for t in range(ntiles):
        tile_body(t)
    fin = P.op("pool", lambda e: e.nop(), reads=[], writes=[])
    P.ops[fin][2].update(store_ops)
    P.emit(nc)
    return nc, P


_CACHE = {}


def _run(inputs, nseq, S, depth, ncores):
    key = (nseq, S, depth)
    if key not in _CACHE:
        _CACHE[key] = build_nc(nseq, S, depth)
    nc, P = _CACHE[key]
    x = np.asarray(inputs["x"], np.float32)
    B = x.shape[0]
    assert B == nseq * ncores
    pvec = pack_pvec(inputs)
    cst = make_consts()
    in_maps = []
    for i in range(ncores):
        xs = x[i * nseq:(i + 1) * nseq].reshape(nseq * S, D)
        m = {"xT": np.ascontiguousarray(xs.T), "pvec": pvec, "cst": cst,
             "cv_w_pw1": np.asarray(inputs["cv_w_pw1"], np.float32),
             "cv_w_pw2": np.asarray(inputs["cv_w_pw2"], np.float32),
             "mlp_w1": np.asarray(inputs["mlp_w1"], np.float32),
             "mlp_w2": np.asarray(inputs["mlp_w2"], np.float32)}
        if depth > 1:
            m["gdn_w_in"] = np.asarray(inputs["gdn_w_in"], np.float32)
            m["gdn_w_out"] = np.asarray(inputs["gdn_w_out"], np.float32)
        in_maps.append(m)
    res = run_bass_kernel_spmd(nc, in_maps, core_ids=list(range(ncores)))
    outs = [np.asarray(r["outT"]).T.reshape(nseq, S, D) for r in res.results]
    return np.ascontiguousarray(np.concatenate(outs, axis=0).astype(np.float32))


def kernel(**inputs):
    return _run(inputs, 2, 4096, 2, 8)
```

```python
import numpy as np
import concourse.bass as bass
import concourse.mybir as mybir
from concourse.bass_utils import run_bass_kernel_spmd

F32 = mybir.dt.float32
F32R = mybir.dt.float32r
CHAIN_R = [False]
BF16 = mybir.dt.bfloat16
AF = mybir.ActivationFunctionType
ALU = mybir.AluOpType

D = 1024
NC8 = 8
NT = 512
EPS = 1e-6
KW = 31
H = 8
QKV = 3072
DEBUG_STOP = [99]
GIN = 4112

COMPUTE = ("pe", "act", "dve", "pool")


class T:
    __slots__ = ("ap", "w", "r", "name")

    def __init__(self, ap, name=""):
        self.ap = ap
        self.w = None
        self.r = {}
        self.name = name


class TV:
    __slots__ = ("ap", "base", "name")

    def __init__(self, base, ap):
        self.base = base
        self.ap = ap
        self.name = base.name + "_v"

    @property
    def w(self):
        return self.base.w

    @w.setter
    def w(self, v):
        self.base.w = v

    @property
    def r(self):
        return self.base.r

    @r.setter
    def r(self, v):
        self.base.r = v


class Prog:
    def __init__(self):
        self.ops = []
        self.last_dma = {}

    def op(self, eng, fn, reads=(), writes=(), dma=None):
        i = len(self.ops)
        deps = set()
        for t in reads:
            if t.w is not None:
                deps.add(t.w)
        for t in writes:
            if t.w is not None:
                deps.add(t.w)
            deps.update(t.r.values())
        if dma is not None and dma in self.last_dma:
            deps.add(self.last_dma[dma])
        if dma is not None:
            self.last_dma[dma] = i
        for t in writes:
            t.w = i
            t.r = {}
        for t in reads:
            if dma is not None:
                t.r[("dma", i)] = i
            else:
                t.r[eng] = i
        self.ops.append((eng, fn, deps, dma))
        return i

    def emit(self, nc, sem_limit=30000):
        ops = self.ops
        n = len(ops)
        red = []
        needed = [False] * n
        for i, (eng, fn, deps, dma) in enumerate(ops):
            best = {}
            dl = []
            for d in deps:
                de, _, _, ddma = ops[d]
                if ddma is not None:
                    dl.append(d)
                else:
                    if de == eng and eng == "pe":
                        continue
                    if de not in best or best[de] < d:
                        best[de] = d
            dl.extend(best.values())
            red.append(dl)
            for d in dl:
                needed[d] = True
        ordinal = [0] * n
        cnt = {}
        dcnt = {}
        for i, (eng, fn, deps, dma) in enumerate(ops):
            if dma is not None:
                dcnt[dma] = dcnt.get(dma, 0) + 1
                ordinal[i] = 16 * dcnt[dma]
            elif needed[i]:
                cnt[eng] = cnt.get(eng, 0) + 1
                ordinal[i] = cnt[eng]
        self.stats = dict(cnt)
        sems = {}

        def get_sem(key):
            if key not in sems:
                sems[key] = nc.alloc_semaphore(name="s_" + "_".join(str(k) for k in key))
            return sems[key]

        def signal_of(i):
            eng, _, _, dma = ops[i]
            if dma is not None:
                return ("d", dma), ordinal[i], ordinal[i]
            o = ordinal[i]
            ep = (o - 1) // sem_limit
            return ("e", eng, ep), o - ep * sem_limit, o

        for i in range(n):
            if ops[i][3] is not None or needed[i]:
                get_sem(signal_of(i)[0])

        by_eng = {}
        for i, o in enumerate(ops):
            by_eng.setdefault(o[0], []).append(i)

        def run_engine(ename, e):
            known = {}
            nwait = 0
            for i in by_eng.get(ename, []):
                eng, fn, deps, dma = ops[i]
                for d in red[i]:
                    key, local, glob = signal_of(d)
                    kk = key if key[0] == "d" else ("e", key[1])
                    if known.get(kk, 0) >= glob:
                        continue
                    known[kk] = glob
                    e.wait_ge(get_sem(key), local)
                    nwait += 1
                ins = fn(e)
                if dma is not None:
                    ins.then_inc(get_sem(("d", dma)), 16)
                elif needed[i]:
                    ins.then_inc(get_sem(signal_of(i)[0]), 1)
            self.stats["wait_" + ename] = nwait

        with nc.Block() as block:
            @block.tensor
            def _(e):
                run_engine("pe", e)

            @block.scalar
            def _(e):
                run_engine("act", e)

            @block.vector
            def _(e):
                run_engine("dve", e)

            @block.gpsimd
            def _(e):
                run_engine("pool", e)

            @block.sync
            def _(e):
                run_engine("sp", e)


class UnitPool:
    def __init__(self, nc, n, dtype, name, width=NT):
        self.free = []
        for i in range(n):
            ap = nc.alloc_sbuf_tensor(f"{name}{i}", [128, width], dtype).ap()
            self.free.append(T(ap, f"{name}{i}"))
        self.n = n

    def get(self):
        if not self.free:
            raise RuntimeError("unit pool exhausted")
        return self.free.pop(0)

    def put(self, t):
        self.free.append(t)


def _pv_layout():
    cols = {}
    off = 0

    def add(name, n):
        nonlocal off
        cols[name] = off
        off += n
    for i in range(2):
        add(f"nmg{i}", 8)
        add(f"nfg{i}", 8)
    add("fng", 8)
    add("bpw1", 16)
    add("wdw", KW * 8)
    add("bdw", 8)
    add("lng", 8)
    add("lnb", 8)
    add("bpw2", 8)
    add("gcw", 4 * 24)
    add("gng", 1)
    add("alog", 8)
    add("dtb", 8)
    add("alog4", 32)
    add("dtb4", 32)
    return cols, off


PV, NPV = _pv_layout()


def _chunked(v):
    return np.ascontiguousarray(v.reshape(-1, 128).T)


def pack_pvec(inp):
    pv = np.zeros((128, NPV), np.float32)

    def put(name, arr):
        pv[:, PV[name]:PV[name] + arr.shape[1]] = arr
    for i in range(2):
        put(f"nmg{i}", _chunked(inp["norm_mix_g"][i]))
        put(f"nfg{i}", _chunked(inp["norm_ffn_g"][i]))
    put("fng", _chunked(inp["final_norm_g"]))
    put("bpw1", _chunked(inp["cv_b_pw1"][0]))
    wdw = inp["cv_w_dw"][0]
    put("wdw", np.concatenate([_chunked(wdw[k]) for k in range(KW)], axis=1))
    put("bdw", _chunked(inp["cv_b_dw"][0]))
    put("lng", _chunked(inp["cv_ln_g"][0]))
    put("lnb", _chunked(inp["cv_ln_b"][0]))
    put("bpw2", _chunked(inp["cv_b_pw2"][0]))
    gcw = inp["gdn_conv_w"][0]
    put("gcw", np.concatenate([_chunked(gcw[k]) for k in range(4)], axis=1))
    put("gng", inp["gdn_norm_g"][0].reshape(128, 1))
    put("alog", np.broadcast_to(inp["gdn_a_log"][0][None, :], (128, 8)))
    put("dtb", np.broadcast_to(inp["gdn_dt_bias"][0][None, :], (128, 8)))
    put("alog4", np.broadcast_to(np.tile(inp["gdn_a_log"][0], 4)[None, :], (128, 32)))
    put("dtb4", np.broadcast_to(np.tile(inp["gdn_dt_bias"][0], 4)[None, :], (128, 32)))
    return pv


def make_consts():
    idx = np.arange(128)
    ident = np.eye(128, dtype=np.float32)
    triu = (idx[:, None] <= idx[None, :]).astype(np.float32)
    ones = np.ones((128, 128), np.float32)
    stri = (idx[:, None] < idx[None, :]).astype(np.float32)
    maskneg = np.where(idx[:, None] <= idx[None, :], 0.0, -30000.0).astype(np.float32)
    lv = []
    for l in range(7):
        sz = 1 << l
        i = idx[None, :]
        j = idx[:, None]
        m = ((i // (2 * sz)) == (j // (2 * sz))) & ((i % (2 * sz)) >= sz) & ((j % (2 * sz)) < sz)
        lv.append(m.astype(np.float32))
    return np.concatenate([ident, triu, ones, stri, maskneg] + lv, axis=1)


def slab_table(depth):
    sl = []
    for li in range(depth):
        if li % 2 == 0:
            for half in range(2):
                sl.append(("cv_w_pw1", 0, 0, half * 512))
                sl.append(("cv_w_pw1", 0, 0, 1024 + half * 512))
                for j in range(4):
                    sl.append(("diag", half * 4 + j, 0, 0))
            for m0 in (0, 512):
                sl.append(("cv_w_pw2", 0, 0, m0))
        else:
            for m0 in range(0, 4096, 512):
                sl.append(("gdn_w_in", 0, 0, m0))
            for m0 in (0, 512):
                sl.append(("gdn_w_out", 0, 0, m0))
        for m0 in range(0, 4096, 512):
            sl.append(("mlp_w1", li, 0, m0))
        for mg in range(2):
            for kg in range(4):
                sl.append(("mlp_w2", li, kg * 1024, mg * 512))
    return sl


def build_nc(nseq, S, depth, nring=3):
    ntok = nseq * S
    ntiles = ntok // NT
    tiles_per_seq = S // NT
    nc = bass.Bass("TRN2", target_bir_lowering=False)
    P = Prog()

    xT = nc.dram_tensor("xT", [D, ntok], F32, kind="ExternalInput").ap()
    outT = nc.dram_tensor("outT", [D, ntok], F32, kind="ExternalOutput").ap()
    pvec_d = nc.dram_tensor("pvec", [128, NPV], F32, kind="ExternalInput").ap()
    cst_d = nc.dram_tensor("cst", [128, 1536], F32, kind="ExternalInput").ap()
    wd = {}
    wd["cv_w_pw1"] = nc.dram_tensor("cv_w_pw1", [1, D, 2 * D], F32, kind="ExternalInput").ap()
    wd["cv_w_pw2"] = nc.dram_tensor("cv_w_pw2", [1, D, D], F32, kind="ExternalInput").ap()
    wd["mlp_w1"] = nc.dram_tensor("mlp_w1", [2, D, 4 * D], F32, kind="ExternalInput").ap()
    wd["mlp_w2"] = nc.dram_tensor("mlp_w2", [2, 4 * D, D], F32, kind="ExternalInput").ap()
    if depth > 1:
        wd["gdn_w_in"] = nc.dram_tensor("gdn_w_in", [1, D, GIN], F32, kind="ExternalInput").ap()
        wd["gdn_w_out"] = nc.dram_tensor("gdn_w_out", [1, D, D], F32, kind="ExternalInput").ap()
    slabs = slab_table(depth)
    nslab = len(slabs)
    wbf = nc.dram_tensor("wbf", [nslab, 128, 4096], BF16, kind="Internal").ap()
    wdiag = nc.dram_tensor("wdiag", [8, 128, 4096], BF16, kind="Internal").ap()

    def sb(name, shape, dt):
        return nc.alloc_sbuf_tensor(name, shape, dt).ap()

    pv = T(sb("pv", [128, NPV], F32), "pv")
    cst = T(sb("cst_sb", [128, 1536], F32), "cst")
    cbf = T(sb("cbf", [128, 640], BF16), "cbf")
    onesD = T(sb("onesD", [128, 128], BF16), "onesD")
    onesH = T(sb("onesH", [128, 128], BF16), "onesH")
    onesD32 = T(sb("onesD32", [128, 128], F32), "onesD32")
    hbuf = [[T(sb(f"h{b}_{c}", [128, NT], F32), f"h{b}_{c}") for c in range(8)] for b in range(2)]
    ring = [T(sb(f"ring{i}", [128, 4096], BF16), f"ring{i}") for i in range(nring)]
    p16 = UnitPool(nc, 41, BF16, "u16_")
    p32 = UnitPool(nc, 14, F32, "u32_")
    glu_tmp = [T(sb(f"glu{i}", [128, NT + KW - 1], BF16), f"glu{i}") for i in range(8)]
    halo0 = [T(sb(f"halo0_{c}", [128, KW - 1], BF16), f"halo0_{c}") for c in range(8)]
    psum = [T(nc.alloc_psum_tensor(f"ps{i}", [128, NT], F32).ap(), f"ps{i}") for i in range(8)]
    ps_i = [0]

    def next_ps():
        t = psum[ps_i[0] % 8]
        ps_i[0] += 1
        return t

    def pvc(name, j=0):
        o = PV[name] + j
        return pv.ap[:, o:o + 1]

    ident32 = cst.ap[:, 0:128]
    triu32 = cst.ap[:, 128:256]
    ones32 = cst.ap[:, 256:384]
    identbf = cbf.ap[:, 0:128]
    stribf = cbf.ap[:, 384:512]
    maskneg32 = cst.ap[:, 512:640]
    masknegbf = cbf.ap[:, 512:640]

    P.op("sp", lambda e: e.dma_start(out=pv.ap, in_=pvec_d), writes=[pv], dma="pv")
    P.op("sp", lambda e: e.dma_start(out=cst.ap, in_=cst_d), writes=[cst], dma="cst")
    P.op("dve", lambda e: e.tensor_copy(out=cbf.ap, in_=cst.ap[:, 0:640]), reads=[cst], writes=[cbf])
    P.op("dve", lambda e: e.memset(onesD.ap, 1.0 / D), writes=[onesD])
    P.op("dve", lambda e: e.memset(onesH.ap, 1.0), writes=[onesH])
    P.op("dve", lambda e: e.memset(onesD32.ap, 1.0 / D), writes=[onesD32])
    for c in range(8):
        P.op("dve", (lambda c: lambda e: e.memset(halo0[c].ap, 0.0))(c), writes=[halo0[c]])

    cast_op = {}
    diag_built = [False]

    def build_diags():
        for c in range(8):
            stage = ring[c % nring]
            for k in range(KW):
                P.op("dve", (lambda stage, c, k: lambda e: e.tensor_scalar(
                    out=stage.ap[:, k * 128:(k + 1) * 128], in0=ident32, scalar1=pvc("wdw", k * 8 + c), scalar2=None,
                    op0=ALU.mult))(stage, c, k), reads=[cst, pv], writes=[stage])
            P.op("dve", (lambda stage: lambda e: e.memset(stage.ap[:, KW * 128:], 0.0))(stage), writes=[stage])
            i = P.op("sp", (lambda stage, c: lambda e: e.dma_start(out=wdiag[c], in_=stage.ap))(stage, c),
                     reads=[stage], dma=("dg", c % nring))
            cast_op[("diag", c)] = i
    build_diags()
    for s_, (wn, li, k0, m0) in enumerate(slabs):
        if wn == "diag":
            continue
        src = wd[wn][li, k0:k0 + 1024, m0:m0 + 512].rearrange("(kc p) m -> p kc m", p=128)
        dst = wbf[s_].rearrange("p (kc m) -> p kc m", kc=8)
        cast_op[s_] = P.op("pool", (lambda src, dst: lambda e: e.dma_start(out=dst, in_=src))(src, dst),
                           writes=[], dma=("wc", s_ % 4))

    slab_ctr = [0]

    def next_slab(tile_first_use_guard=None):
        g = slab_ctr[0]
        slab_ctr[0] += 1
        s = g % nslab
        slot = ring[g % nring]
        srcap = wdiag[slabs[s][1]] if slabs[s][0] == "diag" else wbf[s]
        i = P.op("sp", (lambda srcap, slot: lambda e: e.dma_start(out=slot.ap, in_=srcap))(srcap, slot),
                 writes=[slot], dma=("ring", g % nring))
        if g < nslab:
            key = ("diag", slabs[s][1]) if slabs[s][0] == "diag" else s
            P.ops[i][2].add(cast_op[key])
        return slot, slot.ap.rearrange("p (kc m) -> p kc m", kc=8)


    def mm(ps, ps_ap, lt, lt_ap, rt, rt_ap, start, stop):
        P.op("pe", lambda e: e.matmul(ps_ap, lt_ap, rt_ap, start=start, stop=stop),
             reads=[lt, rt], writes=[ps])

    def act(out_t, out_ap, in_t, in_ap, func, bias=None, scale=None, extra_reads=()):
        kw = {}
        if bias is not None:
            kw["bias"] = bias
        if scale is not None:
            kw["scale"] = scale
        P.op("act", lambda e: e.activation(out_ap, in_ap, func, **kw),
             reads=[in_t, *extra_reads], writes=[out_t])

    def rstd_from(ps_stat, eps=EPS):
        t = p32.get()
        act(t, t.ap, ps_stat, ps_stat.ap, AF.Ln, bias=eps_ap(eps), extra_reads=[epsT])
        act(t, t.ap, t, t.ap, AF.Exp, scale=-0.5)
        return t

    epsT = T(sb("epsc", [128, 2], F32), "epsc")
    P.op("dve", lambda e: e.memset(epsT.ap[:, 0:1], EPS), writes=[epsT])
    P.op("dve", lambda e: e.memset(epsT.ap[:, 1:2], 1.0), writes=[epsT])

    def eps_ap(eps):
        return epsT.ap[:, 0:1]

    def rmsnorm(h, gname):
        ps = next_ps()
        for c in range(8):
            sq = p16.get()
            act(sq, sq.ap, h[c], h[c].ap, AF.Square)
            mm(ps, ps.ap, onesD, onesD.ap, sq, sq.ap, c == 0, c == 7)
            p16.put(sq)
        r = rstd_from(ps)
        out = []
        for c in range(8):
            o = p16.get()
            P.op("dve", (lambda o, c: lambda e: e.scalar_tensor_tensor(
                out=o.ap, in0=h[c].ap, scalar=pvc(gname, c), in1=r.ap, op0=ALU.mult, op1=ALU.mult))(o, c),
                reads=[h[c], pv, r], writes=[o])
            out.append(o)
        p32.put(r)
        return out

    def conformer(h, hn, first_of_seq):
        acc = [None] * 8
        for half in range(2):
            wa_t, wa = next_slab()
            wg_t, wg = next_slab()
            pga = [next_ps() for _ in range(4)]
            for kc in range(8):
                for j in range(4):
                    mm(pga[j], pga[j].ap, wa_t, wa[:, kc, j * 128:(j + 1) * 128], hn[kc], hn[kc].ap, kc == 0, kc == 7)
            pgg = [next_ps() for _ in range(4)]
            for kc in range(8):
                for j in range(4):
                    mm(pgg[j], pgg[j].ap, wg_t, wg[:, kc, j * 128:(j + 1) * 128], hn[kc], hn[kc].ap, kc == 0, kc == 7)
            for j in range(4):
                c = half * 4 + j
                psa = pga[j]
                psg = pgg[j]
                sg = p32.get()
                act(sg, sg.ap, psg, psg.ap, AF.Sigmoid, bias=pvc("bpw1", 8 + c), extra_reads=[pv])
                gt = glu_tmp[c]
                P.op("dve", (lambda gt, psa, sg, c: lambda e: e.scalar_tensor_tensor(
                    out=gt.ap[:, KW - 1:], in0=psa.ap, scalar=pvc("bpw1", c), in1=sg.ap,
                    op0=ALU.add, op1=ALU.mult))(gt, psa, sg, c),
                    reads=[psa, sg, pv], writes=[gt])
                p32.put(sg)
                if first_of_seq:
                    P.op("pool", (lambda gt: lambda e: e.memset(gt.ap[:, 0:KW - 1], 0.0))(gt), writes=[gt])
                else:
                    P.op("pool", (lambda gt, c: lambda e: e.tensor_copy(out=gt.ap[:, 0:KW - 1], in_=halo0[c].ap))(gt, c),
                         reads=[halo0[c]], writes=[gt])
                P.op("pool", (lambda gt, c: lambda e: e.tensor_copy(out=halo0[c].ap, in_=gt.ap[:, NT:NT + KW - 1]))(gt, c),
                     reads=[gt], writes=[halo0[c]])
            for j in range(4):
                c = half * 4 + j
                d_t, _ = next_slab()
                gt = glu_tmp[c]
                psc = next_ps()
                for k in range(KW):
                    mm(psc, psc.ap, d_t, d_t.ap[:, k * 128:(k + 1) * 128], gt, gt.ap[:, k:k + NT], k == 0, k == KW - 1)
                a = p32.get()
                act(a, a.ap, psc, psc.ap, AF.Identity, bias=pvc("bdw", c), extra_reads=[pv])
                acc[c] = a
        for t in hn:
            p16.put(t)
        psm = next_ps()
        for c in range(8):
            mm(psm, psm.ap, onesD32, onesD32.ap, acc[c], acc[c].ap, c == 0, c == 7)
        for c in range(8):
            P.op("dve", (lambda c: lambda e: e.tensor_tensor(
                out=acc[c].ap, in0=acc[c].ap, in1=psm.ap, op=ALU.subtract))(c),
                reads=[acc[c], psm], writes=[acc[c]])
        psv = next_ps()
        for c in range(8):
            sq = p16.get()
            act(sq, sq.ap, acc[c], acc[c].ap, AF.Square)
            mm(psv, psv.ap, onesD, onesD.ap, sq, sq.ap, c == 0, c == 7)
            p16.put(sq)
        r = rstd_from(psv)
        ua = []
        for c in range(8):
            P.op("dve", (lambda c: lambda e: e.tensor_tensor(
                out=acc[c].ap, in0=acc[c].ap, in1=r.ap, op=ALU.mult))(c),
                reads=[acc[c], r], writes=[acc[c]])
            o = p16.get()
            act(o, o.ap, acc[c], acc[c].ap, AF.Silu, bias=pvc("lnb", c), scale=pvc("lng", c), extra_reads=[pv])
            ua.append(o)
            p32.put(acc[c])
        p32.put(r)
        for half in range(2):
            w_t, w = next_slab()
            pgrp = [next_ps() for _ in range(4)]
            for kc in range(8):
                for j in range(4):
                    mm(pgrp[j], pgrp[j].ap, w_t, w[:, kc, j * 128:(j + 1) * 128], ua[kc], ua[kc].ap, kc == 0, kc == 7)
            for j in range(4):
                mc = half * 4 + j
                ps = pgrp[j]
                P.op("dve", (lambda ps, mc: lambda e: e.scalar_tensor_tensor(
                    out=h[mc].ap, in0=ps.ap, scalar=pvc("bpw2", mc), in1=h[mc].ap,
                    op0=ALU.add, op1=ALU.add))(ps, mc), reads=[ps, pv, h[mc]], writes=[h[mc]])
        for t in ua:
            p16.put(t)

    def mlp(h, hn):
        hid = []
        for g in range(8):
            w_t, w = next_slab()
            pgrp = [next_ps() for _ in range(4)]
            for kc in range(8):
                for j in range(4):
                    mm(pgrp[j], pgrp[j].ap, w_t, w[:, kc, j * 128:(j + 1) * 128], hn[kc], hn[kc].ap, kc == 0, kc == 7)
            for j in range(4):
                ps = pgrp[j]
                sq = p32.get()
                act(sq, sq.ap, ps, ps.ap, AF.Square)
                o = p16.get()
                P.op("dve", (lambda o, ps, sq: lambda e: e.scalar_tensor_tensor(
                    out=o.ap, in0=ps.ap, scalar=0.0, in1=sq.ap, op0=ALU.is_gt, op1=ALU.mult))(o, ps, sq),
                    reads=[ps, sq], writes=[o])
                p32.put(sq)
                hid.append(o)
        for t in hn:
            p16.put(t)
        for mg in range(2):
            pss = [next_ps() for _ in range(4)]
            for kg in range(4):
                w_t, w = next_slab()
                for j in range(4):
                    for kc in range(8):
                        mm(pss[j], pss[j].ap, w_t, w[:, kc, j * 128:(j + 1) * 128],
                           hid[kg * 8 + kc], hid[kg * 8 + kc].ap, kg == 0 and kc == 0, kg == 3 and kc == 7)
            for j in range(4):
                mc = mg * 4 + j
                P.op("dve", (lambda ps, mc: lambda e: e.tensor_tensor(
                    out=h[mc].ap, in0=h[mc].ap, in1=ps.ap, op=ALU.add))(pss[j], mc),
                    reads=[pss[j], h[mc]], writes=[h[mc]])
        for t in hid:
            p16.put(t)


    if depth > 1:
        wab32 = T(sb("wab32", [128, 8, 16], F32), "wab32")
        wab = T(sb("wab", [128, 8, 16], BF16), "wab")
        negA4 = T(sb("negA4", [128, 32], F32), "negA4")
        onesE = T(sb("onesE", [128, 128], BF16), "onesE")
        halo1 = T(sb("halo1", [128, 72], F32), "halo1")
        S32 = [T(sb(f"S32_{i}", [128, 512], F32), f"S32_{i}") for i in range(2)]
        Sbf = [T(sb(f"Sbf_{i}", [128, 512], BF16), f"Sbf_{i}") for i in range(2)]
        pre_tmp = [T(sb(f"pre{i}", [128, NT + 3], F32), f"pre{i}") for i in range(4)]
        GE = T(sb("GE", [128, 1024], F32), "GE")

        def b16(name):
            return T(sb(name, [128, 1024], BF16), name)
        Yk, kd, vtok, qkT, qgT, dec, egcR = [b16(n) for n in ("Yk", "kd", "vtok", "qkT", "qgT", "dec", "egcR")]
        decs = T(sb("decs", [128, 1024], F32), "decs")

        def pair(name, dt):
            return [T(sb(f"{name}_{i}", [128, 512], dt), f"{name}_{i}") for i in range(2)]
        CDT = BF16
        TW = pair("TW", CDT)
        P0 = pair("P0", CDT)
        Am = pair("Am", CDT)
        Xm = pair("Xm", CDT)
        MT = Am
        identR = T(sb("identR", [128, 128], CDT), "identR")
        P.op("dve", lambda e: e.tensor_copy(out=identR.ap, in_=ident32), reads=[cst], writes=[identR])
        nZw = pair("nZw", BF16)
        vnew = pair("vnew", BF16)
        gatesT = T(sb("gates", [128, 12 * 32], F32), "gates")

        class NS:
            pass
        BS = [NS(), NS()]
        BS[0].Yk, BS[0].kd, BS[0].vtok, BS[0].qkT, BS[0].qgT = Yk, kd, vtok, qkT, qgT
        BS[0].TW, BS[0].P0, BS[0].Am, BS[0].Xm, BS[0].nZw, BS[0].vnew = TW, P0, Am, Xm, nZw, vnew
        BS[1].Yk = TV(pre_tmp[0], pre_tmp[0].ap.bitcast(BF16)[:, 0:1024])
        BS[1].kd = TV(pre_tmp[1], pre_tmp[1].ap.bitcast(BF16)[:, 0:1024])
        BS[1].vtok = TV(pre_tmp[2], pre_tmp[2].ap.bitcast(BF16)[:, 0:1024])
        BS[1].qkT = b16("qkT1")
        BS[1].qgT = b16("qgT1")
        BS[1].TW = pair("TW1", CDT)
        BS[1].P0 = pair("P01", CDT)
        BS[1].Am = pair("Am1", CDT)
        BS[1].Xm = pair("Xm1", CDT)
        BS[1].nZw = [TV(glu_tmp[i], glu_tmp[i].ap[:, 0:512]) for i in range(2)]
        BS[1].vnew = [TV(glu_tmp[2 + i], glu_tmp[2 + i].ap[:, 0:512]) for i in range(2)]
        src = wd["gdn_w_in"][0, :, 4096:4112].rearrange("(kc p) m -> p kc m", p=128)
        P.op("sp", lambda e: e.dma_start(out=wab32.ap, in_=src), writes=[wab32], dma="wab")
        P.op("dve", lambda e: e.tensor_copy(out=wab.ap, in_=wab32.ap), reads=[wab32], writes=[wab])
        P.op("dve", lambda e: e.memset(onesE.ap, 1.0 / 128), writes=[onesE])
        P.op("act", lambda e: e.activation(negA4.ap, pv.ap[:, PV["alog4"]:PV["alog4"] + 32], AF.Exp),
             reads=[pv], writes=[negA4])
        P.op("dve", lambda e: e.tensor_scalar(out=negA4.ap, in0=negA4.ap, scalar1=-1.0, scalar2=None, op0=ALU.mult),
             reads=[negA4], writes=[negA4])

    def v3(ap, n):
        return ap.rearrange("p (h i) -> p h i", h=n)

    def q4(t, h4):
        return t.ap[:, h4 * 128:(h4 + 1) * 128]

    def q4r(t, h4):
        return t.ap[:, h4 * 128:(h4 + 1) * 128]

    def f32(ap):
        return ap.bitcast(F32) if CHAIN_R[0] else ap

    def hs(t, h):
        return t.ap[:, h * 128:(h + 1) * 128]

    def gdn(h, hn, first):
        if first:
            for i in range(2):
                P.op("pool", lambda e, i=i: e.memset(S32[i].ap, 0.0), writes=[S32[i]])
                P.op("pool", lambda e, i=i: e.memset(Sbf[i].ap, 0.0), writes=[Sbf[i]])
        one_ap = epsT.ap[:, 1:2]
        gs = gatesT

        def G(i):
            return gs.ap[:, i * 32:(i + 1) * 32]

        def G3(i):
            return G(i).rearrange("p (b x) -> p b x", x=8)

        def gop(eng, fn, extra_r=(), extra_w=()):
            P.op(eng, fn, reads=[gs, *extra_r], writes=[gs, *extra_w])
        pab = next_ps()
        for b in range(4):
            for kc in range(8):
                mm(pab, pab.ap[:, b * 16:(b + 1) * 16], hn[kc], hn[kc].ap[:, b * 128:(b + 1) * 128],
                   wab, wab.ap[:, kc, :], kc == 0, kc == 7)
        pab3 = pab.ap[:, 0:64].rearrange("p (b x) -> p b x", x=16)
        dtb3 = pv.ap[:, PV["dtb4"]:PV["dtb4"] + 32].rearrange("p (b x) -> p b x", x=8)
        gop("dve", lambda e, pab3=pab3: e.tensor_tensor(out=G3(0), in0=pab3[:, :, 0:8], in1=dtb3, op=ALU.add),
            extra_r=[pab, pv])
        gop("act", lambda e: e.activation(G(1), G(0), AF.Exp))
        gop("act", lambda e: e.activation(G(1), G(1), AF.Ln, bias=one_ap), extra_r=[epsT])
        gop("dve", lambda e: e.tensor_tensor(out=G(2), in0=G(1), in1=negA4.ap, op=ALU.mult), extra_r=[negA4])
        gop("act", lambda e, pab3=pab3: e.activation(G3(3), pab3[:, :, 8:16], AF.Exp, scale=-1.0), extra_r=[pab])
        gop("dve", lambda e: e.tensor_scalar(out=G(3), in0=G(3), scalar1=1.0, scalar2=None, op0=ALU.add))
        gop("dve", lambda e: e.reciprocal(out=G(4), in_=G(3)))
        gop("dve", lambda e: e.tensor_scalar(out=G(5), in0=G(4), scalar1=-1.0, scalar2=None, op0=ALU.mult))
        pcs = next_ps()
        mm(pcs, pcs.ap[:, 0:32], cst, triu32, gs, G(2), True, True)
        mm(pcs, pcs.ap[:, 32:64], cst, ones32, gs, G(2), True, True)
        gop("act", lambda e, pcs=pcs: e.activation(G(6), pcs.ap[:, 0:32], AF.Copy), extra_r=[pcs])
        gop("act", lambda e, pcs=pcs: e.activation(G(7), pcs.ap[:, 0:32], AF.Exp), extra_r=[pcs])
        gop("dve", lambda e: e.tensor_scalar(out=G(10), in0=G(6), scalar1=-1.0, scalar2=None, op0=ALU.mult))
        gop("dve", lambda e, pcs=pcs: e.tensor_tensor(out=G(8), in0=pcs.ap[:, 32:64], in1=G(6), op=ALU.subtract),
            extra_r=[pcs])
        gop("act", lambda e: e.activation(G(8), G(8), AF.Exp))
        gop("act", lambda e, pcs=pcs: e.activation(G(9), pcs.ap[:, 32:64], AF.Exp), extra_r=[pcs])

        qT = [None] * 8
        kT = [None] * 8
        vT = [None] * 8
        sz = [None] * 8
        grp = {}

        def S1(g):
            w_t, w = next_slab()
            st = grp[g] = {"a": [None] * 4}
            pgrp = [next_ps() for _ in range(4)]
            for kc in range(8):
                for j in range(4):
                    mm(pgrp[j], pgrp[j].ap, w_t, w[:, kc, j * 128:(j + 1) * 128], hn[kc], hn[kc].ap, kc == 0, kc == 7)
            for j in range(4):
                cc = g * 4 + j
                ps = pgrp[j]
                if cc >= 24:
                    o = p16.get()
                    act(o, o.ap, ps, ps.ap, AF.Silu)
                    sz[cc - 24] = o
                    continue
                pre = pre_tmp[cc % 4]
                act(pre, pre.ap[:, 3:], ps, ps.ap, AF.Copy)
                if first:
                    P.op("pool", (lambda pre: lambda e: e.memset(pre.ap[:, 0:3], 0.0))(pre), writes=[pre])
                else:
                    P.op("pool", (lambda pre, cc: lambda e: e.tensor_copy(
                        out=pre.ap[:, 0:3], in_=halo1.ap[:, cc * 3:cc * 3 + 3]))(pre, cc),
                        reads=[halo1], writes=[pre])
                P.op("pool", (lambda pre, cc: lambda e: e.tensor_copy(
                    out=halo1.ap[:, cc * 3:cc * 3 + 3], in_=pre.ap[:, NT:NT + 3]))(pre, cc),
                    reads=[pre], writes=[halo1])
                a = p32.get()
                P.op("dve", (lambda a, pre, cc: lambda e: e.tensor_scalar(
                    out=a.ap, in0=pre.ap[:, 0:NT], scalar1=pvc("gcw", cc), scalar2=None, op0=ALU.mult))(a, pre, cc),
                    reads=[pre, pv], writes=[a])
                for k in range(1, 4):
                    P.op("dve", (lambda a, pre, cc, k: lambda e: e.scalar_tensor_tensor(
                        out=a.ap, in0=pre.ap[:, k:k + NT], scalar=pvc("gcw", k * 24 + cc), in1=a.ap,
                        op0=ALU.mult, op1=ALU.add))(a, pre, cc, k), reads=[pre, pv, a], writes=[a])
                st["a"][j] = a

        def S2(g):
            if g >= 6:
                return
            for j in range(4):
                cc = g * 4 + j
                a = grp[g]["a"][j]
                if cc >= 16:
                    o = p16.get()
                    act(o, o.ap, a, a.ap, AF.Silu)
                    vT[cc - 16] = o
                    p32.put(a)
                else:
                    act(a, a.ap, a, a.ap, AF.Silu)

        def S3(g):
            if g >= 4:
                return
            pss = []
            sqs = []
            for j in range(4):
                a = grp[g]["a"][j]
                sq = p16.get()
                act(sq, sq.ap, a, a.ap, AF.Square)
                ps2 = next_ps()
                mm(ps2, ps2.ap, onesH, onesH.ap, sq, sq.ap, True, True)
                pss.append(ps2)
                sqs.append(sq)
            for sq in sqs:
                p16.put(sq)
            rs = []
            for j in range(4):
                t = p32.get()
                act(t, t.ap, pss[j], pss[j].ap, AF.Ln, bias=eps_ap(EPS), extra_reads=[epsT])
                rs.append(t)
            for j in range(4):
                act(rs[j], rs[j].ap, rs[j], rs[j].ap, AF.Exp, scale=-0.5)
            for j in range(4):
                cc = g * 4 + j
                a = grp[g]["a"][j]
                r = rs[j]
                o = p16.get()
                if cc < 8:
                    P.op("dve", (lambda o, a, r: lambda e: e.scalar_tensor_tensor(
                        out=o.ap, in0=a.ap, scalar=float(128 ** -0.5), in1=r.ap, op0=ALU.mult, op1=ALU.mult))(o, a, r),
                        reads=[a, r], writes=[o])
                    qT[cc] = o
                else:
                    P.op("dve", (lambda o, a, r: lambda e: e.tensor_tensor(
                        out=o.ap, in0=a.ap, in1=r.ap, op=ALU.mult))(o, a, r), reads=[a, r], writes=[o])
                    kT[cc - 8] = o
                p32.put(a)
                p32.put(r)

        for step in range(9):
            if step < 8:
                S1(step)
            if step >= 1:
                S2(step - 1)
                S3(step - 1)

        oT = [p32.get() for _ in range(8)]
        triu_b8 = triu32.unsqueeze(1).broadcast_to([128, 8, 128])
        stri_b8 = cst.ap[:, 384:512].unsqueeze(1).broadcast_to([128, 8, 128])
        ident_b4 = ident32.unsqueeze(1).broadcast_to([128, 4, 128])
        def lmask(l):
            return cst.ap[:, 640 + l * 128:640 + (l + 1) * 128].unsqueeze(1).broadcast_to([128, 4, 128])

        def prep(b, B):
            tb = slice(b * 128, (b + 1) * 128)

            def gb(i):
                return gs.ap[:, i * 32 + b * 8:i * 32 + b * 8 + 8]

            def g1(i, hh):
                return gs.ap[:, i * 32 + b * 8 + hh:i * 32 + b * 8 + hh + 1]
            P.op("dve", lambda e: e.tensor_copy(
                out=v3(GE.ap, 8), in_=gb(2).unsqueeze(2).broadcast_to([128, 8, 128])),
                reads=[gs], writes=[GE])
            pR = [next_ps(), next_ps()]
            pRm = [next_ps(), next_ps()]
            for hh in range(8):
                mm(pR[hh // 4], q4(pR[hh // 4], hh % 4), GE, hs(GE, hh), cst, triu32, True, True)
                mm(pRm[hh // 4], q4(pRm[hh // 4], hh % 4), GE, hs(GE, hh), cst, triu32, True, False)
                mm(pRm[hh // 4], q4(pRm[hh // 4], hh % 4), cbf, identbf, cbf, masknegbf, False, True)
            ptr = next_ps()
            ptr_bf = ptr.ap.bitcast(BF16)
            for hh in range(8):
                P.op("pe", lambda e, hh=hh: e.transpose(
                    out=ptr_bf[:, hh * 128:(hh + 1) * 128], in_=kT[hh].ap[:, tb], identity=identbf),
                    reads=[kT[hh], cbf], writes=[ptr])
            ptv = next_ps()
            ptv_bf = ptv.ap.bitcast(BF16)
            for hh in range(8):
                P.op("pe", lambda e, hh=hh: e.transpose(
                    out=ptv_bf[:, hh * 128:(hh + 1) * 128], in_=vT[hh].ap[:, tb], identity=identbf),
                    reads=[vT[hh], cbf], writes=[ptv])
            for hg in range(2):
                act(egcR, egcR.ap[:, hg * 512:(hg + 1) * 512], pR[hg], pR[hg].ap, AF.Exp)
            for hh in range(8):
                act(dec, hs(dec, hh), pRm[hh // 4], q4(pRm[hh // 4], hh % 4), AF.Exp,
                    bias=g1(10, hh), extra_reads=[gs])
            P.op("dve", lambda e: e.tensor_tensor(
                out=v3(decs.ap, 8), in0=v3(dec.ap, 8), in1=stri_b8, op=ALU.mult),
                reads=[dec, cst], writes=[decs])
            P.op("dve", lambda e: e.tensor_tensor(
                out=v3(decs.ap, 8), in0=v3(decs.ap, 8), in1=gb(5).unsqueeze(2).broadcast_to([128, 8, 128]), op=ALU.mult),
                reads=[decs, gs], writes=[decs])
            P.op("dve", lambda e: e.tensor_tensor(
                out=v3(B.Yk.ap, 8), in0=v3(ptr_bf, 8), in1=gb(7).unsqueeze(2).broadcast_to([128, 8, 128]), op=ALU.mult),
                reads=[ptr, gs], writes=[B.Yk])
            P.op("dve", lambda e: e.tensor_tensor(
                out=v3(B.kd.ap, 8), in0=v3(ptr_bf, 8), in1=gb(8).unsqueeze(2).broadcast_to([128, 8, 128]), op=ALU.mult),
                reads=[ptr, gs], writes=[B.kd])
            act(B.vtok, B.vtok.ap, ptv, ptv_bf, AF.Copy)
            for hg in range(2):
                pkk = next_ps()
                pqk = next_ps()
                for h4 in range(4):
                    hh = hg * 4 + h4
                    mm(pkk, q4(pkk, h4), kT[hh], kT[hh].ap[:, tb], kT[hh], kT[hh].ap[:, tb], True, True)
                    mm(pqk, q4(pqk, h4), kT[hh], kT[hh].ap[:, tb], qT[hh], qT[hh].ap[:, tb], True, True)
                P.op("dve", lambda e, hg=hg, pkk=pkk: e.tensor_tensor(
                    out=B.TW[hg].ap, in0=pkk.ap, in1=decs.ap[:, hg * 512:(hg + 1) * 512], op=ALU.mult),
                    reads=[pkk, decs], writes=[B.TW[hg]])
                P.op("dve", lambda e, hg=hg, pqk=pqk: e.tensor_tensor(
                    out=B.qkT.ap[:, hg * 512:(hg + 1) * 512], in0=pqk.ap, in1=dec.ap[:, hg * 512:(hg + 1) * 512], op=ALU.mult),
                    reads=[pqk, dec], writes=[B.qkT])
            for hh in range(8):
                P.op("pool", lambda e, hh=hh: e.tensor_tensor(
                    out=hs(B.qgT, hh), in0=qT[hh].ap[:, tb], in1=hs(egcR, hh), op=ALU.mult),
                    reads=[qT[hh], egcR], writes=[B.qgT])
            for hg in range(2):
                pp = next_ps()
                pp_b = pp.ap.bitcast(BF16)
                for h4 in range(4):
                    P.op("pe", lambda e, h4=h4, hg=hg, pp_b=pp_b: e.transpose(
                        out=pp_b[:, h4 * 128:(h4 + 1) * 128], in_=q4(B.TW[hg], h4), identity=identbf),
                        reads=[B.TW[hg], cbf], writes=[pp])
                act(B.P0[hg], B.P0[hg].ap, pp, pp_b[:, 0:512], AF.Copy)
                P.op("dve", lambda e, hg=hg: e.tensor_tensor(
                    out=v3(B.Am[hg].ap, 4), in0=v3(B.TW[hg].ap, 4), in1=lmask(0), op=ALU.mult),
                    reads=[B.TW[hg], cst], writes=[B.Am[hg]])
                P.op("dve", lambda e, hg=hg: e.tensor_tensor(
                    out=v3(B.Am[hg].ap, 4), in0=v3(B.Am[hg].ap, 4), in1=ident_b4, op=ALU.add),
                    reads=[B.Am[hg], cst], writes=[B.Am[hg]])
            for hg in range(2):
                px = next_ps()
                px_b = px.ap.bitcast(BF16)
                for h4 in range(4):
                    P.op("pe", lambda e, h4=h4, hg=hg, px_b=px_b: e.transpose(
                        out=px_b[:, h4 * 128:(h4 + 1) * 128], in_=q4(B.Am[hg], h4), identity=identbf),
                        reads=[B.Am[hg], cbf], writes=[px])
                act(B.Xm[hg], B.Xm[hg].ap, px, px_b[:, 0:512], AF.Copy)

        def levelW(B, l):
            for hg in range(2):
                pW = next_ps()
                for h4 in range(4):
                    mm(pW, q4(pW, h4), B.P0[hg], q4(B.P0[hg], h4), B.Am[hg], q4(B.Am[hg], h4), True, True)
                P.op("dve", lambda e, hg=hg, pW=pW: e.tensor_tensor(
                    out=v3(B.TW[hg].ap, 4), in0=v3(pW.ap, 4), in1=lmask(l), op=ALU.mult),
                    reads=[pW, cst], writes=[B.TW[hg]])

        def levelAX(B, l):
            for hg in range(2):
                pA = next_ps()
                for h4 in range(4):
                    mm(pA, q4(pA, h4), B.Xm[hg], q4(B.Xm[hg], h4), B.TW[hg], q4(B.TW[hg], h4), True, False)
                    mm(pA, q4(pA, h4), cbf, identbf, B.Am[hg], q4(B.Am[hg], h4), False, True)
                if l < 6:
                    pX = next_ps()
                    for h4 in range(4):
                        mm(pX, q4(pX, h4), B.TW[hg], q4(B.TW[hg], h4), B.Xm[hg], q4(B.Xm[hg], h4), True, False)
                        mm(pX, q4(pX, h4), cbf, identbf, B.Xm[hg], q4(B.Xm[hg], h4), False, True)
                act(B.Am[hg], B.Am[hg].ap, pA, pA.ap, AF.Copy)
                if l < 6:
                    act(B.Xm[hg], B.Xm[hg].ap, pX, pX.ap, AF.Copy)

        def post(B):
            for hg in range(2):
                pz = next_ps()
                for h4 in range(4):
                    hh = hg * 4 + h4
                    mm(pz, q4(pz, h4), B.Yk, hs(B.Yk, hh), B.Am[hg], q4(B.Am[hg], h4), True, True)
                act(B.nZw[hg], B.nZw[hg].ap, pz, pz.ap, AF.Copy, scale=-1.0)

        def scan(b, B):
            tb = slice(b * 128, (b + 1) * 128)

            def gb(i):
                return gs.ap[:, i * 32 + b * 8:i * 32 + b * 8 + 8]

            def g1(i, hh):
                return gs.ap[:, i * 32 + b * 8 + hh:i * 32 + b * 8 + hh + 1]
            for hg in range(2):
                pvn = next_ps()
                for h4 in range(4):
                    hh = hg * 4 + h4
                    mm(pvn, q4(pvn, h4), B.Am[hg], q4(B.Am[hg], h4), B.vtok, hs(B.vtok, hh), True, False)
                    mm(pvn, q4(pvn, h4), B.nZw[hg], q4(B.nZw[hg], h4), Sbf[hg], q4(Sbf[hg], h4), False, True)
                P.op("dve", lambda e, hg=hg, pvn=pvn: e.tensor_tensor(
                    out=v3(B.vnew[hg].ap, 4), in0=v3(pvn.ap, 4),
                    in1=gb(4)[:, hg * 4:hg * 4 + 4].unsqueeze(2).broadcast_to([128, 4, 128]), op=ALU.mult),
                    reads=[pvn, gs], writes=[B.vnew[hg]])
            for hg in range(2):
                po = next_ps()
                for h4 in range(4):
                    hh = hg * 4 + h4
                    mm(po, q4(po, h4), Sbf[hg], q4(Sbf[hg], h4), B.qgT, hs(B.qgT, hh), True, False)
                    mm(po, q4(po, h4), B.vnew[hg], q4(B.vnew[hg], h4), B.qkT, hs(B.qkT, hh), False, True)
                for h4 in range(4):
                    hh = hg * 4 + h4
                    act(oT[hh], oT[hh].ap[:, tb], po, q4(po, h4), AF.Copy)
            for hg in range(2):
                pS = next_ps()
                for h4 in range(4):
                    hh = hg * 4 + h4
                    mm(pS, q4(pS, h4), B.kd, hs(B.kd, hh), B.vnew[hg], q4(B.vnew[hg], h4), True, True)
                for h4 in range(4):
                    hh = hg * 4 + h4
                    P.op("dve", lambda e, hh=hh, h4=h4, hg=hg, pS=pS: e.scalar_tensor_tensor(
                        out=q4(S32[hg], h4), in0=q4(S32[hg], h4), scalar=g1(9, hh), in1=q4(pS, h4),
                        op0=ALU.mult, op1=ALU.add), reads=[S32[hg], gs, pS], writes=[S32[hg]])
                act(Sbf[hg], Sbf[hg].ap, S32[hg], S32[hg].ap, AF.Copy)

        for pr in range(2):
            b0, b1 = 2 * pr, 2 * pr + 1
            prep(b0, BS[0])
            prep(b1, BS[1])
            for l in range(1, 7):
                levelW(BS[0], l)
                levelW(BS[1], l)
                levelAX(BS[0], l)
                levelAX(BS[1], l)
            post(BS[0])
            post(BS[1])
            scan(b0, BS[0])
            scan(b1, BS[1])
        for t in hn + qT + kT + vT:
            p16.put(t)
        onT = []
        for hh in range(8):
            sq = p16.get()
            act(sq, sq.ap, oT[hh], oT[hh].ap, AF.Square)
            ps = next_ps()
            mm(ps, ps.ap, onesE, onesE.ap, sq, sq.ap, True, True)
            p16.put(sq)
            r = rstd_from(ps)
            P.op("dve", lambda e, hh=hh, r=r: e.scalar_tensor_tensor(
                out=oT[hh].ap, in0=oT[hh].ap, scalar=pvc("gng"), in1=r.ap, op0=ALU.mult, op1=ALU.mult),
                reads=[oT[hh], pv, r], writes=[oT[hh]])
            o = p16.get()
            P.op("dve", lambda e, hh=hh, o=o: e.tensor_tensor(
                out=o.ap, in0=oT[hh].ap, in1=sz[hh].ap, op=ALU.mult), reads=[oT[hh], sz[hh]], writes=[o])
            onT.append(o)
            p32.put(r)
            p32.put(oT[hh])
            p16.put(sz[hh])
        for half in range(2):
            w_t, w = next_slab()
            pgrp = [next_ps() for _ in range(4)]
            for kc in range(8):
                for j in range(4):
                    mm(pgrp[j], pgrp[j].ap, w_t, w[:, kc, j * 128:(j + 1) * 128], onT[kc], onT[kc].ap, kc == 0, kc == 7)
            for j in range(4):
                mc = half * 4 + j
                ps = pgrp[j]
                P.op("dve", lambda e, ps=ps, mc=mc: e.tensor_tensor(
                    out=h[mc].ap, in0=h[mc].ap, in1=ps.ap, op=ALU.add), reads=[ps, h[mc]], writes=[h[mc]])
        for t in onT:
            p16.put(t)

    def load_x(t):
        hb = hbuf[t % 2]
        for c in range(8):
            P.op("sp", (lambda c, hb, t: lambda e: e.dma_start(
                out=hb[c].ap, in_=xT[c * 128:(c + 1) * 128, t * NT:(t + 1) * NT]))(c, hb, t),
                writes=[hb[c]], dma=("x", t % 2, c))

    store_ops = []

    def tile_body(t):
        h = hbuf[t % 2]
        first = (t % tiles_per_seq) == 0
        if t + 1 < ntiles:
            load_x(t + 1)
        for li in range(depth):
            hn = rmsnorm(h, f"nmg{li}")
            if li % 2 == 0:
                conformer(h, hn, first)
            else:
                gdn(h, hn, first)
            hn = rmsnorm(h, f"nfg{li}")
            mlp(h, hn)
        ps = next_ps()
        for c in range(8):
            sq = p16.get()
            act(sq, sq.ap, h[c], h[c].ap, AF.Square)
            mm(ps, ps.ap, onesD, onesD.ap, sq, sq.ap, c == 0, c == 7)
            p16.put(sq)
        r = rstd_from(ps)
        for c in range(8):
            o = p32.get()
            P.op("dve", (lambda o, c: lambda e: e.scalar_tensor_tensor(
                out=o.ap, in0=h[c].ap, scalar=pvc("fng", c), in1=r.ap, op0=ALU.mult, op1=ALU.mult))(o, c),
                reads=[h[c], pv, r], writes=[o])
            i = P.op("pool", (lambda o, c: lambda e: e.dma_start(
                out=outT[c * 128:(c + 1) * 128, t * NT:(t + 1) * NT], in_=o.ap))(o, c),
                reads=[o], dma=("st", o.name))
            store_ops.append(i)
            p32.put(o)
        p32.put(r)

    load_x(0)
    for t in range(ntiles):
        tile_body(t)
    fin = P.op("pool", lambda e: e.nop(), reads=[], writes=[])
    P.ops[fin][2].update(store_ops)
    P.emit(nc)
    return nc, P


_CACHE = {}


def _run(inputs, nseq, S, depth, ncores):
    key = (nseq, S, depth)
    if key not in _CACHE:
        _CACHE[key] = build_nc(nseq, S, depth)
    nc, P = _CACHE[key]
    x = np.asarray(inputs["x"], np.float32)
    B = x.shape[0]
    assert B == nseq * ncores
    pvec = pack_pvec(inputs)
    cst = make_consts()
    in_maps = []
    for i in range(ncores):
        xs = x[i * nseq:(i + 1) * nseq].reshape(nseq * S, D)
        m = {"xT": np.ascontiguousarray(xs.T), "pvec": pvec, "cst": cst,
             "cv_w_pw1": np.asarray(inputs["cv_w_pw1"], np.float32),
             "cv_w_pw2": np.asarray(inputs["cv_w_pw2"], np.float32),
             "mlp_w1": np.asarray(inputs["mlp_w1"], np.float32),
             "mlp_w2": np.asarray(inputs["mlp_w2"], np.float32)}
        if depth > 1:
            m["gdn_w_in"] = np.asarray(inputs["gdn_w_in"], np.float32)
            m["gdn_w_out"] = np.asarray(inputs["gdn_w_out"], np.float32)
        in_maps.append(m)
    res = run_bass_kernel_spmd(nc, in_maps, core_ids=list(range(ncores)))
    outs = [np.asarray(r["outT"]).T.reshape(nseq, S, D) for r in res.results]
    return np.ascontiguousarray(np.concatenate(outs, axis=0).astype(np.float32))


def kernel(**inputs):
    return _run(inputs, 2, 4096, 2, 8)
```

```python
import numpy as np
import concourse.bass as bass
import concourse.mybir as mybir
from concourse.bass_utils import run_bass_kernel_spmd

F32 = mybir.dt.float32
F32R = mybir.dt.float32r
CHAIN_R = [False]
BF16 = mybir.dt.bfloat16
AF = mybir.ActivationFunctionType
ALU = mybir.AluOpType

D = 1024
NC8 = 8
NT = 512
EPS = 1e-6
KW = 31
H = 8
QKV = 3072
DEBUG_STOP = [99]
GIN = 4112

COMPUTE = ("pe", "act", "dve", "pool")


class T:
    __slots__ = ("ap", "w", "r", "name")

    def __init__(self, ap, name=""):
        self.ap = ap
        self.w = None
        self.r = {}
        self.name = name


class TV:
    __slots__ = ("ap", "base", "name")

    def __init__(self, base, ap):
        self.base = base
        self.ap = ap
        self.name = base.name + "_v"

    @property
    def w(self):
        return self.base.w

    @w.setter
    def w(self, v):
        self.base.w = v

    @property
    def r(self):
        return self.base.r

    @r.setter
    def r(self, v):
        self.base.r = v


class Prog:
    def __init__(self):
        self.ops = []
        self.last_dma = {}

    def op(self, eng, fn, reads=(), writes=(), dma=None):
        i = len(self.ops)
        deps = set()
        for t in reads:
            if t.w is not None:
                deps.add(t.w)
        for t in writes:
            if t.w is not None:
                deps.add(t.w)
            deps.update(t.r.values())
        if dma is not None and dma in self.last_dma:
            deps.add(self.last_dma[dma])
        if dma is not None:
            self.last_dma[dma] = i
        for t in writes:
            t.w = i
            t.r = {}
        for t in reads:
            if dma is not None:
                t.r[("dma", i)] = i
            else:
                t.r[eng] = i
        self.ops.append((eng, fn, deps, dma))
        return i

    def emit(self, nc, sem_limit=30000):
        ops = self.ops
        n = len(ops)
        red = []
        needed = [False] * n
        for i, (eng, fn, deps, dma) in enumerate(ops):
            best = {}
            dl = []
            for d in deps:
                de, _, _, ddma = ops[d]
                if ddma is not None:
                    dl.append(d)
                else:
                    if de == eng and eng == "pe":
                        continue
                    if de not in best or best[de] < d:
                        best[de] = d
            dl.extend(best.values())
            red.append(dl)
            for d in dl:
                needed[d] = True
        ordinal = [0] * n
        cnt = {}
        dcnt = {}
        for i, (eng, fn, deps, dma) in enumerate(ops):
            if dma is not None:
                dcnt[dma] = dcnt.get(dma, 0) + 1
                ordinal[i] = 16 * dcnt[dma]
            elif needed[i]:
                cnt[eng] = cnt.get(eng, 0) + 1
                ordinal[i] = cnt[eng]
        self.stats = dict(cnt)
        sems = {}

        def get_sem(key):
            if key not in sems:
                sems[key] = nc.alloc_semaphore(name="s_" + "_".join(str(k) for k in key))
            return sems[key]

        def signal_of(i):
            eng, _, _, dma = ops[i]
            if dma is not None:
                return ("d", dma), ordinal[i], ordinal[i]
            o = ordinal[i]
            ep = (o - 1) // sem_limit
            return ("e", eng, ep), o - ep * sem_limit, o

        for i in range(n):
            if ops[i][3] is not None or needed[i]:
                get_sem(signal_of(i)[0])

        by_eng = {}
        for i, o in enumerate(ops):
            by_eng.setdefault(o[0], []).append(i)

        def run_engine(ename, e):
            known = {}
            nwait = 0
            for i in by_eng.get(ename, []):
                eng, fn, deps, dma = ops[i]
                for d in red[i]:
                    key, local, glob = signal_of(d)
                    kk = key if key[0] == "d" else ("e", key[1])
                    if known.get(kk, 0) >= glob:
                        continue
                    known[kk] = glob
                    e.wait_ge(get_sem(key), local)
                    nwait += 1
                ins = fn(e)
                if dma is not None:
                    ins.then_inc(get_sem(("d", dma)), 16)
                elif needed[i]:
                    ins.then_inc(get_sem(signal_of(i)[0]), 1)
            self.stats["wait_" + ename] = nwait

        with nc.Block() as block:
            @block.tensor
            def _(e):
                run_engine("pe", e)

            @block.scalar
            def _(e):
                run_engine("act", e)

            @block.vector
            def _(e):
                run_engine("dve", e)

            @block.gpsimd
            def _(e):
                run_engine("pool", e)

            @block.sync
            def _(e):
                run_engine("sp", e)


class UnitPool:
    def __init__(self, nc, n, dtype, name, width=NT):
        self.free = []
        for i in range(n):
            ap = nc.alloc_sbuf_tensor(f"{name}{i}", [128, width], dtype).ap()
            self.free.append(T(ap, f"{name}{i}"))
        self.n = n

    def get(self):
        if not self.free:
            raise RuntimeError("unit pool exhausted")
        return self.free.pop(0)

    def put(self, t):
        self.free.append(t)


def _pv_layout():
    cols = {}
    off = 0

    def add(name, n):
        nonlocal off
        cols[name] = off
        off += n
    for i in range(2):
        add(f"nmg{i}", 8)
        add(f"nfg{i}", 8)
    add("fng", 8)
    add("bpw1", 16)
    add("wdw", KW * 8)
    add("bdw", 8)
    add("lng", 8)
    add("lnb", 8)
    add("bpw2", 8)
    add("gcw", 4 * 24)
    add("gng", 1)
    add("alog", 8)
    add("dtb", 8)
    add("alog4", 32)
    add("dtb4", 32)
    return cols, off


PV, NPV = _pv_layout()


def _chunked(v):
    return np.ascontiguousarray(v.reshape(-1, 128).T)


def pack_pvec(inp):
    pv = np.zeros((128, NPV), np.float32)

    def put(name, arr):
        pv[:, PV[name]:PV[name] + arr.shape[1]] = arr
    for i in range(2):
        put(f"nmg{i}", _chunked(inp["norm_mix_g"][i]))
        put(f"nfg{i}", _chunked(inp["norm_ffn_g"][i]))
    put("fng", _chunked(inp["final_norm_g"]))
    put("bpw1", _chunked(inp["cv_b_pw1"][0]))
    wdw = inp["cv_w_dw"][0]
    put("wdw", np.concatenate([_chunked(wdw[k]) for k in range(KW)], axis=1))
    put("bdw", _chunked(inp["cv_b_dw"][0]))
    put("lng", _chunked(inp["cv_ln_g"][0]))
    put("lnb", _chunked(inp["cv_ln_b"][0]))
    put("bpw2", _chunked(inp["cv_b_pw2"][0]))
    gcw = inp["gdn_conv_w"][0]
    put("gcw", np.concatenate([_chunked(gcw[k]) for k in range(4)], axis=1))
    put("gng", inp["gdn_norm_g"][0].reshape(128, 1))
    put("alog", np.broadcast_to(inp["gdn_a_log"][0][None, :], (128, 8)))
    put("dtb", np.broadcast_to(inp["gdn_dt_bias"][0][None, :], (128, 8)))
    put("alog4", np.broadcast_to(np.tile(inp["gdn_a_log"][0], 4)[None, :], (128, 32)))
    put("dtb4", np.broadcast_to(np.tile(inp["gdn_dt_bias"][0], 4)[None, :], (128, 32)))
    return pv


def make_consts():
    idx = np.arange(128)
    ident = np.eye(128, dtype=np.float32)
    triu = (idx[:, None] <= idx[None, :]).astype(np.float32)
    ones = np.ones((128, 128), np.float32)
    stri = (idx[:, None] < idx[None, :]).astype(np.float32)
    maskneg = np.where(idx[:, None] <= idx[None, :], 0.0, -30000.0).astype(np.float32)
    lv = []
    for l in range(7):
        sz = 1 << l
        i = idx[None, :]
        j = idx[:, None]
        m = ((i // (2 * sz)) == (j // (2 * sz))) & ((i % (2 * sz)) >= sz) & ((j % (2 * sz)) < sz)
        lv.append(m.astype(np.float32))
    return np.concatenate([ident, triu, ones, stri, maskneg] + lv, axis=1)


def slab_table(depth):
    sl = []
    for li in range(depth):
        if li % 2 == 0:
            for half in range(2):
                sl.append(("cv_w_pw1", 0, 0, half * 512))
                sl.append(("cv_w_pw1", 0, 0, 1024 + half * 512))
                for j in range(4):
                    sl.append(("diag", half * 4 + j, 0, 0))
            for m0 in (0, 512):
                sl.append(("cv_w_pw2", 0, 0, m0))
        else:
            for m0 in range(0, 4096, 512):
                sl.append(("gdn_w_in", 0, 0, m0))
            for m0 in (0, 512):
                sl.append(("gdn_w_out", 0, 0, m0))
        for m0 in range(0, 4096, 512):
            sl.append(("mlp_w1", li, 0, m0))
        for mg in range(2):
            for kg in range(4):
                sl.append(("mlp_w2", li, kg * 1024, mg * 512))
    return sl


def build_nc(nseq, S, depth, nring=3):
    ntok = nseq * S
    ntiles = ntok // NT
    tiles_per_seq = S // NT
    nc = bass.Bass("TRN2", target_bir_lowering=False)
    P = Prog()

    xT = nc.dram_tensor("xT", [D, ntok], F32, kind="ExternalInput").ap()
    outT = nc.dram_tensor("outT", [D, ntok], F32, kind="ExternalOutput").ap()
    pvec_d = nc.dram_tensor("pvec", [128, NPV], F32, kind="ExternalInput").ap()
    cst_d = nc.dram_tensor("cst", [128, 1536], F32, kind="ExternalInput").ap()
    wd = {}
    wd["cv_w_pw1"] = nc.dram_tensor("cv_w_pw1", [1, D, 2 * D], F32, kind="ExternalInput").ap()
    wd["cv_w_pw2"] = nc.dram_tensor("cv_w_pw2", [1, D, D], F32, kind="ExternalInput").ap()
    wd["mlp_w1"] = nc.dram_tensor("mlp_w1", [2, D, 4 * D], F32, kind="ExternalInput").ap()
    wd["mlp_w2"] = nc.dram_tensor("mlp_w2", [2, 4 * D, D], F32, kind="ExternalInput").ap()
    if depth > 1:
        wd["gdn_w_in"] = nc.dram_tensor("gdn_w_in", [1, D, GIN], F32, kind="ExternalInput").ap()
        wd["gdn_w_out"] = nc.dram_tensor("gdn_w_out", [1, D, D], F32, kind="ExternalInput").ap()
    slabs = slab_table(depth)
    nslab = len(slabs)
    wbf = nc.dram_tensor("wbf", [nslab, 128, 4096], BF16, kind="Internal").ap()
    wdiag = nc.dram_tensor("wdiag", [8, 128, 4096], BF16, kind="Internal").ap()

    def sb(name, shape, dt):
        return nc.alloc_sbuf_tensor(name, shape, dt).ap()

    pv = T(sb("pv", [128, NPV], F32), "pv")
    cst = T(sb("cst_sb", [128, 1536], F32), "cst")
    cbf = T(sb("cbf", [128, 640], BF16), "cbf")
    onesD = T(sb("onesD", [128, 128], BF16), "onesD")
    onesH = T(sb("onesH", [128, 128], BF16), "onesH")
    onesD32 = T(sb("onesD32", [128, 128], F32), "onesD32")
    hbuf = [[T(sb(f"h{b}_{c}", [128, NT], F32), f"h{b}_{c}") for c in range(8)] for b in range(2)]
    ring = [T(sb(f"ring{i}", [128, 4096], BF16), f"ring{i}") for i in range(nring)]
    p16 = UnitPool(nc, 41, BF16, "u16_")
    p32 = UnitPool(nc, 14, F32, "u32_")
    glu_tmp = [T(sb(f"glu{i}", [128, NT + KW - 1], BF16), f"glu{i}") for i in range(8)]
    halo0 = [T(sb(f"halo0_{c}", [128, KW - 1], BF16), f"halo0_{c}") for c in range(8)]
    psum = [T(nc.alloc_psum_tensor(f"ps{i}", [128, NT], F32).ap(), f"ps{i}") for i in range(8)]
    ps_i = [0]

    def next_ps():
        t = psum[ps_i[0] % 8]
        ps_i[0] += 1
        return t

    def pvc(name, j=0):
        o = PV[name] + j
        return pv.ap[:, o:o + 1]

    ident32 = cst.ap[:, 0:128]
    triu32 = cst.ap[:, 128:256]
    ones32 = cst.ap[:, 256:384]
    identbf = cbf.ap[:, 0:128]
    stribf = cbf.ap[:, 384:512]
    maskneg32 = cst.ap[:, 512:640]
    masknegbf = cbf.ap[:, 512:640]

    P.op("sp", lambda e: e.dma_start(out=pv.ap, in_=pvec_d), writes=[pv], dma="pv")
    P.op("sp", lambda e: e.dma_start(out=cst.ap, in_=cst_d), writes=[cst], dma="cst")
    P.op("dve", lambda e: e.tensor_copy(out=cbf.ap, in_=cst.ap[:, 0:640]), reads=[cst], writes=[cbf])
    P.op("dve", lambda e: e.memset(onesD.ap, 1.0 / D), writes=[onesD])
    P.op("dve", lambda e: e.memset(onesH.ap, 1.0), writes=[onesH])
    P.op("dve", lambda e: e.memset(onesD32.ap, 1.0 / D), writes=[onesD32])
    for c in range(8):
        P.op("dve", (lambda c: lambda e: e.memset(halo0[c].ap, 0.0))(c), writes=[halo0[c]])

    cast_op = {}
    diag_built = [False]

    def build_diags():
        for c in range(8):
            stage = ring[c % nring]
            for k in range(KW):
                P.op("dve", (lambda stage, c, k: lambda e: e.tensor_scalar(
                    out=stage.ap[:, k * 128:(k + 1) * 128], in0=ident32, scalar1=pvc("wdw", k * 8 + c), scalar2=None,
                    op0=ALU.mult))(stage, c, k), reads=[cst, pv], writes=[stage])
            P.op("dve", (lambda stage: lambda e: e.memset(stage.ap[:, KW * 128:], 0.0))(stage), writes=[stage])
            i = P.op("sp", (lambda stage, c: lambda e: e.dma_start(out=wdiag[c], in_=stage.ap))(stage, c),
                     reads=[stage], dma=("dg", c % nring))
            cast_op[("diag", c)] = i
    build_diags()
    for s_, (wn, li, k0, m0) in enumerate(slabs):
        if wn == "diag":
            continue
        src = wd[wn][li, k0:k0 + 1024, m0:m0 + 512].rearrange("(kc p) m -> p kc m", p=128)
        dst = wbf[s_].rearrange("p (kc m) -> p kc m", kc=8)
        cast_op[s_] = P.op("pool", (lambda src, dst: lambda e: e.dma_start(out=dst, in_=src))(src, dst),
                           writes=[], dma=("wc", s_ % 4))

    slab_ctr = [0]

    def next_slab(tile_first_use_guard=None):
        g = slab_ctr[0]
        slab_ctr[0] += 1
        s = g % nslab
        slot = ring[g % nring]
        srcap = wdiag[slabs[s][1]] if slabs[s][0] == "diag" else wbf[s]
        i = P.op("sp", (lambda srcap, slot: lambda e: e.dma_start(out=slot.ap, in_=srcap))(srcap, slot),
                 writes=[slot], dma=("ring", g % nring))
        if g < nslab:
            key = ("diag", slabs[s][1]) if slabs[s][0] == "diag" else s
            P.ops[i][2].add(cast_op[key])
        return slot, slot.ap.rearrange("p (kc m) -> p kc m", kc=8)


    def mm(ps, ps_ap, lt, lt_ap, rt, rt_ap, start, stop):
        P.op("pe", lambda e: e.matmul(ps_ap, lt_ap, rt_ap, start=start, stop=stop),
             reads=[lt, rt], writes=[ps])

    def act(out_t, out_ap, in_t, in_ap, func, bias=None, scale=None, extra_reads=()):
        kw = {}
        if bias is not None:
            kw["bias"] = bias
        if scale is not None:
            kw["scale"] = scale
        P.op("act", lambda e: e.activation(out_ap, in_ap, func, **kw),
             reads=[in_t, *extra_reads], writes=[out_t])

    def rstd_from(ps_stat, eps=EPS):
        t = p32.get()
        act(t, t.ap, ps_stat, ps_stat.ap, AF.Ln, bias=eps_ap(eps), extra_reads=[epsT])
        act(t, t.ap, t, t.ap, AF.Exp, scale=-0.5)
        return t

    epsT = T(sb("epsc", [128, 2], F32), "epsc")
    P.op("dve", lambda e: e.memset(epsT.ap[:, 0:1], EPS), writes=[epsT])
    P.op("dve", lambda e: e.memset(epsT.ap[:, 1:2], 1.0), writes=[epsT])

    def eps_ap(eps):
        return epsT.ap[:, 0:1]

    def rmsnorm(h, gname):
        ps = next_ps()
        for c in range(8):
            sq = p16.get()
            act(sq, sq.ap, h[c], h[c].ap, AF.Square)
            mm(ps, ps.ap, onesD, onesD.ap, sq, sq.ap, c == 0, c == 7)
            p16.put(sq)
        r = rstd_from(ps)
        out = []
        for c in range(8):
            o = p16.get()
            P.op("dve", (lambda o, c: lambda e: e.scalar_tensor_tensor(
                out=o.ap, in0=h[c].ap, scalar=pvc(gname, c), in1=r.ap, op0=ALU.mult, op1=ALU.mult))(o, c),
                reads=[h[c], pv, r], writes=[o])
            out.append(o)
        p32.put(r)
        return out

    def conformer(h, hn, first_of_seq):
        acc = [None] * 8
        for half in range(2):
            wa_t, wa = next_slab()
            wg_t, wg = next_slab()
            pga = [next_ps() for _ in range(4)]
            for kc in range(8):
                for j in range(4):
                    mm(pga[j], pga[j].ap, wa_t, wa[:, kc, j * 128:(j + 1) * 128], hn[kc], hn[kc].ap, kc == 0, kc == 7)
            pgg = [next_ps() for _ in range(4)]
            for kc in range(8):
                for j in range(4):
                    mm(pgg[j], pgg[j].ap, wg_t, wg[:, kc, j * 128:(j + 1) * 128], hn[kc], hn[kc].ap, kc == 0, kc == 7)
            for j in range(4):
                c = half * 4 + j
                psa = pga[j]
                psg = pgg[j]
                sg = p32.get()
                act(sg, sg.ap, psg, psg.ap, AF.Sigmoid, bias=pvc("bpw1", 8 + c), extra_reads=[pv])
                gt = glu_tmp[c]
                P.op("dve", (lambda gt, psa, sg, c: lambda e: e.scalar_tensor_tensor(
                    out=gt.ap[:, KW - 1:], in0=psa.ap, scalar=pvc("bpw1", c), in1=sg.ap,
                    op0=ALU.add, op1=ALU.mult))(gt, psa, sg, c),
                    reads=[psa, sg, pv], writes=[gt])
                p32.put(sg)
                if first_of_seq:
                    P.op("pool", (lambda gt: lambda e: e.memset(gt.ap[:, 0:KW - 1], 0.0))(gt), writes=[gt])
                else:
                    P.op("pool", (lambda gt, c: lambda e: e.tensor_copy(out=gt.ap[:, 0:KW - 1], in_=halo0[c].ap))(gt, c),
                         reads=[halo0[c]], writes=[gt])
                P.op("pool", (lambda gt, c: lambda e: e.tensor_copy(out=halo0[c].ap, in_=gt.ap[:, NT:NT + KW - 1]))(gt, c),
                     reads=[gt], writes=[halo0[c]])
            for j in range(4):
                c = half * 4 + j
                d_t, _ = next_slab()
                gt = glu_tmp[c]
                psc = next_ps()
                for k in range(KW):
                    mm(psc, psc.ap, d_t, d_t.ap[:, k * 128:(k + 1) * 128], gt, gt.ap[:, k:k + NT], k == 0, k == KW - 1)
                a = p32.get()
                act(a, a.ap, psc, psc.ap, AF.Identity, bias=pvc("bdw", c), extra_reads=[pv])
                acc[c] = a
        for t in hn:
            p16.put(t)
        psm = next_ps()
        for c in range(8):
            mm(psm, psm.ap, onesD32, onesD32.ap, acc[c], acc[c].ap, c == 0, c == 7)
        for c in range(8):
            P.op("dve", (lambda c: lambda e: e.tensor_tensor(
                out=acc[c].ap, in0=acc[c].ap, in1=psm.ap, op=ALU.subtract))(c),
                reads=[acc[c], psm], writes=[acc[c]])
        psv = next_ps()
        for c in range(8):
            sq = p16.get()
            act(sq, sq.ap, acc[c], acc[c].ap, AF.Square)
            mm(psv, psv.ap, onesD, onesD.ap, sq, sq.ap, c == 0, c == 7)
            p16.put(sq)
        r = rstd_from(psv)
        ua = []
        for c in range(8):
            P.op("dve", (lambda c: lambda e: e.tensor_tensor(
                out=acc[c].ap, in0=acc[c].ap, in1=r.ap, op=ALU.mult))(c),
                reads=[acc[c], r], writes=[acc[c]])
            o = p16.get()
            act(o, o.ap, acc[c], acc[c].ap, AF.Silu, bias=pvc("lnb", c), scale=pvc("lng", c), extra_reads=[pv])
            ua.append(o)
            p32.put(acc[c])
        p32.put(r)
        for half in range(2):
            w_t, w = next_slab()
            pgrp = [next_ps() for _ in range(4)]
            for kc in range(8):
                for j in range(4):
                    mm(pgrp[j], pgrp[j].ap, w_t, w[:, kc, j * 128:(j + 1) * 128], ua[kc], ua[kc].ap, kc == 0, kc == 7)
            for j in range(4):
                mc = half * 4 + j
                ps = pgrp[j]
                P.op("dve", (lambda ps, mc: lambda e: e.scalar_tensor_tensor(
                    out=h[mc].ap, in0=ps.ap, scalar=pvc("bpw2", mc), in1=h[mc].ap,
                    op0=ALU.add, op1=ALU.add))(ps, mc), reads=[ps, pv, h[mc]], writes=[h[mc]])
        for t in ua:
            p16.put(t)

    def mlp(h, hn):
        hid = []
        for g in range(8):
            w_t, w = next_slab()
            pgrp = [next_ps() for _ in range(4)]
            for kc in range(8):
                for j in range(4):
                    mm(pgrp[j], pgrp[j].ap, w_t, w[:, kc, j * 128:(j + 1) * 128], hn[kc], hn[kc].ap, kc == 0, kc == 7)
            for j in range(4):
                ps = pgrp[j]
                sq = p32.get()
                act(sq, sq.ap, ps, ps.ap, AF.Square)
                o = p16.get()
                P.op("dve", (lambda o, ps, sq: lambda e: e.scalar_tensor_tensor(
                    out=o.ap, in0=ps.ap, scalar=0.0, in1=sq.ap, op0=ALU.is_gt, op1=ALU.mult))(o, ps, sq),
                    reads=[ps, sq], writes=[o])
                p32.put(sq)
                hid.append(o)
        for t in hn:
            p16.put(t)
        for mg in range(2):
            pss = [next_ps() for _ in range(4)]
            for kg in range(4):
                w_t, w = next_slab()
                for j in range(4):
                    for kc in range(8):
                        mm(pss[j], pss[j].ap, w_t, w[:, kc, j * 128:(j + 1) * 128],
                           hid[kg * 8 + kc], hid[kg * 8 + kc].ap, kg == 0 and kc == 0, kg == 3 and kc == 7)
            for j in range(4):
                mc = mg * 4 + j
                P.op("dve", (lambda ps, mc: lambda e: e.tensor_tensor(
                    out=h[mc].ap, in0=h[mc].ap, in1=ps.ap, op=ALU.add))(pss[j], mc),
                    reads=[pss[j], h[mc]], writes=[h[mc]])
        for t in hid:
            p16.put(t)


    if depth > 1:
        wab32 = T(sb("wab32", [128, 8, 16], F32), "wab32")
        wab = T(sb("wab", [128, 8, 16], BF16), "wab")
        negA4 = T(sb("negA4", [128, 32], F32), "negA4")
        onesE = T(sb("onesE", [128, 128], BF16), "onesE")
        halo1 = T(sb("halo1", [128, 72], F32), "halo1")
        S32 = [T(sb(f"S32_{i}", [128, 512], F32), f"S32_{i}") for i in range(2)]
        Sbf = [T(sb(f"Sbf_{i}", [128, 512], BF16), f"Sbf_{i}") for i in range(2)]
        pre_tmp = [T(sb(f"pre{i}", [128, NT + 3], F32), f"pre{i}") for i in range(4)]
        GE = T(sb("GE", [128, 1024], F32), "GE")

        def b16(name):
            return T(sb(name, [128, 1024], BF16), name)
        Yk, kd, vtok, qkT, qgT, dec, egcR = [b16(n) for n in ("Yk", "kd", "vtok", "qkT", "qgT", "dec", "egcR")]
        decs = T(sb("decs", [128, 1024], F32), "decs")

        def pair(name, dt):
            return [T(sb(f"{name}_{i}", [128, 512], dt), f"{name}_{i}") for i in range(2)]
        CDT = BF16
        TW = pair("TW", CDT)
        P0 = pair("P0", CDT)
        Am = pair("Am", CDT)
        Xm = pair("Xm", CDT)
        MT = Am
        identR = T(sb("identR", [128, 128], CDT), "identR")
        P.op("dve", lambda e: e.tensor_copy(out=identR.ap, in_=ident32), reads=[cst], writes=[identR])
        nZw = pair("nZw", BF16)
        vnew = pair("vnew", BF16)
        gatesT = T(sb("gates", [128, 12 * 32], F32), "gates")

        class NS:
            pass
        BS = [NS(), NS()]
        BS[0].Yk, BS[0].kd, BS[0].vtok, BS[0].qkT, BS[0].qgT = Yk, kd, vtok, qkT, qgT
        BS[0].TW, BS[0].P0, BS[0].Am, BS[0].Xm, BS[0].nZw, BS[0].vnew = TW, P0, Am, Xm, nZw, vnew
        BS[1].Yk = TV(pre_tmp[0], pre_tmp[0].ap.bitcast(BF16)[:, 0:1024])
        BS[1].kd = TV(pre_tmp[1], pre_tmp[1].ap.bitcast(BF16)[:, 0:1024])
        BS[1].vtok = TV(pre_tmp[2], pre_tmp[2].ap.bitcast(BF16)[:, 0:1024])
        BS[1].qkT = b16("qkT1")
        BS[1].qgT = b16("qgT1")
        BS[1].TW = pair("TW1", CDT)
        BS[1].P0 = pair("P01", CDT)
        BS[1].Am = pair("Am1", CDT)
        BS[1].Xm = pair("Xm1", CDT)
        BS[1].nZw = [TV(glu_tmp[i], glu_tmp[i].ap[:, 0:512]) for i in range(2)]
        BS[1].vnew = [TV(glu_tmp[2 + i], glu_tmp[2 + i].ap[:, 0:512]) for i in range(2)]
        src = wd["gdn_w_in"][0, :, 4096:4112].rearrange("(kc p) m -> p kc m", p=128)
        P.op("sp", lambda e: e.dma_start(out=wab32.ap, in_=src), writes=[wab32], dma="wab")
        P.op("dve", lambda e: e.tensor_copy(out=wab.ap, in_=wab32.ap), reads=[wab32], writes=[wab])
        P.op("dve", lambda e: e.memset(onesE.ap, 1.0 / 128), writes=[onesE])
        P.op("act", lambda e: e.activation(negA4.ap, pv.ap[:, PV["alog4"]:PV["alog4"] + 32], AF.Exp),
             reads=[pv], writes=[negA4])
        P.op("dve", lambda e: e.tensor_scalar(out=negA4.ap, in0=negA4.ap, scalar1=-1.0, scalar2=None, op0=ALU.mult),
             reads=[negA4], writes=[negA4])

    def v3(ap, n):
        return ap.rearrange("p (h i) -> p h i", h=n)

    def q4(t, h4):
        return t.ap[:, h4 * 128:(h4 + 1) * 128]

    def q4r(t, h4):
        return t.ap[:, h4 * 128:(h4 + 1) * 128]

    def f32(ap):
        return ap.bitcast(F32) if CHAIN_R[0] else ap

    def hs(t, h):
        return t.ap[:, h * 128:(h + 1) * 128]

    def gdn(h, hn, first):
        if first:
            for i in range(2):
                P.op("pool", lambda e, i=i: e.memset(S32[i].ap, 0.0), writes=[S32[i]])
                P.op("pool", lambda e, i=i: e.memset(Sbf[i].ap, 0.0), writes=[Sbf[i]])
        one_ap = epsT.ap[:, 1:2]
        gs = gatesT

        def G(i):
            return gs.ap[:, i * 32:(i + 1) * 32]

        def G3(i):
            return G(i).rearrange("p (b x) -> p b x", x=8)

        def gop(eng, fn, extra_r=(), extra_w=()):
            P.op(eng, fn, reads=[gs, *extra_r], writes=[gs, *extra_w])
        pab = next_ps()
        for b in range(4):
            for kc in range(8):
                mm(pab, pab.ap[:, b * 16:(b + 1) * 16], hn[kc], hn[kc].ap[:, b * 128:(b + 1) * 128],
                   wab, wab.ap[:, kc, :], kc == 0, kc == 7)
        pab3 = pab.ap[:, 0:64].rearrange("p (b x) -> p b x", x=16)
        dtb3 = pv.ap[:, PV["dtb4"]:PV["dtb4"] + 32].rearrange("p (b x) -> p b x", x=8)
        gop("dve", lambda e, pab3=pab3: e.tensor_tensor(out=G3(0), in0=pab3[:, :, 0:8], in1=dtb3, op=ALU.add),
            extra_r=[pab, pv])
        gop("act", lambda e: e.activation(G(1), G(0), AF.Exp))
        gop("act", lambda e: e.activation(G(1), G(1), AF.Ln, bias=one_ap), extra_r=[epsT])
        gop("dve", lambda e: e.tensor_tensor(out=G(2), in0=G(1), in1=negA4.ap, op=ALU.mult), extra_r=[negA4])
        gop("act", lambda e, pab3=pab3: e.activation(G3(3), pab3[:, :, 8:16], AF.Exp, scale=-1.0), extra_r=[pab])
        gop("dve", lambda e: e.tensor_scalar(out=G(3), in0=G(3), scalar1=1.0, scalar2=None, op0=ALU.add))
        gop("dve", lambda e: e.reciprocal(out=G(4), in_=G(3)))
        gop("dve", lambda e: e.tensor_scalar(out=G(5), in0=G(4), scalar1=-1.0, scalar2=None, op0=ALU.mult))

        qT = [None] * 8
        kT = [None] * 8
        vT = [None] * 8
        sz = [None] * 8
        grp = {}

        def S1(g):
            w_t, w = next_slab()
            st = grp[g] = {"a": [None] * 4}
            pgrp = [next_ps() for _ in range(4)]
            for kc in range(8):
                for j in range(4):
                    mm(pgrp[j], pgrp[j].ap, w_t, w[:, kc, j * 128:(j + 1) * 128], hn[kc], hn[kc].ap, kc == 0, kc == 7)
            for j in range(4):
                cc = g * 4 + j
                ps = pgrp[j]
                if cc >= 24:
                    o = p16.get()
                    act(o, o.ap, ps, ps.ap, AF.Silu)
                    sz[cc - 24] = o
                    continue
                pre = pre_tmp[cc % 4]
                act(pre, pre.ap[:, 3:], ps, ps.ap, AF.Copy)
                if first:
                    P.op("pool", (lambda pre: lambda e: e.memset(pre.ap[:, 0:3], 0.0))(pre), writes=[pre])
                else:
                    P.op("pool", (lambda pre, cc: lambda e: e.tensor_copy(
                        out=pre.ap[:, 0:3], in_=halo1.ap[:, cc * 3:cc * 3 + 3]))(pre, cc),
                        reads=[halo1], writes=[pre])
                P.op("pool", (lambda pre, cc: lambda e: e.tensor_copy(
                    out=halo1.ap[:, cc * 3:cc * 3 + 3], in_=pre.ap[:, NT:NT + 3]))(pre, cc),
                    reads=[pre], writes=[halo1])
                a = p32.get()
                P.op("dve", (lambda a, pre, cc: lambda e: e.tensor_scalar(
                    out=a.ap, in0=pre.ap[:, 0:NT], scalar1=pvc("gcw", cc), scalar2=None, op0=ALU.mult))(a, pre, cc),
                    reads=[pre, pv], writes=[a])
                for k in range(1, 4):
                    P.op("dve", (lambda a, pre, cc, k: lambda e: e.scalar_tensor_tensor(
                        out=a.ap, in0=pre.ap[:, k:k + NT], scalar=pvc("gcw", k * 24 + cc), in1=a.ap,
                        op0=ALU.mult, op1=ALU.add))(a, pre, cc, k), reads=[pre, pv, a], writes=[a])
                st["a"][j] = a

        def S2(g):
            if g >= 6:
                return
            for j in range(4):
                cc = g * 4 + j
                a = grp[g]["a"][j]
                if cc >= 16:
                    o = p16.get()
                    act(o, o.ap, a, a.ap, AF.Silu)
                    vT[cc - 16] = o
                    p32.put(a)
                else:
                    act(a, a.ap, a, a.ap, AF.Silu)

        def S3(g):
            if g >= 4:
                return
            pss = []
            sqs = []
            for j in range(4):
                a = grp[g]["a"][j]
                sq = p16.get()
                act(sq, sq.ap, a, a.ap, AF.Square)
                ps2 = next_ps()
                mm(ps2, ps2.ap, onesH, onesH.ap, sq, sq.ap, True, True)
                pss.append(ps2)
                sqs.append(sq)
            for sq in sqs:
                p16.put(sq)
            rs = []
            for j in range(4):
                t = p32.get()
                act(t, t.ap, pss[j], pss[j].ap, AF.Ln, bias=eps_ap(EPS), extra_reads=[epsT])
                rs.append(t)
            for j in range(4):
                act(rs[j], rs[j].ap, rs[j], rs[j].ap, AF.Exp, scale=-0.5)
            for j in range(4):
                cc = g * 4 + j
                a = grp[g]["a"][j]
                r = rs[j]
                o = p16.get()
                if cc < 8:
                    P.op("dve", (lambda o, a, r: lambda e: e.scalar_tensor_tensor(
                        out=o.ap, in0=a.ap, scalar=float(128 ** -0.5), in1=r.ap, op0=ALU.mult, op1=ALU.mult))(o, a, r),
                        reads=[a, r], writes=[o])
                    qT[cc] = o
                else:
                    P.op("dve", (lambda o, a, r: lambda e: e.tensor_tensor(
                        out=o.ap, in0=a.ap, in1=r.ap, op=ALU.mult))(o, a, r), reads=[a, r], writes=[o])
                    kT[cc - 8] = o
                p32.put(a)
                p32.put(r)

        def gates_part2():
            pcs = next_ps()
            mm(pcs, pcs.ap[:, 0:32], cst, triu32, gs, G(2), True, True)
            mm(pcs, pcs.ap[:, 32:64], cst, ones32, gs, G(2), True, True)
            gop("act", lambda e, pcs=pcs: e.activation(G(6), pcs.ap[:, 0:32], AF.Copy), extra_r=[pcs])
            gop("act", lambda e, pcs=pcs: e.activation(G(7), pcs.ap[:, 0:32], AF.Exp), extra_r=[pcs])
            gop("dve", lambda e: e.tensor_scalar(out=G(10), in0=G(6), scalar1=-1.0, scalar2=None, op0=ALU.mult))
            gop("dve", lambda e, pcs=pcs: e.tensor_tensor(out=G(8), in0=pcs.ap[:, 32:64], in1=G(6), op=ALU.subtract),
                extra_r=[pcs])
            gop("act", lambda e: e.activation(G(8), G(8), AF.Exp))
            gop("act", lambda e, pcs=pcs: e.activation(G(9), pcs.ap[:, 32:64], AF.Exp), extra_r=[pcs])

        for step in range(9):
            if step < 8:
                S1(step)
            if step == 0:
                gates_part2()
            if step >= 1:
                S2(step - 1)
                S3(step - 1)

        oT = [p32.get() for _ in range(8)]
        triu_b8 = triu32.unsqueeze(1).broadcast_to([128, 8, 128])
        stri_b8 = cst.ap[:, 384:512].unsqueeze(1).broadcast_to([128, 8, 128])
        ident_b4 = ident32.unsqueeze(1).broadcast_to([128, 4, 128])
        def lmask(l):
            return cst.ap[:, 640 + l * 128:640 + (l + 1) * 128].unsqueeze(1).broadcast_to([128, 4, 128])

        def prep(b, B):
            tb = slice(b * 128, (b + 1) * 128)

            def gb(i):
                return gs.ap[:, i * 32 + b * 8:i * 32 + b * 8 + 8]

            def g1(i, hh):
                return gs.ap[:, i * 32 + b * 8 + hh:i * 32 + b * 8 + hh + 1]
            P.op("dve", lambda e: e.tensor_copy(
                out=v3(GE.ap, 8), in_=gb(2).unsqueeze(2).broadcast_to([128, 8, 128])),
                reads=[gs], writes=[GE])
            pR = [next_ps(), next_ps()]
            pRm = [next_ps(), next_ps()]
            for hh in range(8):
                mm(pR[hh // 4], q4(pR[hh // 4], hh % 4), GE, hs(GE, hh), cst, triu32, True, True)
                mm(pRm[hh // 4], q4(pRm[hh // 4], hh % 4), GE, hs(GE, hh), cst, triu32, True, False)
                mm(pRm[hh // 4], q4(pRm[hh // 4], hh % 4), cbf, identbf, cbf, masknegbf, False, True)
            ptr = next_ps()
            ptr_bf = ptr.ap.bitcast(BF16)
            for hh in range(8):
                P.op("pe", lambda e, hh=hh: e.transpose(
                    out=ptr_bf[:, hh * 128:(hh + 1) * 128], in_=kT[hh].ap[:, tb], identity=identbf),
                    reads=[kT[hh], cbf], writes=[ptr])
            ptv = next_ps()
            ptv_bf = ptv.ap.bitcast(BF16)
            for hh in range(8):
                P.op("pe", lambda e, hh=hh: e.transpose(
                    out=ptv_bf[:, hh * 128:(hh + 1) * 128], in_=vT[hh].ap[:, tb], identity=identbf),
                    reads=[vT[hh], cbf], writes=[ptv])
            for hg in range(2):
                act(egcR, egcR.ap[:, hg * 512:(hg + 1) * 512], pR[hg], pR[hg].ap, AF.Exp)
            for hh in range(8):
                act(dec, hs(dec, hh), pRm[hh // 4], q4(pRm[hh // 4], hh % 4), AF.Exp,
                    bias=g1(10, hh), extra_reads=[gs])
            P.op("dve", lambda e: e.tensor_tensor(
                out=v3(decs.ap, 8), in0=v3(dec.ap, 8), in1=stri_b8, op=ALU.mult),
                reads=[dec, cst], writes=[decs])
            P.op("dve", lambda e: e.tensor_tensor(
                out=v3(decs.ap, 8), in0=v3(decs.ap, 8), in1=gb(5).unsqueeze(2).broadcast_to([128, 8, 128]), op=ALU.mult),
                reads=[decs, gs], writes=[decs])
            P.op("dve", lambda e: e.tensor_tensor(
                out=v3(B.Yk.ap, 8), in0=v3(ptr_bf, 8), in1=gb(7).unsqueeze(2).broadcast_to([128, 8, 128]), op=ALU.mult),
                reads=[ptr, gs], writes=[B.Yk])
            P.op("dve", lambda e: e.tensor_tensor(
                out=v3(B.kd.ap, 8), in0=v3(ptr_bf, 8), in1=gb(8).unsqueeze(2).broadcast_to([128, 8, 128]), op=ALU.mult),
                reads=[ptr, gs], writes=[B.kd])
            act(B.vtok, B.vtok.ap, ptv, ptv_bf, AF.Copy)
            for hg in range(2):
                pkk = next_ps()
                pqk = next_ps()
                for h4 in range(4):
                    hh = hg * 4 + h4
                    mm(pkk, q4(pkk, h4), kT[hh], kT[hh].ap[:, tb], kT[hh], kT[hh].ap[:, tb], True, True)
                    mm(pqk, q4(pqk, h4), kT[hh], kT[hh].ap[:, tb], qT[hh], qT[hh].ap[:, tb], True, True)
                P.op("dve", lambda e, hg=hg, pkk=pkk: e.tensor_tensor(
                    out=B.TW[hg].ap, in0=pkk.ap, in1=decs.ap[:, hg * 512:(hg + 1) * 512], op=ALU.mult),
                    reads=[pkk, decs], writes=[B.TW[hg]])
                P.op("dve", lambda e, hg=hg, pqk=pqk: e.tensor_tensor(
                    out=B.qkT.ap[:, hg * 512:(hg + 1) * 512], in0=pqk.ap, in1=dec.ap[:, hg * 512:(hg + 1) * 512], op=ALU.mult),
                    reads=[pqk, dec], writes=[B.qkT])
            for hh in range(8):
                P.op("pool", lambda e, hh=hh: e.tensor_tensor(
                    out=hs(B.qgT, hh), in0=qT[hh].ap[:, tb], in1=hs(egcR, hh), op=ALU.mult),
                    reads=[qT[hh], egcR], writes=[B.qgT])
            for hg in range(2):
                pp = next_ps()
                pp_b = pp.ap.bitcast(BF16)
                for h4 in range(4):
                    P.op("pe", lambda e, h4=h4, hg=hg, pp_b=pp_b: e.transpose(
                        out=pp_b[:, h4 * 128:(h4 + 1) * 128], in_=q4(B.TW[hg], h4), identity=identbf),
                        reads=[B.TW[hg], cbf], writes=[pp])
                act(B.P0[hg], B.P0[hg].ap, pp, pp_b[:, 0:512], AF.Copy)
                P.op("dve", lambda e, hg=hg: e.tensor_tensor(
                    out=v3(B.Am[hg].ap, 4), in0=v3(B.TW[hg].ap, 4), in1=lmask(0), op=ALU.mult),
                    reads=[B.TW[hg], cst], writes=[B.Am[hg]])
                P.op("dve", lambda e, hg=hg: e.tensor_tensor(
                    out=v3(B.Am[hg].ap, 4), in0=v3(B.Am[hg].ap, 4), in1=ident_b4, op=ALU.add),
                    reads=[B.Am[hg], cst], writes=[B.Am[hg]])
            for hg in range(2):
                px = next_ps()
                px_b = px.ap.bitcast(BF16)
                for h4 in range(4):
                    P.op("pe", lambda e, h4=h4, hg=hg, px_b=px_b: e.transpose(
                        out=px_b[:, h4 * 128:(h4 + 1) * 128], in_=q4(B.Am[hg], h4), identity=identbf),
                        reads=[B.Am[hg], cbf], writes=[px])
                act(B.Xm[hg], B.Xm[hg].ap, px, px_b[:, 0:512], AF.Copy)

        def levelW(B, l):
            for hg in range(2):
                pW = next_ps()
                for h4 in range(4):
                    mm(pW, q4(pW, h4), B.P0[hg], q4(B.P0[hg], h4), B.Am[hg], q4(B.Am[hg], h4), True, True)
                P.op("dve", lambda e, hg=hg, pW=pW: e.tensor_tensor(
                    out=v3(B.TW[hg].ap, 4), in0=v3(pW.ap, 4), in1=lmask(l), op=ALU.mult),
                    reads=[pW, cst], writes=[B.TW[hg]])

        def levelAX(B, l):
            for hg in range(2):
                pA = next_ps()
                for h4 in range(4):
                    mm(pA, q4(pA, h4), B.Xm[hg], q4(B.Xm[hg], h4), B.TW[hg], q4(B.TW[hg], h4), True, False)
                    mm(pA, q4(pA, h4), cbf, identbf, B.Am[hg], q4(B.Am[hg], h4), False, True)
                if l < 6:
                    pX = next_ps()
                    for h4 in range(4):
                        mm(pX, q4(pX, h4), B.TW[hg], q4(B.TW[hg], h4), B.Xm[hg], q4(B.Xm[hg], h4), True, False)
                        mm(pX, q4(pX, h4), cbf, identbf, B.Xm[hg], q4(B.Xm[hg], h4), False, True)
                act(B.Am[hg], B.Am[hg].ap, pA, pA.ap, AF.Copy)
                if l < 6:
                    act(B.Xm[hg], B.Xm[hg].ap, pX, pX.ap, AF.Copy)

        def post(B):
            for hg in range(2):
                pz = next_ps()
                for h4 in range(4):
                    hh = hg * 4 + h4
                    mm(pz, q4(pz, h4), B.Yk, hs(B.Yk, hh), B.Am[hg], q4(B.Am[hg], h4), True, True)
                act(B.nZw[hg], B.nZw[hg].ap, pz, pz.ap, AF.Copy, scale=-1.0)

        def scan(b, B):
            tb = slice(b * 128, (b + 1) * 128)

            def gb(i):
                return gs.ap[:, i * 32 + b * 8:i * 32 + b * 8 + 8]

            def g1(i, hh):
                return gs.ap[:, i * 32 + b * 8 + hh:i * 32 + b * 8 + hh + 1]
            for hg in range(2):
                pvn = next_ps()
                for h4 in range(4):
                    hh = hg * 4 + h4
                    mm(pvn, q4(pvn, h4), B.Am[hg], q4(B.Am[hg], h4), B.vtok, hs(B.vtok, hh), True, False)
                    mm(pvn, q4(pvn, h4), B.nZw[hg], q4(B.nZw[hg], h4), Sbf[hg], q4(Sbf[hg], h4), False, True)
                P.op("dve", lambda e, hg=hg, pvn=pvn: e.tensor_tensor(
                    out=v3(B.vnew[hg].ap, 4), in0=v3(pvn.ap, 4),
                    in1=gb(4)[:, hg * 4:hg * 4 + 4].unsqueeze(2).broadcast_to([128, 4, 128]), op=ALU.mult),
                    reads=[pvn, gs], writes=[B.vnew[hg]])
            for hg in range(2):
                po = next_ps()
                for h4 in range(4):
                    hh = hg * 4 + h4
                    mm(po, q4(po, h4), Sbf[hg], q4(Sbf[hg], h4), B.qgT, hs(B.qgT, hh), True, False)
                    mm(po, q4(po, h4), B.vnew[hg], q4(B.vnew[hg], h4), B.qkT, hs(B.qkT, hh), False, True)
                for h4 in range(4):
                    hh = hg * 4 + h4
                    act(oT[hh], oT[hh].ap[:, tb], po, q4(po, h4), AF.Copy)
            for hg in range(2):
                pS = next_ps()
                for h4 in range(4):
                    hh = hg * 4 + h4
                    mm(pS, q4(pS, h4), B.kd, hs(B.kd, hh), B.vnew[hg], q4(B.vnew[hg], h4), True, True)
                for h4 in range(4):
                    hh = hg * 4 + h4
                    P.op("dve", lambda e, hh=hh, h4=h4, hg=hg, pS=pS: e.scalar_tensor_tensor(
                        out=q4(S32[hg], h4), in0=q4(S32[hg], h4), scalar=g1(9, hh), in1=q4(pS, h4),
                        op0=ALU.mult, op1=ALU.add), reads=[S32[hg], gs, pS], writes=[S32[hg]])
                act(Sbf[hg], Sbf[hg].ap, S32[hg], S32[hg].ap, AF.Copy)

        for pr in range(2):
            b0, b1 = 2 * pr, 2 * pr + 1
            prep(b0, BS[0])
            prep(b1, BS[1])
            for l in range(1, 7):
                levelW(BS[0], l)
                levelW(BS[1], l)
                levelAX(BS[0], l)
                levelAX(BS[1], l)
            post(BS[0])
            post(BS[1])
            scan(b0, BS[0])
            scan(b1, BS[1])
        for t in hn + qT + kT + vT:
            p16.put(t)
        onT = []
        for hh in range(8):
            sq = p16.get()
            act(sq, sq.ap, oT[hh], oT[hh].ap, AF.Square)
            ps = next_ps()
            mm(ps, ps.ap, onesE, onesE.ap, sq, sq.ap, True, True)
            p16.put(sq)
            r = rstd_from(ps)
            P.op("dve", lambda e, hh=hh, r=r: e.scalar_tensor_tensor(
                out=oT[hh].ap, in0=oT[hh].ap, scalar=pvc("gng"), in1=r.ap, op0=ALU.mult, op1=ALU.mult),
                reads=[oT[hh], pv, r], writes=[oT[hh]])
            o = p16.get()
            P.op("dve", lambda e, hh=hh, o=o: e.tensor_tensor(
                out=o.ap, in0=oT[hh].ap, in1=sz[hh].ap, op=ALU.mult), reads=[oT[hh], sz[hh]], writes=[o])
            onT.append(o)
            p32.put(r)
            p32.put(oT[hh])
            p16.put(sz[hh])
        for half in range(2):
            w_t, w = next_slab()
            pgrp = [next_ps() for _ in range(4)]
            for kc in range(8):
                for j in range(4):
                    mm(pgrp[j], pgrp[j].ap, w_t, w[:, kc, j * 128:(j + 1) * 128], onT[kc], onT[kc].ap, kc == 0, kc == 7)
            for j in range(4):
                mc = half * 4 + j
                ps = pgrp[j]
                P.op("dve", lambda e, ps=ps, mc=mc: e.tensor_tensor(
                    out=h[mc].ap, in0=h[mc].ap, in1=ps.ap, op=ALU.add), reads=[ps, h[mc]], writes=[h[mc]])
        for t in onT:
            p16.put(t)

    def load_x(t):
        hb = hbuf[t % 2]
        for c in range(8):
            P.op("sp", (lambda c, hb, t: lambda e: e.dma_start(
                out=hb[c].ap, in_=xT[c * 128:(c + 1) * 128, t * NT:(t + 1) * NT]))(c, hb, t),
                writes=[hb[c]], dma=("x", t % 2, c))

    store_ops = []

    def tile_body(t):
        h = hbuf[t % 2]
        first = (t % tiles_per_seq) == 0
        if t + 1 < ntiles:
            load_x(t + 1)
        for li in range(depth):
            hn = rmsnorm(h, f"nmg{li}")
            if li % 2 == 0:
                conformer(h, hn, first)
            else:
                gdn(h, hn, first)
            hn = rmsnorm(h, f"nfg{li}")
            mlp(h, hn)
        ps = next_ps()
        for c in range(8):
            sq = p16.get()
            act(sq, sq.ap, h[c], h[c].ap, AF.Square)
            mm(ps, ps.ap, onesD, onesD.ap, sq, sq.ap, c == 0, c == 7)
            p16.put(sq)
        r = rstd_from(ps)
        for c in range(8):
            o = p32.get()
            P.op("dve", (lambda o, c: lambda e: e.scalar_tensor_tensor(
                out=o.ap, in0=h[c].ap, scalar=pvc("fng", c), in1=r.ap, op0=ALU.mult, op1=ALU.mult))(o, c),
                reads=[h[c], pv, r], writes=[o])
            i = P.op("pool", (lambda o, c: lambda e: e.dma_start(
                out=outT[c * 128:(c + 1) * 128, t * NT:(t + 1) * NT], in_=o.ap))(o, c),
                reads=[o], dma=("st", o.name))
            store_ops.append(i)
            p32.put(o)
        p32.put(r)

    load_x(0)
    for t in range(ntiles):
        tile_body(t)
    fin = P.op("pool", lambda e: e.nop(), reads=[], writes=[])
    P.ops[fin][2].update(store_ops)
    P.emit(nc)
    return nc, P


_CACHE = {}


def _run(inputs, nseq, S, depth, ncores):
    key = (nseq, S, depth)
    if key not in _CACHE:
        _CACHE[key] = build_nc(nseq, S, depth)
    nc, P = _CACHE[key]
    x = np.asarray(inputs["x"], np.float32)
    B = x.shape[0]
    assert B == nseq * ncores
    pvec = pack_pvec(inputs)
    cst = make_consts()
    in_maps = []
    for i in range(ncores):
        xs = x[i * nseq:(i + 1) * nseq].reshape(nseq * S, D)
        m = {"xT": np.ascontiguousarray(xs.T), "pvec": pvec, "cst": cst,
             "cv_w_pw1": np.asarray(inputs["cv_w_pw1"], np.float32),
             "cv_w_pw2": np.asarray(inputs["cv_w_pw2"], np.float32),
             "mlp_w1": np.asarray(inputs["mlp_w1"], np.float32),
             "mlp_w2": np.asarray(inputs["mlp_w2"], np.float32)}
        if depth > 1:
            m["gdn_w_in"] = np.asarray(inputs["gdn_w_in"], np.float32)
            m["gdn_w_out"] = np.asarray(inputs["gdn_w_out"], np.float32)
        in_maps.append(m)
    res = run_bass_kernel_spmd(nc, in_maps, core_ids=list(range(ncores)))
    outs = [np.asarray(r["outT"]).T.reshape(nseq, S, D) for r in res.results]
    return np.ascontiguousarray(np.concatenate(outs, axis=0).astype(np.float32))


def kernel(**inputs):
    return _run(inputs, 2, 4096, 2, 8)
```
